# Optimizing a Trainium2 kernel written in Bass

```python
import math
import jax
import jax.numpy as jnp
from jax import lax
import numpy as np

D_MODEL = 2048
BATCH = 4
SEQ = 4096
DEPTH = 4

GRID_W = 64
CTX_LEN = 256
N_MIXERS = 3
EXPAND = 2
D_INNER = EXPAND * D_MODEL
CONV_W = 3
DIFF_HEADS = D_INNER // 128
DIFF_HEAD_DIM = 64
DIFF_V_DIM = 2 * DIFF_HEAD_DIM
WIN_HEAD_DIM = 128
WIN_HEADS = D_INNER // WIN_HEAD_DIM
WIN_KV_HEADS = 8
WIN_GROUP = WIN_HEADS // WIN_KV_HEADS
WINDOW = 128
BLOCK = 128
ROPE_BASE = 10000.0
EPS = 1e-6
NEG_INF = -1e30

kernel_name = "hybrid_interleaved_dit_block"


def rms_norm(t, g, eps=EPS):
    tf = t.astype(jnp.float32)
    y = tf * lax.rsqrt(jnp.mean(tf * tf, axis=-1, keepdims=True) + eps)
    return (y * g.astype(jnp.float32)).astype(t.dtype)


def modulation(cvec, w_mod, b_mod):
    m = jax.nn.silu(cvec) @ w_mod + b_mod
    return jnp.split(m, 3, axis=-1)


def axial_rope_tables(rows, head_dim, dtype):
    row = jnp.repeat(jnp.arange(rows), GRID_W).astype(jnp.float32)
    col = jnp.tile(jnp.arange(GRID_W), rows).astype(jnp.float32)
    n_freq = head_dim // 4
    inv_freq = ROPE_BASE ** (-(jnp.arange(n_freq, dtype=jnp.float32) / n_freq))
    ang = jnp.concatenate([row[:, None] * inv_freq, col[:, None] * inv_freq], axis=-1)
    return jnp.cos(ang).astype(dtype), jnp.sin(ang).astype(dtype)


def apply_axial_rope(t, cos, sin):
    d = t.shape[-1]
    q4 = d // 4
    tr = t.reshape(t.shape[:-1] + (2, 2, q4))
    t1, t2 = tr[..., 0, :], tr[..., 1, :]
    bshape = (cos.shape[0],) + (1,) * (t.ndim - 3) + (2, q4)
    cs, sn = cos.reshape(bshape), sin.reshape(bshape)
    return jnp.stack([t1 * cs - t2 * sn, t1 * sn + t2 * cs], axis=-2).reshape(t.shape)


def short_conv_branch(h, w_in, conv_w, conv_b, w_out):
    b_gate, c_gate, xt, z = jnp.split(h @ w_in, 4, axis=-1)
    u = c_gate * xt
    n = u.shape[1]
    half = CONV_W // 2
    up = jnp.pad(u, ((0, 0), (half, half), (0, 0)))
    y = sum(up[:, k:k + n] * conv_w[k] for k in range(CONV_W)) + conv_b
    return (b_gate * y * jax.nn.silu(z)) @ w_out


def short_conv_mixer(h, hc, params, need_ctx_out):
    w_in, conv_w, conv_b, w_out = params
    y = short_conv_branch(h, w_in, conv_w, conv_b, w_out)
    yc = short_conv_branch(hc, w_in, conv_w, conv_b, w_out) if need_ctx_out else None
    return y, yc


def _diff_heads(t, g, rope):
    B, L = t.shape[:2]
    t = rms_norm(t.reshape(B, L, DIFF_HEADS, 2, DIFF_HEAD_DIM), g).swapaxes(2, 3)
    if rope is not None:
        t = apply_axial_rope(t, *rope)
    return t


def diff_project(h, w_in, q_norm, k_norm, rope, kv_only):
    B, L, _ = h.shape
    if kv_only:
        k, v = jnp.split(h @ w_in[:, D_INNER:3 * D_INNER], 2, axis=-1)
        q = z = None
    else:
        q, k, v, z = jnp.split(h @ w_in, 4, axis=-1)
        q = _diff_heads(q, q_norm, rope)
    k = _diff_heads(k, k_norm, rope)
    v = v.reshape(B, L, DIFF_HEADS, DIFF_V_DIM)
    return q, k, v, z


def diff_attend(q, k, v, lam):
    s = jnp.einsum("bqmhd,bkmhd->bmhqk", q, k).astype(jnp.float32) * (DIFF_HEAD_DIM ** -0.5)
    p = jax.nn.softmax(s, axis=-1)
    w = p[:, 0] - lam * p[:, 1]
    return jnp.einsum("bhqk,bkhe->bqhe", w.astype(v.dtype), v)


def diff_finish(o, z, sub_norm, lam_init, w_out):
    B, L = o.shape[:2]
    o = rms_norm(o, sub_norm) * (1.0 - lam_init)
    return (o.reshape(B, L, D_INNER) * jax.nn.silu(z)) @ w_out


def diff_attention_mixer(h, hc, params, layer_idx, need_ctx_out, rope):
    w_in, q_norm, k_norm, lq1, lk1, lq2, lk2, sub_norm, w_out = params
    f32 = jnp.float32
    lam_init = 0.8 - 0.6 * math.exp(-0.3 * layer_idx)
    lam = (jnp.exp(jnp.sum(lq1.astype(f32) * lk1.astype(f32)))
           - jnp.exp(jnp.sum(lq2.astype(f32) * lk2.astype(f32))) + lam_init)
    B, L, _ = h.shape
    q, k, v, z = diff_project(h, w_in, q_norm, k_norm, rope, False)
    qc, kc, vc, zc = diff_project(hc, w_in, q_norm, k_norm, None, not need_ctx_out)
    k_all = jnp.concatenate([k, kc], axis=1)
    v_all = jnp.concatenate([v, vc], axis=1)

    def block(i):
        qb = lax.dynamic_slice_in_dim(q, i * BLOCK, BLOCK, axis=1)
        return diff_attend(qb, k_all, v_all, lam)

    o = lax.map(block, jnp.arange(L // BLOCK))
    o = jnp.moveaxis(o, 0, 1).reshape(B, L, DIFF_HEADS, DIFF_V_DIM)
    y = diff_finish(o, z, sub_norm, lam_init, w_out)
    yc = None
    if need_ctx_out:
        oc = diff_attend(qc, kc, vc, lam)
        yc = diff_finish(oc, zc, sub_norm, lam_init, w_out)
    return y, yc


def win_project(h, w_in, q_norm, k_norm, rope, kv_only):
    B, L, _ = h.shape
    kv_w = WIN_KV_HEADS * WIN_HEAD_DIM
    if kv_only:
        k, v = jnp.split(h @ w_in[:, D_INNER:D_INNER + 2 * kv_w], 2, axis=-1)
        q = z = None
    else:
        q, k, v, z = jnp.split(h @ w_in, [D_INNER, D_INNER + kv_w, D_INNER + 2 * kv_w], axis=-1)
        q = rms_norm(q.reshape(B, L, WIN_KV_HEADS, WIN_GROUP, WIN_HEAD_DIM), q_norm)
        if rope is not None:
            q = apply_axial_rope(q, *rope)
    k = rms_norm(k.reshape(B, L, WIN_KV_HEADS, WIN_HEAD_DIM), k_norm)
    if rope is not None:
        k = apply_axial_rope(k, *rope)
    v = v.reshape(B, L, WIN_KV_HEADS, WIN_HEAD_DIM)
    return q, k, v, z


def window_gqa_mixer(h, hc, params, need_ctx_out, rope):
    w_in, q_norm, k_norm, sink, w_out = params
    f32 = jnp.float32
    B, L, _ = h.shape
    scale = WIN_HEAD_DIM ** -0.5
    q, k, v, z = win_project(h, w_in, q_norm, k_norm, rope, False)
    qc, kc, vc, zc = win_project(hc, w_in, q_norm, k_norm, None, not need_ctx_out)
    n_ctx = kc.shape[1]
    sink_logit = sink.astype(f32).reshape(WIN_KV_HEADS, WIN_GROUP, 1, 1)
    pad = ((0, 0), (BLOCK, BLOCK), (0, 0), (0, 0))
    kp, vp = jnp.pad(k, pad), jnp.pad(v, pad)
    band = 3 * BLOCK
    offset = jnp.arange(BLOCK)[:, None] + BLOCK - jnp.arange(band)[None, :]
    in_window = jnp.abs(offset) <= WINDOW

    def block(i):
        qb = lax.dynamic_slice_in_dim(q, i * BLOCK, BLOCK, axis=1)
        kb = lax.dynamic_slice_in_dim(kp, i * BLOCK, band, axis=1)
        vb = lax.dynamic_slice_in_dim(vp, i * BLOCK, band, axis=1)
        kpos = i * BLOCK - BLOCK + jnp.arange(band)
        valid = in_window & ((kpos >= 0) & (kpos < L))[None, :]
        s_band = jnp.einsum("bqngd,bknd->bngqk", qb, kb).astype(f32) * scale
        s_band = jnp.where(valid, s_band, NEG_INF)
        s_ctx = jnp.einsum("bqngd,bknd->bngqk", qb, kc).astype(f32) * scale
        s_sink = jnp.broadcast_to(sink_logit, s_band.shape[:-1] + (1,))
        p = jax.nn.softmax(jnp.concatenate([s_band, s_ctx, s_sink], axis=-1), axis=-1)
        p_band = p[..., :band].astype(v.dtype)
        p_ctx = p[..., band:band + n_ctx].astype(v.dtype)
        return (jnp.einsum("bngqk,bknd->bqngd", p_band, vb)
                + jnp.einsum("bngqk,bknd->bqngd", p_ctx, vc))

    o = lax.map(block, jnp.arange(L // BLOCK))
    o = jnp.moveaxis(o, 0, 1).reshape(B, L, D_INNER)
    y = (o * jax.nn.silu(z)) @ w_out
    yc = None
    if need_ctx_out:
        s = jnp.einsum("bqngd,bknd->bngqk", qc, kc).astype(f32) * scale
        s_sink = jnp.broadcast_to(sink_logit, s.shape[:-1] + (1,))
        p = jax.nn.softmax(jnp.concatenate([s, s_sink], axis=-1), axis=-1)[..., :n_ctx]
        oc = jnp.einsum("bngqk,bknd->bqngd", p.astype(vc.dtype), vc).reshape(B, n_ctx, D_INNER)
        yc = (oc * jax.nn.silu(zc)) @ w_out
    return y, yc


def setup_inputs(seed: int = 0) -> dict:
    key = jax.random.key(seed)
    keys = iter(jax.random.split(key, 64))

    def rnd(shape, scale):
        return jax.random.normal(next(keys), shape, dtype=jnp.float32) * scale

    d_scale = D_MODEL ** -0.5
    e_scale = D_INNER ** -0.5
    kv_w = WIN_KV_HEADS * WIN_HEAD_DIM
    inp = {
        "x": rnd((BATCH, SEQ, D_MODEL), 1.0),
        "c": rnd((BATCH, D_MODEL), 1.0),
        "ctx": rnd((BATCH, CTX_LEN, D_MODEL), 1.0),
        "c_ctx": rnd((D_MODEL,), 1.0),
    }
    for i in range(DEPTH):
        p = f"l{i}_"
        kind = i % N_MIXERS
        inp[p + "norm"] = 1.0 + rnd((D_MODEL,), 0.05)
        inp[p + "w_mod"] = rnd((D_MODEL, 3 * D_MODEL), 0.5 * d_scale)
        inp[p + "b_mod"] = rnd((3 * D_MODEL,), 0.02)
        if kind == 0:
            inp[p + "w_in"] = rnd((D_MODEL, 4 * D_INNER), d_scale)
            inp[p + "conv_w"] = rnd((CONV_W, D_INNER), CONV_W ** -0.5)
            inp[p + "conv_b"] = rnd((D_INNER,), 0.02)
        elif kind == 1:
            inp[p + "w_in"] = rnd((D_MODEL, 4 * D_INNER), d_scale)
            inp[p + "q_norm"] = 1.0 + rnd((DIFF_HEAD_DIM,), 0.05)
            inp[p + "k_norm"] = 1.0 + rnd((DIFF_HEAD_DIM,), 0.05)
            inp[p + "lam_q1"] = rnd((DIFF_HEAD_DIM,), 0.1)
            inp[p + "lam_k1"] = rnd((DIFF_HEAD_DIM,), 0.1)
            inp[p + "lam_q2"] = rnd((DIFF_HEAD_DIM,), 0.1)
            inp[p + "lam_k2"] = rnd((DIFF_HEAD_DIM,), 0.1)
            inp[p + "sub_norm"] = 1.0 + rnd((DIFF_V_DIM,), 0.05)
        else:
            inp[p + "w_in"] = rnd((D_MODEL, 2 * D_INNER + 2 * kv_w), d_scale)
            inp[p + "q_norm"] = 1.0 + rnd((WIN_HEAD_DIM,), 0.05)
            inp[p + "k_norm"] = 1.0 + rnd((WIN_HEAD_DIM,), 0.05)
            inp[p + "sink"] = rnd((WIN_HEADS,), 1.0)
        inp[p + "w_out"] = rnd((D_INNER, D_MODEL), e_scale)
    return inp


def reference(x, c, ctx, c_ctx,
              l0_norm, l0_w_mod, l0_b_mod, l0_w_in, l0_conv_w, l0_conv_b, l0_w_out,
              l1_norm, l1_w_mod, l1_b_mod, l1_w_in, l1_q_norm, l1_k_norm,
              l1_lam_q1, l1_lam_k1, l1_lam_q2, l1_lam_k2, l1_sub_norm, l1_w_out,
              l2_norm, l2_w_mod, l2_b_mod, l2_w_in, l2_q_norm, l2_k_norm, l2_sink, l2_w_out,
              l3_norm, l3_w_mod, l3_b_mod, l3_w_in, l3_conv_w, l3_conv_b, l3_w_out):
    n_tok = x.shape[1]
    rows = n_tok // GRID_W
    rope_diff = axial_rope_tables(rows, DIFF_HEAD_DIM, x.dtype)
    rope_win = axial_rope_tables(rows, WIN_HEAD_DIM, x.dtype)
    layers = [
        (l0_norm, l0_w_mod, l0_b_mod, (l0_w_in, l0_conv_w, l0_conv_b, l0_w_out)),
        (l1_norm, l1_w_mod, l1_b_mod, (l1_w_in, l1_q_norm, l1_k_norm, l1_lam_q1, l1_lam_k1,
                                       l1_lam_q2, l1_lam_k2, l1_sub_norm, l1_w_out)),
        (l2_norm, l2_w_mod, l2_b_mod, (l2_w_in, l2_q_norm, l2_k_norm, l2_sink, l2_w_out)),
        (l3_norm, l3_w_mod, l3_b_mod, (l3_w_in, l3_conv_w, l3_conv_b, l3_w_out)),
    ]
    for i in range(DEPTH):
        norm_g, w_mod, b_mod, mix_p = layers[i]
        kind = i % N_MIXERS
        reads_ctx = kind != 0
        need_ctx_out = any((j % N_MIXERS) != 0 for j in range(i + 1, DEPTH))
        shift, scale, gate = modulation(c, w_mod, b_mod)
        h = rms_norm(x, norm_g) * (1 + scale[:, None, :]) + shift[:, None, :]
        hc = None
        if reads_ctx or need_ctx_out:
            shift_c, scale_c, gate_c = modulation(c_ctx, w_mod, b_mod)
            hc = rms_norm(ctx, norm_g) * (1 + scale_c) + shift_c
        if kind == 0:
            y, yc = short_conv_mixer(h, hc, mix_p, need_ctx_out)
        elif kind == 1:
            y, yc = diff_attention_mixer(h, hc, mix_p, i, need_ctx_out, rope_diff)
        else:
            y, yc = window_gqa_mixer(h, hc, mix_p, need_ctx_out, rope_win)
        x = x + gate[:, None, :] * y
        if need_ctx_out:
            ctx = ctx + gate_c * yc
    return x
```

```python
import numpy as np
import ml_dtypes
import concourse.bass as bass
import concourse.mybir as mybir
from concourse.bass_utils import run_bass_kernel_spmd

F32 = mybir.dt.float32
BF16 = mybir.dt.bfloat16
AF = mybir.ActivationFunctionType
ALU = mybir.AluOpType
AX = mybir.AxisListType

D = 2048
DI = 4096
T_LAT = 4096
T_CTX = 256
NT_LAT = T_LAT // 128
NT = (T_LAT + T_CTX) // 128
KC = D // 128
EPS = 1e-6
NCORES = 8
OWN = 16


class _Op:
    __slots__ = ("eng", "fn", "deps", "dma", "sig", "sem", "val", "lane")


class Sched:
    ENGS = ("pe", "act", "dve", "pool", "sp")
    NL = 8

    def __init__(self):
        self.ops = {e: [] for e in self.ENGS}
        self.last_w = {}
        self.rd_eng = {}
        self.rd_dma = {}
        self.lane_rr = {"sp": 0, "pool": 0}
        self.lane_last = {}
        self.last_on = {}

    def add(self, eng, fn, reads=(), writes=(), dma=False):
        op = _Op()
        op.eng, op.fn, op.dma, op.sig, op.sem, op.val, op.lane = eng, fn, dma, False, None, 0, None
        deps = set()
        for r in reads:
            w = self.last_w.get(r)
            if w is not None:
                deps.add(w)
        for r in writes:
            w = self.last_w.get(r)
            if w is not None:
                deps.add(w)
            for o in self.rd_eng.get(r, {}).values():
                deps.add(o)
            for o in self.rd_dma.get(r, ()):
                deps.add(o)
        for r in reads:
            if dma:
                self.rd_dma.setdefault(r, []).append(op)
            else:
                self.rd_eng.setdefault(r, {})[eng] = op
        for r in writes:
            self.last_w[r] = op
            self.rd_eng[r] = {}
            self.rd_dma[r] = []
        if dma:
            lane = (eng, self.lane_rr[eng])
            self.lane_rr[eng] = (self.lane_rr[eng] + 1) % self.NL
            op.lane = lane
            prev = self.lane_last.get(lane)
            if prev is not None:
                deps.add(prev)
            self.lane_last[lane] = op
        if eng == "pe":
            deps = {d for d in deps if d.dma or d.eng != "pe"}
        deps.discard(op)
        op.deps = deps
        self.ops[eng].append(op)
        if not dma:
            self.last_on[eng] = op
        return op

    def barrier(self):
        tails = [o for o in self.last_on.values()]
        for lane, o in self.lane_last.items():
            tails.append(o)
        for e in self.ENGS:
            op = self.add(e, None)
            op.deps = set(t for t in tails)
        self.last_w.clear(); self.rd_eng.clear(); self.rd_dma.clear()

    def emit(self, nc, stack):
        sems = {}
        for e in ("pe", "act", "dve", "pool"):
            sems[e] = stack.enter_context(nc.semaphore("sem_" + e))
        for q in ("sp", "pool"):
            for l in range(self.NL):
                sems[(q, l)] = stack.enter_context(nc.semaphore("lane_%s%d" % (q, l)))
        for e in self.ENGS:
            for op in self.ops[e]:
                for d in op.deps:
                    d.sig = True
        cnt = {k: 0 for k in sems}
        for e in self.ENGS:
            for op in self.ops[e]:
                if op.dma:
                    cnt[op.lane] += 16
                    op.sem, op.val = op.lane, cnt[op.lane]
                elif op.sig and op.fn is not None:
                    cnt[e] += 1
                    op.sem, op.val = e, cnt[e]
        final = dict(cnt)
        block = stack.enter_context(nc.Block())
        engmap = {"pe": block.tensor, "act": block.scalar, "dve": block.vector,
                  "pool": block.gpsimd, "sp": block.sync}
        nwaits = [0]

        def run(ename, e):
            waited = {}
            for op in self.ops[ename]:
                need = {}
                for d in op.deps:
                    if d.sem is None:
                        continue
                    if d.val > need.get(d.sem, 0):
                        need[d.sem] = d.val
                for s, v in need.items():
                    if v > waited.get(s, 0):
                        e.wait_ge(sems[s], v)
                        waited[s] = v
                        nwaits[0] += 1
                if op.fn is None:
                    continue
                ins = op.fn(e)
                if op.dma:
                    ins.then_inc(sems[op.sem], 16)
                elif op.sig:
                    ins.then_inc(sems[op.sem], 1)
            if ename in ("sp", "pool"):
                for l in range(self.NL):
                    if final[(ename, l)] > 0:
                        e.wait_ge(sems[(ename, l)], final[(ename, l)])

        for ename in self.ENGS:
            engmap[ename](lambda e, ename=ename: run(ename, e))
        self.nwaits = nwaits[0]


class Buf:
    def __init__(self, t, name):
        self.t = t
        self.name = name

    def __getitem__(self, idx):
        return self.t[idx]


def build_program(nlayers=4, debug_out=None):
    nc = bass.Bass("TRN2", target_bir_lowering=False)
    S = Sched()
    from contextlib import ExitStack
    top = ExitStack()

    def dram(name, shape, dt, kind="Internal"):
        return nc.dram_tensor(name, list(shape), dt, kind=kind).ap()

    x_in = dram("x", [T_LAT, D], F32, "ExternalInput")
    ctx_in = dram("ctx", [T_CTX, D], F32, "ExternalInput")
    cT_in = dram("cT", [128, 2, KC], F32, "ExternalInput")
    out_d = dram("out", [OWN * 128, D], F32, "ExternalOutput")
    ident_in = dram("ident", [128, 128], BF16, "ExternalInput")
    kinds = [0, 1, 2, 0]
    W = []
    for l in range(nlayers):
        kind = kinds[l]
        n_in = 16384 if kind != 2 else 10240
        w = dict(kind=kind, n_in=n_in)
        w["norm"] = dram(f"l{l}_norm", [1, D], F32, "ExternalInput")
        w["w_mod"] = dram(f"l{l}_w_mod", [D, 3 * D], F32, "ExternalInput")
        w["b_mod"] = dram(f"l{l}_b_mod", [1, 3 * D], F32, "ExternalInput")
        w["w_in"] = dram(f"l{l}_w_in", [D, n_in], F32, "ExternalInput")
        w["w_out"] = dram(f"l{l}_w_out", [DI, D], F32, "ExternalInput")
        if kind == 0:
            w["convp"] = dram(f"l{l}_convp", [128, 32, 4], F32, "ExternalInput")
        w["w_mod_b"] = dram(f"l{l}_w_mod_b", [12, 128, KC, 512], BF16)
        w["w_in_b"] = dram(f"l{l}_w_in_b", [n_in // 512, 128, KC, 512], BF16)
        w["w_out_b"] = dram(f"l{l}_w_out_b", [8, 128, KC, 512], BF16)
        w["mod_d"] = dram(f"l{l}_mod_d", [2, 3 * D], F32)
        W.append(w)

    xbuf = [dram("xbuf0", [NT * 128, D], F32), dram("xbuf1", [NT * 128, D], F32)]
    hT_d = dram("hT_d", [128, KC, NT * 128], BF16)
    UW = 1 + T_LAT + 2 + T_CTX + 1
    u_d = dram("u_d", [32, 128, UW], F32)

    NTOK = NT * 128
    dk = "ExternalOutput" if debug_out in ("full", "attn") else "Internal"
    if debug_out == "attn":
        dbg2 = dram("dbg2", [4, 128, 512], F32, "ExternalOutput")
    qT_d = dram("qT_d", [32, 128, NTOK], BF16, dk)
    kT_d = dram("kT_d", [32, 128, NTOK], BF16, dk)
    v_d = dram("v_d", [NTOK, DI], BF16, dk)
    sz_d = dram("sz_d", [NTOK, DI], F32, dk)
    gT_d = dram("gT_d", [32, 128, NTOK], BF16, dk)
    szT_d = dram("szT_d", [32, 128, NTOK], F32)
    if nlayers > 1:
        rope64 = dram("rope64", [NTOK, 2, 32], F32, "ExternalInput")
        lamv = dram("l1_lamv", [1, 4 * 64], F32, "ExternalInput")
        W[1]["q_norm"] = dram("l1_q_norm", [1, 64], F32, "ExternalInput")
        W[1]["k_norm"] = dram("l1_k_norm", [1, 64], F32, "ExternalInput")
        W[1]["sub_norm"] = dram("l1_sub_norm", [1, 128], F32, "ExternalInput")
    if nlayers > 2:
        rope128 = dram("rope128", [NTOK, 2, 64], F32, "ExternalInput")
        masks_in = dram("masks", [128, 2, 128], BF16, "ExternalInput")
        W[2]["q_norm"] = dram("l2_q_norm", [1, 128], F32, "ExternalInput")
        W[2]["k_norm"] = dram("l2_k_norm", [1, 128], F32, "ExternalInput")
        W[2]["sink"] = dram("l2_sink", [1, 32], F32, "ExternalInput")

    def ucol(tok):
        return 1 + tok if tok < T_LAT else 1 + tok + 2

    uid = [0]

    def sb(stack, name, shape, dt):
        uid[0] += 1
        name = "s%d_%s" % (uid[0], name)
        return Buf(stack.enter_context(nc.sbuf_tensor(name, list(shape), dt)), name)

    def ps(stack, name, shape, dt):
        uid[0] += 1
        name = "p%d_%s" % (uid[0], name)
        return Buf(stack.enter_context(nc.psum_tensor(name, list(shape), dt)), name)

    def dma(q, out_ap, in_ap, reads, writes, **kw):
        return S.add(q, lambda e: e.dma_start(out=out_ap, in_=in_ap, **kw), reads, writes, dma=True)

    def x_src(l, tile):
        if l == 0:
            if tile < NT_LAT:
                return x_in[tile * 128:(tile + 1) * 128, :], None
            return ctx_in[(tile - NT_LAT) * 128:(tile - NT_LAT + 1) * 128, :], None
        return xbuf[l % 2][tile * 128:(tile + 1) * 128, :], ("x", l, tile)

    def x_dst(l, tile):
        if l == nlayers - 1 and tile < OWN and debug_out is None:
            return out_d[tile * 128:(tile + 1) * 128, :], ("out", tile)
        return xbuf[(l + 1) % 2][tile * 128:(tile + 1) * 128, :], ("x", l + 1, tile)

    def cast_weights(l):
        w = W[l]
        for s in range(12):
            src = w["w_mod"][:, s * 512:(s + 1) * 512].rearrange("(k p) n -> p k n", p=128)
            dma("pool", w["w_mod_b"][s], src, [], [("wmod", l, s)])
        for s in range(w["n_in"] // 512):
            src = w["w_in"][:, s * 512:(s + 1) * 512].rearrange("(k p) n -> p k n", p=128)
            dma("pool", w["w_in_b"][s], src, [], [("win", l, s)])
        for n in range(4):
            for kh in range(2):
                src = w["w_out"][kh * 2048:(kh + 1) * 2048, n * 512:(n + 1) * 512].rearrange("(k p) n -> p k n", p=128)
                dma("pool", w["w_out_b"][n * 2 + kh], src, [], [("wout", l, n * 2 + kh)])

    eps_t = sb(top, "eps_t", [128, 1], F32)
    S.add("dve", lambda e: e.memset(eps_t[:, :], EPS), [], [eps_t])
    zcol = sb(top, "zcol", [128, 2], F32)
    S.add("dve", lambda e: e.memset(zcol[:, :], 0.0), [], [zcol])
    for (c0, n) in ((0, 1), (T_LAT + 1, 2), (UW - 1, 1)):
        dma("sp", u_d[:, :, c0:c0 + n].rearrange("j p c -> p j c"),
            zcol[:, 0:n].unsqueeze(1).to_broadcast([128, 32, n]), [zcol], [("upad", c0)], allow_slow_non_contiguous=True)

    ident = sb(top, "ident", [128, 128], BF16)
    dma("sp", ident[:, :], ident_in[:, :], [], [ident])

    cast_weights(0)

    def modulation(l, stack):
        w = W[l]
        cT = sb(stack, "cT", [128, 2, KC], F32)
        sc = sb(stack, "sc", [128, KC, 2], BF16)
        bm = sb(stack, "bm", [2, 3 * D], F32)
        gg = sb(stack, "gg", [2, D], F32)
        msb = sb(stack, "msb", [2, 3 * D], F32)
        wm = [sb(stack, "wm%d" % i, [128, KC, 512], BF16) for i in range(2)]
        pm = [ps(stack, "pm%d" % i, [2, 512], F32) for i in range(2)]
        dma("sp", cT[:, :, :], cT_in[:, :, :], [], [cT])
        dma("sp", bm[:, :], w["b_mod"][0:1, :].partition_broadcast(2).rearrange("p o n -> p (o n)"), [], [bm])
        dma("sp", gg[:, :], w["norm"][0:1, :].partition_broadcast(2).rearrange("p o n -> p (o n)"), [], [gg])
        S.add("act", lambda e: e.activation(out=sc[:, :, :].rearrange("p k r -> p r k"), in_=cT[:, :, :], func=AF.Silu),
              [cT], [sc])
        for s in range(12):
            wb = wm[s % 2]
            pb = pm[s % 2]
            dma("sp", wb[:, :, :], w["w_mod_b"][s], [("wmod", l, s)], [wb])
            for k in range(KC):
                S.add("pe", lambda e, k=k, wb=wb, pb=pb: e.matmul(pb[:, :], sc[:, k, :], wb[:, k, :], start=(k == 0), stop=(k == KC - 1)),
                      [sc, wb], [pb])
            S.add("dve", lambda e, s=s, pb=pb: e.tensor_tensor(out=msb[:, s * 512:(s + 1) * 512], in0=pb[:, :], in1=bm[:, s * 512:(s + 1) * 512], op=ALU.add),
                  [pb, bm], [msb])
        S.add("dve", lambda e: e.scalar_tensor_tensor(out=msb[:, D:2 * D], in0=msb[:, D:2 * D], scalar=1.0, in1=gg[:, :], op0=ALU.add, op1=ALU.mult),
              [msb, gg], [msb])
        md = w["mod_d"]
        dma("sp", md[:, 0:D], msb[:, D:2 * D], [msb], [("mod", l, 0)])
        dma("sp", md[:, D:2 * D], msb[:, 0:D], [msb], [("mod", l, 1)])
        dma("sp", md[:, 2 * D:3 * D], msb[:, 2 * D:3 * D], [msb], [("mod", l, 2)])

    def load_mod(l, tile_buf, row, which):
        md = W[l]["mod_d"]
        src = md[row:row + 1, which * D:(which + 1) * D].partition_broadcast(128).rearrange("p o n -> p (o n)")
        dma("sp", tile_buf[:, :], src, [("mod", l, which)], [tile_buf])

    def phase_norm(l, stack, tiles):
        gs = sb(stack, "gs", [128, D], F32)
        sh = sb(stack, "sh", [128, D], F32)
        xt = [sb(stack, "xt%d" % i, [128, D], F32) for i in range(2)]
        hb = [sb(stack, "hb%d" % i, [128, D], BF16) for i in range(2)]
        junk = sb(stack, "junk", [128, D], BF16)
        ss = [sb(stack, "ss%d" % i, [128, 1], F32) for i in range(2)]
        rs = [sb(stack, "rs%d" % i, [128, 1], F32) for i in range(2)]
        hTs = [sb(stack, "hTs%d" % i, [128, KC, 128], BF16) for i in range(2)]
        pT = [ps(stack, "pT%d" % i, [128, KC, 128], BF16) for i in range(2)]
        cur_row = None
        for n, tile in enumerate(tiles):
            row = 0 if tile < NT_LAT else 1
            if row != cur_row:
                load_mod(l, gs, row, 0)
                load_mod(l, sh, row, 1)
                cur_row = row
            b = n % 2
            x_ap, x_res = x_src(l, tile)
            dma("sp", xt[b][:, :], x_ap, [x_res] if x_res else [], [xt[b]])
            S.add("act", lambda e, b=b: e.activation(out=junk[:, :], in_=xt[b][:, :], func=AF.Square, accum_out=ss[b][:, :]),
                  [xt[b]], [junk, ss[b]])
            S.add("act", lambda e, b=b: e.activation(out=rs[b][:, :], in_=ss[b][:, :], func=AF.Sqrt, bias=eps_t[:, :], scale=1.0 / D),
                  [ss[b], eps_t], [rs[b]])
            S.add("dve", lambda e, b=b: e.reciprocal(out=rs[b][:, :], in_=rs[b][:, :]), [rs[b]], [rs[b]])
            S.add("dve", lambda e, b=b: e.scalar_tensor_tensor(out=xt[b][:, :], in0=xt[b][:, :], scalar=rs[b][:, 0:1], in1=gs[:, :], op0=ALU.mult, op1=ALU.mult),
                  [xt[b], rs[b], gs], [xt[b]])
            S.add("pool", lambda e, b=b: e.tensor_tensor(out=hb[b][:, :], in0=xt[b][:, :], in1=sh[:, :], op=ALU.add),
                  [xt[b], sh], [hb[b]])
            for k in range(KC):
                S.add("pe", lambda e, b=b, k=k: e.transpose(out=pT[b][:, k, :], in_=hb[b][:, k * 128:(k + 1) * 128], identity=ident[:, :]),
                      [hb[b], ident], [pT[b]])
            S.add("act", lambda e, b=b: e.activation(out=hTs[b][:, :, :], in_=pT[b][:, :, :], func=AF.Copy),
                  [pT[b]], [hTs[b]])
            dma("pool", hT_d[:, :, tile * 128:(tile + 1) * 128], hTs[b][:, :, :], [hTs[b]], [("hT", tile)])

    def token_blocks(tiles_lat, tiles_ctx, tb_tiles):
        blocks = []
        for group in (tiles_lat, tiles_ctx):
            for i in range(0, len(group), tb_tiles):
                blocks.append(group[i:i + tb_tiles])
        return blocks

    def load_hT_block(hTb, blk):
        n = len(blk)
        src = hT_d[:, :, blk[0] * 128:(blk[0] + n) * 128]
        dst = hTb[:, :, 0:n * 128]
        dma("sp", dst, src, [("hT", t) for t in blk], [hTb])

    def outproj_block(l, blk, gated, gate, wslab, pY, xo, xn, eit):
        w = W[l]
        for n in range(4):
            for kh in range(2):
                wo = wslab(w["w_out_b"][n * 2 + kh], ("wout", l, n * 2 + kh))
                for ti, tile in enumerate(blk):
                    for k in range(KC):
                        S.add("pe", lambda e, ti=ti, wo=wo, k=k, kh=kh: e.matmul(
                            pY[ti][:, :], gated[:, kh * 16 + k, ti * 128:(ti + 1) * 128], wo[:, k, :],
                            start=(kh == 0 and k == 0), stop=(kh == 1 and k == KC - 1)),
                            [wo, gated], [pY[ti]])
            for ti, tile in enumerate(blk):
                xob, xnb = xo[eit % 2], xn[eit % 2]
                eit += 1
                x_ap, x_res = x_src(l, tile)
                dma("sp", xob[:, :], x_ap[:, n * 512:(n + 1) * 512], [x_res] if x_res else [], [xob])
                S.add("dve", lambda e, xnb=xnb, ti=ti, n=n: e.tensor_tensor(out=xnb[:, :], in0=pY[ti][:, :], in1=gate[:, n * 512:(n + 1) * 512], op=ALU.mult),
                      [pY[ti], gate], [xnb])
                S.add("pool", lambda e, xnb=xnb, xob=xob: e.tensor_tensor(out=xnb[:, :], in0=xnb[:, :], in1=xob[:, :], op=ALU.add),
                      [xnb, xob], [xnb])
                d_ap, d_res = x_dst(l, tile)
                dma("pool", d_ap[:, n * 512:(n + 1) * 512], xnb[:, :], [xnb], [(d_res, n)])
        return eit

    def make_wslab(stack, nbuf=4):
        wsl = [sb(stack, "wsl%d" % i, [128, KC, 512], BF16) for i in range(nbuf)]
        wrr = [0]

        def wslab(src_ap, res):
            b = wsl[wrr[0] % nbuf]
            wrr[0] += 1
            dma("sp", b[:, :, :], src_ap, [res], [b])
            return b
        return wslab

    def outproj_phase(l, stack, need_ctx, lat):
        ctxt = list(range(NT_LAT, NT)) if need_ctx else []
        blocks = token_blocks(lat, ctxt, 4)
        wslab = make_wslab(stack)
        gated = sb(stack, "gated", [128, 32, 512], BF16)
        gate = sb(stack, "gate", [128, D], F32)
        xo = [sb(stack, "xo%d" % i, [128, 512], F32) for i in range(2)]
        xn = [sb(stack, "xn%d" % i, [128, 512], F32) for i in range(2)]
        pY = [ps(stack, "pY%d" % i, [128, 512], F32) for i in range(4)]
        cur_row = None
        eit = 0
        for blk in blocks:
            ntok = len(blk) * 128
            row = 0 if blk[0] < NT_LAT else 1
            if row != cur_row:
                load_mod(l, gate, row, 2)
                cur_row = row
            t0 = blk[0] * 128
            dma("sp", gated[:, :, 0:ntok], gT_d[:, :, t0:t0 + ntok].rearrange("j p t -> p j t"),
                [("gT", j, t) for j in range(32) for t in blk], [gated])
            eit = outproj_block(l, blk, gated, gate, wslab, pY, xo, xn, eit)

    def conv_layer(l, stack, need_ctx, latA, latB):
        w = W[l]
        ctxt = list(range(NT_LAT, NT)) if need_ctx else []
        blocksA = token_blocks(latA, ctxt, 4)
        blocksB = token_blocks(latB, ctxt, 4)
        blocks = blocksA
        wsl = [sb(stack, "wsl%d" % i, [128, KC, 512], BF16) for i in range(4)]
        wrr = [0]

        def wslab(src_ap, res):
            b = wsl[wrr[0] % 4]
            wrr[0] += 1
            dma("sp", b[:, :, :], src_ap, [res], [b])
            return b

        hTb = sb(stack, "hTb", [128, KC, 512], BF16)
        pA = [ps(stack, "pA%d" % i, [128, 512], F32) for i in range(4)]
        cg_sb = [sb(stack, "cg_sb%d" % i, [128, 512], F32) for i in range(2)]
        u_sb = [sb(stack, "u_sb%d" % i, [128, 512], F32) for i in range(2)]
        it = 0
        for blk in blocks:
            ntok = len(blk) * 128
            load_hT_block(hTb, blk)
            for g in range(8):
                wcg = wslab(w["w_in_b"][8 + g], ("win", l, 8 + g))
                wxt = wslab(w["w_in_b"][16 + g], ("win", l, 16 + g))
                for jj in range(4):
                    j = g * 4 + jj
                    p0 = pA[(it * 2) % 4]
                    p1 = pA[(it * 2 + 1) % 4]
                    cs = cg_sb[it % 2]
                    us = u_sb[it % 2]
                    it += 1
                    for (pp, ww) in ((p0, wcg), (p1, wxt)):
                        for k in range(KC):
                            S.add("pe", lambda e, pp=pp, ww=ww, k=k, jj=jj, ntok=ntok: e.matmul(
                                pp[:, 0:ntok], ww[:, k, jj * 128:(jj + 1) * 128], hTb[:, k, 0:ntok], start=(k == 0), stop=(k == KC - 1)),
                                [ww, hTb], [pp])
                    S.add("act", lambda e, cs=cs, p0=p0, ntok=ntok: e.activation(out=cs[:, 0:ntok], in_=p0[:, 0:ntok], func=AF.Copy),
                          [p0], [cs])
                    S.add("dve", lambda e, us=us, cs=cs, p1=p1, ntok=ntok: e.tensor_tensor(out=us[:, 0:ntok], in0=cs[:, 0:ntok], in1=p1[:, 0:ntok], op=ALU.mult),
                          [cs, p1], [us])
                    c0 = ucol(blk[0] * 128)
                    dma("pool", u_d[j, :, c0:c0 + ntok], us[:, 0:ntok], [us], [("u", j, blk[0])])
        gated = sb(stack, "gated", [128, 32, 512], BF16)
        uw = [sb(stack, "uw%d" % i, [128, 514], F32) for i in range(2)]
        yv = [sb(stack, "yv%d" % i, [128, 512], F32) for i in range(2)]
        sz = [sb(stack, "sz%d" % i, [128, 512], F32) for i in range(2)]
        cvp = sb(stack, "cvp", [128, 32, 4], F32)
        gate = sb(stack, "gate", [128, D], F32)
        xo = [sb(stack, "xo%d" % i, [128, 512], F32) for i in range(2)]
        xn = [sb(stack, "xn%d" % i, [128, 512], F32) for i in range(2)]
        pY = [ps(stack, "pY%d" % i, [128, 512], F32) for i in range(4)]
        dma("sp", cvp[:, :, :], w["convp"][:, :, :], [], [cvp])
        cur_row = None
        it = 0
        eit = 0
        for blk in blocksB:
            ntok = len(blk) * 128
            row = 0 if blk[0] < NT_LAT else 1
            if row != cur_row:
                load_mod(l, gate, row, 2)
                cur_row = row
            load_hT_block(hTb, blk)
            c0 = ucol(blk[0] * 128)
            for g in range(8):
                wbg = wslab(w["w_in_b"][g], ("win", l, g))
                wz = wslab(w["w_in_b"][24 + g], ("win", l, 24 + g))
                for jj in range(4):
                    j = g * 4 + jj
                    p0 = pA[(it * 2) % 4]
                    p1 = pA[(it * 2 + 1) % 4]
                    uwb, yb, szb = uw[it % 2], yv[it % 2], sz[it % 2]
                    it += 1
                    ures = [("u", j, bb[0]) for bb in blocksA]
                    dma("sp", uwb[:, 0:ntok + 2], u_d[j, :, c0 - 1:c0 + ntok + 1], ures + [("upad", 0), ("upad", T_LAT + 1), ("upad", UW - 1)], [uwb])
                    for (pp, ww) in ((p0, wbg), (p1, wz)):
                        for k in range(KC):
                            S.add("pe", lambda e, pp=pp, ww=ww, k=k, jj=jj, ntok=ntok: e.matmul(
                                pp[:, 0:ntok], ww[:, k, jj * 128:(jj + 1) * 128], hTb[:, k, 0:ntok], start=(k == 0), stop=(k == KC - 1)),
                                [ww, hTb], [pp])
                    S.add("pool", lambda e, yb=yb, uwb=uwb, j=j, ntok=ntok: e.tensor_scalar(
                        out=yb[:, 0:ntok], in0=uwb[:, 0:ntok], scalar1=cvp[:, j, 0:1], scalar2=cvp[:, j, 3:4], op0=ALU.mult, op1=ALU.add),
                        [uwb, cvp], [yb])
                    S.add("dve", lambda e, yb=yb, uwb=uwb, j=j, ntok=ntok: e.scalar_tensor_tensor(
                        out=yb[:, 0:ntok], in0=uwb[:, 1:ntok + 1], scalar=cvp[:, j, 1:2], in1=yb[:, 0:ntok], op0=ALU.mult, op1=ALU.add),
                        [uwb, cvp, yb], [yb])
                    S.add("dve", lambda e, yb=yb, uwb=uwb, j=j, ntok=ntok: e.scalar_tensor_tensor(
                        out=yb[:, 0:ntok], in0=uwb[:, 2:ntok + 2], scalar=cvp[:, j, 2:3], in1=yb[:, 0:ntok], op0=ALU.mult, op1=ALU.add),
                        [uwb, cvp, yb], [yb])
                    S.add("act", lambda e, szb=szb, p1=p1, ntok=ntok: e.activation(out=szb[:, 0:ntok], in_=p1[:, 0:ntok], func=AF.Silu),
                          [p1], [szb])
                    S.add("dve", lambda e, yb=yb, p0=p0, ntok=ntok: e.tensor_tensor(out=yb[:, 0:ntok], in0=yb[:, 0:ntok], in1=p0[:, 0:ntok], op=ALU.mult),
                          [yb, p0], [yb])
                    S.add("dve", lambda e, yb=yb, szb=szb, j=j, ntok=ntok: e.tensor_tensor(out=gated[:, j, 0:ntok], in0=yb[:, 0:ntok], in1=szb[:, 0:ntok], op=ALU.mult),
                          [yb, szb], [gated])
            eit = outproj_block(l, blk, gated, gate, wslab, pY, xo, xn, eit)

    def bload(tile_buf, src_row_ap, reads=()):
        dma("sp", tile_buf[:, :], src_row_ap.partition_broadcast(128).rearrange("p o n -> p (o n)"), list(reads), [tile_buf])

    def proj_phase(l, stack, hd, roles, qscale, rope_d, groups):
        w = W[l]
        ng = 512 // hd
        nf = hd // 4
        blocks = []
        for (gt, groles) in groups:
            for i in range(0, len(gt), 4):
                blocks.append((gt[i:i + 4], groles))
        wslab = make_wslab(stack)
        hTb = sb(stack, "hTb", [128, KC, 512], BF16)
        nw = {"q": sb(stack, "nwq", [128, hd], F32), "k": sb(stack, "nwk", [128, hd], F32)}
        bload(nw["q"], w["q_norm"][0:1, :])
        bload(nw["k"], w["k_norm"][0:1, :])
        S.add("dve", lambda e: e.tensor_scalar(out=nw["q"][:, :], in0=nw["q"][:, :], scalar1=float(qscale), scalar2=None, op0=ALU.mult),
              [nw["q"]], [nw["q"]])
        pp = [ps(stack, "pp%d" % i, [128, 512], F32) for i in range(4)]
        pT = [ps(stack, "pT%d" % i, [128, 4, 128], BF16) for i in range(2)]
        csx = [sb(stack, "csx%d" % i, [128, ng, 2 * nf], F32) for i in range(4)]
        snx = [sb(stack, "snx%d" % i, [128, ng, 2 * nf], F32) for i in range(4)]
        NB = 4
        qraw = [sb(stack, "qraw%d" % i, [128, 512], F32) for i in range(NB)]
        sq = [sb(stack, "sq%d" % i, [128, 512], F32) for i in range(NB)]
        ss8 = [sb(stack, "ss8%d" % i, [128, ng], F32) for i in range(NB)]
        rs8 = [sb(stack, "rs8%d" % i, [128, ng], F32) for i in range(NB)]
        qn = [sb(stack, "qn%d" % i, [128, 512], F32) for i in range(NB)]
        tA = [sb(stack, "tA%d" % i, [128, 256], F32) for i in range(NB)]
        tB = [sb(stack, "tB%d" % i, [128, 256], F32) for i in range(NB)]
        tC = [sb(stack, "tC%d" % i, [128, 256], F32) for i in range(NB)]
        tD = [sb(stack, "tD%d" % i, [128, 256], F32) for i in range(NB)]
        qr = [sb(stack, "qr%d" % i, [128, 512], BF16) for i in range(NB)]
        qTs = [sb(stack, "qTs%d" % i, [128, 4, 128], BF16) for i in range(NB)]
        vb = [sb(stack, "vb%d" % i, [128, 512], BF16) for i in range(NB)]
        szb = [sb(stack, "szb%d" % i, [128, 512], F32) for i in range(NB)]
        it = 0
        trc = [0]
        pending = []
        for (blk, broles) in blocks:
            ntok = len(blk) * 128
            load_hT_block(hTb, blk)
            for ti, tile in enumerate(blk):
                t0 = tile * 128
                dma("sp", csx[ti][:, :, :], rope_d[t0:t0 + 128, 0, :].unsqueeze(1).to_broadcast([128, ng, 2 * nf]), [], [csx[ti]])
                dma("sp", snx[ti][:, :, :], rope_d[t0:t0 + 128, 1, :].unsqueeze(1).to_broadcast([128, ng, 2 * nf]), [], [snx[ti]])
            for (s_idx, role, base) in roles:
                if role not in broles:
                    continue
                wsb = wslab(w["w_in_b"][s_idx], ("win", l, s_idx))
                if role == "zT":
                    tb0 = blk[0] * 128
                    for c in range(4):
                        p = pp[it % 4]
                        b = it % NB
                        it += 1
                        for k in range(KC):
                            S.add("pe", lambda e, p=p, k=k, c=c, wsb=wsb, ntok=ntok: e.matmul(
                                p[:, 0:ntok], wsb[:, k, c * 128:(c + 1) * 128], hTb[:, k, 0:ntok], start=(k == 0), stop=(k == KC - 1)),
                                [hTb, wsb], [p])
                        S.add("act", lambda e, p=p, b=b, ntok=ntok: e.activation(out=szb[b][:, 0:ntok], in_=p[:, 0:ntok], func=AF.Silu), [p], [szb[b]])
                        dma("pool", szT_d[base * 4 + c, :, tb0:tb0 + ntok], szb[b][:, 0:ntok], [szb[b]], [("szT", base * 4 + c, t) for t in blk])
                    continue
                for ti, tile in enumerate(blk):
                    t0 = tile * 128
                    p = pp[it % 4]
                    b = it % NB
                    it += 1
                    for k in range(KC):
                        S.add("pe", lambda e, p=p, k=k, ti=ti, wsb=wsb: e.matmul(
                            p[:, :], hTb[:, k, ti * 128:(ti + 1) * 128], wsb[:, k, :], start=(k == 0), stop=(k == KC - 1)),
                            [hTb, wsb], [p])
                    if role == "v":
                        S.add("act", lambda e, p=p, b=b: e.activation(out=vb[b][:, :], in_=p[:, :], func=AF.Copy), [p], [vb[b]])
                        dma("pool", v_d[t0:t0 + 128, base * 512:(base + 1) * 512], vb[b][:, :], [vb[b]], [("v", base, tile)])
                    elif role == "z":
                        S.add("act", lambda e, p=p, b=b: e.activation(out=szb[b][:, :], in_=p[:, :], func=AF.Silu), [p], [szb[b]])
                        dma("pool", sz_d[t0:t0 + 128, base * 512:(base + 1) * 512], szb[b][:, :], [szb[b]], [("sz", base, tile)])
                    else:
                        S.add("act", lambda e, p=p, b=b: e.activation(out=sq[b][:, :], in_=p[:, :], func=AF.Square), [p], [sq[b]])
                        S.add("act", lambda e, p=p, b=b: e.activation(out=qraw[b][:, :], in_=p[:, :], func=AF.Copy), [p], [qraw[b]])
                        p3 = qraw[b][:, :].rearrange("p (g d) -> p g d", d=hd)
                        S.add("dve", lambda e, b=b: e.tensor_reduce(out=ss8[b][:, :], in_=sq[b][:, :].rearrange("p (g d) -> p g d", d=hd), axis=AX.X, op=ALU.add),
                              [sq[b]], [ss8[b]])
                        S.add("act", lambda e, b=b: e.activation(out=rs8[b][:, :], in_=ss8[b][:, :], func=AF.Sqrt, bias=eps_t[:, :], scale=1.0 / hd),
                              [ss8[b], eps_t], [rs8[b]])
                        S.add("dve", lambda e, b=b: e.reciprocal(out=rs8[b][:, :], in_=rs8[b][:, :]), [rs8[b]], [rs8[b]])
                        S.add("dve", lambda e, b=b, p3=p3: e.tensor_tensor(out=qn[b][:, :].rearrange("p (g d) -> p g d", d=hd), in0=p3,
                                                                            in1=rs8[b][:, :].unsqueeze(2).to_broadcast([128, ng, hd]), op=ALU.mult),
                              [qraw[b], rs8[b]], [qn[b]])
                        nwt = nw[role]
                        S.add("pool", lambda e, b=b, nwt=nwt: e.tensor_tensor(out=qn[b][:, :].rearrange("p (g d) -> p g d", d=hd),
                                                                               in0=qn[b][:, :].rearrange("p (g d) -> p g d", d=hd),
                                                                               in1=nwt[:, :].unsqueeze(1).to_broadcast([128, ng, hd]), op=ALU.mult),
                              [qn[b], nwt], [qn[b]])
                        qv = qn[b][:, :].rearrange("p (ga h f) -> p ga h f", h=2, f=nf)
                        ov = qr[b][:, :].rearrange("p (ga h f) -> p ga h f", h=2, f=nf)
                        t1, t2 = qv[:, :, 0, :], qv[:, :, 1, :]
                        cs = csx[ti][:, :, :].rearrange("p g (a f) -> p (g a) f", f=nf)
                        sn = snx[ti][:, :, :].rearrange("p g (a f) -> p (g a) f", f=nf)
                        v3 = lambda tb: tb[:, :].rearrange("p (ga f) -> p ga f", f=nf)
                        S.add("pool", lambda e, b=b, t1=t1, cs=cs: e.tensor_tensor(out=v3(tA[b]), in0=t1, in1=cs, op=ALU.mult), [qn[b], csx[ti]], [tA[b]])
                        S.add("pool", lambda e, b=b, t2=t2, sn=sn: e.tensor_tensor(out=v3(tB[b]), in0=t2, in1=sn, op=ALU.mult), [qn[b], snx[ti]], [tB[b]])
                        S.add("dve", lambda e, b=b, ov=ov: e.tensor_tensor(out=ov[:, :, 0, :], in0=v3(tA[b]), in1=v3(tB[b]), op=ALU.subtract), [tA[b], tB[b]], [qr[b]])
                        S.add("dve", lambda e, b=b, t1=t1, sn=sn: e.tensor_tensor(out=v3(tC[b]), in0=t1, in1=sn, op=ALU.mult), [qn[b], snx[ti]], [tC[b]])
                        S.add("pool", lambda e, b=b, t2=t2, cs=cs: e.tensor_tensor(out=v3(tD[b]), in0=t2, in1=cs, op=ALU.mult), [qn[b], csx[ti]], [tD[b]])
                        S.add("dve", lambda e, b=b, ov=ov: e.tensor_tensor(out=ov[:, :, 1, :], in0=v3(tC[b]), in1=v3(tD[b]), op=ALU.add), [tC[b], tD[b]], [qr[b]])
                        def fin(b=b, role=role, base=base, tile=tile, t0=t0):
                            pTb = pT[trc[0] % 2]
                            trc[0] += 1
                            for c in range(4):
                                S.add("pe", lambda e, b=b, c=c, pTb=pTb: e.transpose(out=pTb[:, c, :], in_=qr[b][:, c * 128:(c + 1) * 128], identity=ident[:, :]),
                                      [qr[b], ident], [pTb])
                            S.add("act", lambda e, b=b, pTb=pTb: e.activation(out=qTs[b][:, :, :], in_=pTb[:, :, :], func=AF.Copy), [pTb], [qTs[b]])
                            dst_t = qT_d if role == "q" else kT_d
                            dma("pool", dst_t[base * 4:(base + 1) * 4, :, t0:t0 + 128].rearrange("h p t -> p h t"), qTs[b][:, :, :], [qTs[b]],
                                [(role + "T", base * 4 + c, tile) for c in range(4)])
                        pending.append(fin)
                        while len(pending) > 2:
                            pending.pop(0)()
        while pending:
            pending.pop(0)()

    def load_head_kv(kTb, vab, kidx, vcol, ranges):
        for tiles in ranges:
            load_head_kv1(kTb, vab, kidx, vcol, tiles)

    def load_head_kv1(kTb, vab, kidx, vcol, tiles):
        t0, t1 = tiles[0] * 128, (tiles[-1] + 1) * 128
        dma("sp", kTb[:, t0:t1], kT_d[kidx, :, t0:t1], [("kT", kidx, t) for t in tiles], [kTb])
        dma("sp", vab[:, tiles[0]:tiles[-1] + 1, 0:128], v_d[t0:t1, vcol * 128:(vcol + 1) * 128].rearrange("(t p) e -> p t e", p=128),
            [("v", vcol // 4, t) for t in tiles], [vab])

    def attn_diff(l, stack, need_ctx, qlat):
        import math
        w = W[l]
        lam_init = 0.8 - 0.6 * math.exp(-0.3 * l)
        lqk = sb(stack, "lqk", [128, 4 * 64], F32)
        bload(lqk, lamv[0:1, :])
        prod = sb(stack, "prod", [128, 2, 64], F32)
        s2 = sb(stack, "s2", [128, 2], F32)
        e2 = sb(stack, "e2", [128, 2], F32)
        nlam = sb(stack, "nlam", [128, 1], F32)
        S.add("dve", lambda e: e.tensor_tensor(out=prod[:, :, :], in0=lqk[:, 0:128].rearrange("p (a d) -> p a d", d=64),
                                                in1=lqk[:, 128:256].rearrange("p (a d) -> p a d", d=64), op=ALU.mult), [lqk], [prod])
        S.add("dve", lambda e: e.tensor_reduce(out=s2[:, :], in_=prod[:, :, :], axis=AX.X, op=ALU.add), [prod], [s2])
        S.add("act", lambda e: e.activation(out=e2[:, :], in_=s2[:, :], func=AF.Exp), [s2], [e2])
        S.add("dve", lambda e: e.tensor_tensor(out=nlam[:, :], in0=e2[:, 1:2], in1=e2[:, 0:1], op=ALU.subtract), [e2], [nlam])
        S.add("dve", lambda e: e.tensor_scalar(out=nlam[:, :], in0=nlam[:, :], scalar1=float(-lam_init), scalar2=None, op0=ALU.add), [nlam], [nlam])
        snwc = sb(stack, "snwc", [128, 1], F32)
        dma("sp", snwc[:, :], w["sub_norm"][0:1, :].rearrange("o n -> n o"), [], [snwc], allow_slow_non_contiguous=True)
        S.add("dve", lambda e: e.tensor_scalar(out=snwc[:, :], in0=snwc[:, :], scalar1=float(1.0 - lam_init), scalar2=None, op0=ALU.mult), [snwc], [snwc])
        ones_f = sb(stack, "ones_f", [128, 128], F32)
        S.add("dve", lambda e: e.memset(ones_f[:, :], 1.0), [], [ones_f])

        kTh = [sb(stack, "kTh%d" % i, [128, NTOK], BF16) for i in range(2)]
        vt = [sb(stack, "vt%d" % i, [128, NT, 128], BF16) for i in range(2)]
        qTb = [sb(stack, "qTb%d" % i, [128, 512], BF16) for i in range(2)]
        E2 = [sb(stack, "E2_%d" % i, [128, 2, 512], BF16) for i in range(6)]
        Eacc = [sb(stack, "Eacc%d" % i, [128, 512], F32) for i in range(2)]
        ones_b = sb(stack, "ones_b", [128, 128], BF16)
        S.add("dve", lambda e: e.memset(ones_b[:, :], 1.0), [], [ones_b])
        psZ1 = ps(stack, "psZ1", [128, 512], F32)
        psS2 = [ps(stack, "psS2_%d" % i, [128, 2, 512], F32) for i in range(2)]
        psOT = [ps(stack, "psOT%d" % m, [128, 512], F32) for m in range(2)]
        psB = ps(stack, "psB", [128, 512], F32)
        OTs = [sb(stack, "OTs%d" % m, [128, 512], F32) for m in range(2)]
        R = [sb(stack, "R%d" % m, [128, 512], F32) for m in range(2)]
        ta = sb(stack, "ta", [128, 512], F32)
        tb = sb(stack, "tb", [128, 512], F32)
        oT = sb(stack, "oT", [128, 512], F32)
        sq = sb(stack, "sq", [128, 512], F32)
        rstd = sb(stack, "rstd", [128, 512], F32)
        szT = [sb(stack, "szT%d" % i, [128, 512], F32) for i in range(2)]
        gT = [sb(stack, "gT%d" % i, [128, 512], BF16) for i in range(2)]
        qblocks = token_blocks(qlat, list(range(NT_LAT, NT)) if need_ctx else [], 4)
        rr = 0
        gi = 0
        for h in range(32):
            kTb, vab = kTh[h % 2], vt[h % 2]
            load_head_kv(kTb, vab, h, h, [list(range(NT))])
            for bi, blk in enumerate(qblocks):
                nq = len(blk) * 128
                q0 = blk[0] * 128
                is_ctx = blk[0] >= NT_LAT
                ktiles = list(range(NT_LAT, NT)) if is_ctx else list(range(NT))
                qb = qTb[gi % 2]
                EA = Eacc[gi % 2]
                szb, gtb = szT[gi % 2], gT[gi % 2]
                gi += 1
                dma("sp", qb[:, 0:nq], qT_d[h, :, q0:q0 + nq], [("qT", h, t) for t in blk], [qb])
                dma("sp", szb[:, 0:nq], szT_d[h, :, q0:q0 + nq], [("szT", h, t) for t in blk], [szb])
                npair = len(ktiles)
                bufs = [(psS2[(rr + p) % 2], E2[(rr + p) % 6]) for p in range(npair)]
                rr += npair

                def score(p):
                    kt = ktiles[p]
                    pS = bufs[p][0]
                    for m in range(2):
                        S.add("pe", lambda e, pS=pS, kt=kt, m=m, kTb=kTb, qb=qb, nq=nq: e.matmul(
                            pS[:, m, 0:nq], kTb[m * 64:(m + 1) * 64, kt * 128:(kt + 1) * 128], qb[m * 64:(m + 1) * 64, 0:nq], start=True, stop=True),
                            [kTb, qb], [pS])
                score(0)
                for p in range(npair):
                    kt = ktiles[p]
                    pS, Eb = bufs[p]
                    if p + 1 < npair:
                        score(p + 1)
                    S.add("act", lambda e, pS=pS, Eb=Eb, nq=nq: e.activation(out=Eb[:, :, 0:nq], in_=pS[:, :, 0:nq], func=AF.Exp), [pS], [Eb])
                    for m in range(2):
                        S.add("pe", lambda e, Eb=Eb, kt=kt, m=m, vab=vab, nq=nq, st=(p == 0), sp=(p == npair - 1): e.matmul(
                            psOT[m][:, 0:nq], vab[:, kt, :], Eb[:, m, 0:nq], start=st, stop=sp), [Eb, vab], [psOT[m]])
                    S.add("pe", lambda e, Eb=Eb, nq=nq, st=(p == 0), sp=(p == npair - 1): e.matmul(
                        psZ1[:, 0:nq], ones_b[:, :], Eb[:, 1, 0:nq], start=st, stop=sp), [Eb, ones_b], [psZ1])
                    if p == 0:
                        S.add("dve", lambda e, Eb=Eb, EA=EA, nq=nq: e.tensor_copy(out=EA[:, 0:nq], in_=Eb[:, 0, 0:nq]), [Eb], [EA])
                    else:
                        S.add("dve", lambda e, Eb=Eb, EA=EA, nq=nq: e.tensor_tensor(out=EA[:, 0:nq], in0=EA[:, 0:nq], in1=Eb[:, 0, 0:nq], op=ALU.add),
                              [Eb, EA], [EA])
                for m in range(2):
                    S.add("dve", lambda e, m=m, nq=nq: e.tensor_copy(out=OTs[m][:, 0:nq], in_=psOT[m][:, 0:nq]), [psOT[m]], [OTs[m]])
                S.add("dve", lambda e, nq=nq: e.tensor_copy(out=R[1][:, 0:nq], in_=psZ1[:, 0:nq]), [psZ1], [R[1]])
                S.add("dve", lambda e, nq=nq: e.reciprocal(out=R[1][:, 0:nq], in_=R[1][:, 0:nq]), [R[1]], [R[1]])
                S.add("pe", lambda e, EA=EA, nq=nq: e.matmul(psB[:, 0:nq], ones_f[:, :], EA[:, 0:nq], start=True, stop=True), [ones_f, EA], [psB])
                S.add("dve", lambda e, nq=nq: e.reciprocal(out=R[0][:, 0:nq], in_=psB[:, 0:nq]), [psB], [R[0]])
                S.add("dve", lambda e, nq=nq: e.tensor_tensor(out=ta[:, 0:nq], in0=OTs[0][:, 0:nq], in1=R[0][:, 0:nq], op=ALU.mult), [OTs[0], R[0]], [ta])
                S.add("dve", lambda e, nq=nq: e.tensor_tensor(out=tb[:, 0:nq], in0=OTs[1][:, 0:nq], in1=R[1][:, 0:nq], op=ALU.mult), [OTs[1], R[1]], [tb])
                S.add("dve", lambda e, nq=nq: e.scalar_tensor_tensor(out=oT[:, 0:nq], in0=tb[:, 0:nq], scalar=nlam[:, 0:1], in1=ta[:, 0:nq], op0=ALU.mult, op1=ALU.add),
                      [ta, tb, nlam], [oT])
                S.add("pool", lambda e, nq=nq: e.tensor_tensor(out=sq[:, 0:nq], in0=oT[:, 0:nq], in1=oT[:, 0:nq], op=ALU.mult), [oT], [sq])
                S.add("pe", lambda e, nq=nq: e.matmul(psB[:, 0:nq], ones_f[:, :], sq[:, 0:nq], start=True, stop=True), [ones_f, sq], [psB])
                S.add("act", lambda e, nq=nq: e.activation(out=rstd[:, 0:nq], in_=psB[:, 0:nq], func=AF.Ln, bias=eps_t[:, :], scale=1.0 / 128), [psB, eps_t], [rstd])
                S.add("act", lambda e, nq=nq: e.activation(out=rstd[:, 0:nq], in_=rstd[:, 0:nq], func=AF.Exp, scale=-0.5), [rstd], [rstd])
                S.add("dve", lambda e, nq=nq: e.scalar_tensor_tensor(out=oT[:, 0:nq], in0=oT[:, 0:nq], scalar=snwc[:, 0:1], in1=rstd[:, 0:nq], op0=ALU.mult, op1=ALU.mult),
                      [oT, snwc, rstd], [oT])
                S.add("pool", lambda e, nq=nq, szb=szb, gtb=gtb: e.tensor_tensor(out=gtb[:, 0:nq], in0=oT[:, 0:nq], in1=szb[:, 0:nq], op=ALU.mult), [oT, szb], [gtb])
                dma("pool", gT_d[h, :, q0:q0 + nq], gtb[:, 0:nq], [gtb], [("gT", h, t) for t in blk])

    def attn_win(l, stack, qtiles, klat):
        w = W[l]
        sinkb = sb(stack, "sinkb", [128, 32], F32)
        bload(sinkb, w["sink"][0:1, :])
        S.add("act", lambda e: e.activation(out=sinkb[:, :], in_=sinkb[:, :], func=AF.Exp), [sinkb], [sinkb])
        masks = sb(stack, "masks", [128, 2, 128], BF16)
        dma("sp", masks[:, :, :], masks_in[:, :, :], [], [masks])
        kTh = [sb(stack, "kTh%d" % i, [128, NTOK], BF16) for i in range(2)]
        vaug = [sb(stack, "vaug%d" % i, [128, NT, 129], BF16) for i in range(2)]
        for i in range(2):
            S.add("dve", lambda e, i=i: e.memset(vaug[i][:, :, 128:129], 1.0), [], [vaug[i]])
        qTb = [sb(stack, "qTb%d" % i, [128, 4, 128], BF16) for i in range(2)]
        E = [sb(stack, "E%d" % i, [128, 512], BF16) for i in range(4)]
        psS = [ps(stack, "psS%d" % i, [128, 512], F32) for i in range(3)]
        psO = [ps(stack, "psO%d" % i, [128, 512], F32) for i in range(4)]
        pTg = ps(stack, "pTg", [128, 4, 128], BF16)
        NB = 8
        pend = []
        zr = [sb(stack, "zr%d" % i, [128, 1], F32) for i in range(NB)]
        o_t = [sb(stack, "o_t%d" % i, [128, 128], F32) for i in range(NB)]
        szt = [sb(stack, "szt%d" % i, [128, 128], F32) for i in range(NB)]
        gtm = [sb(stack, "gtm%d" % i, [128, 128], BF16) for i in range(NB)]
        gTs = [sb(stack, "gTs%d" % i, [128, 4, 128], BF16) for i in range(2)]
        rr = 0
        fi = 0
        gi = 0
        for n in range(8):
            kTb, vab = kTh[n % 2], vaug[n % 2]
            load_head_kv(kTb, vab, n, n, [klat, list(range(NT_LAT, NT))])
            for i in qtiles:
                qb = qTb[gi % 2]
                dma("sp", qb[:, :, :], qT_d[n * 4:(n + 1) * 4, :, i * 128:(i + 1) * 128].rearrange("h p t -> p h t"),
                    [("qT", n * 4 + g, i) for g in range(4)], [qb])
                keys = []
                if i > 0:
                    keys.append((i - 1, 0))
                keys.append((i, None))
                if i + 1 <= klat[-1]:
                    keys.append((i + 1, 1))
                keys += [(t, None) for t in range(NT_LAT, NT)]
                pO = [psO[(gi % 2) * 2], psO[(gi % 2) * 2 + 1]]
                bufs = [(psS[(rr + si) % 3], E[(rr + si) % 4]) for si in range(len(keys))]
                rr += len(keys)

                def score(si):
                    kt = keys[si][0]
                    pS = bufs[si][0]
                    S.add("pe", lambda e, pS=pS, kt=kt, kTb=kTb, qb=qb: e.matmul(pS[:, :], kTb[:, kt * 128:(kt + 1) * 128], qb[:, :, :].rearrange("p g t -> p (g t)"),
                                                                  start=True, stop=True), [kTb, qb], [pS])
                score(0)
                for si, (kt, mk) in enumerate(keys):
                    pS, Eb = bufs[si]
                    if si + 1 < len(keys):
                        score(si + 1)
                    S.add("act", lambda e, pS=pS, Eb=Eb: e.activation(out=Eb[:, :], in_=pS[:, :], func=AF.Exp), [pS], [Eb])
                    if mk is not None:
                        S.add("pool", lambda e, Eb=Eb, mk=mk: e.tensor_tensor(out=Eb[:, :].rearrange("p (g t) -> p g t", t=128),
                                                                               in0=Eb[:, :].rearrange("p (g t) -> p g t", t=128),
                                                                               in1=masks[:, mk, :].unsqueeze(1).to_broadcast([128, 4, 128]), op=ALU.mult),
                              [Eb, masks], [Eb])
                    for g in range(4):
                        S.add("pe", lambda e, Eb=Eb, g=g, kt=kt, vab=vab, pOg=pO[g // 2], st=(si == 0 and g % 2 == 0), sp=(si == len(keys) - 1): e.matmul(
                            pOg[:, (g % 2) * 256:(g % 2) * 256 + 129], Eb[:, g * 128:(g + 1) * 128], vab[:, kt, :],
                            start=st, stop=sp, skip_group_check=True), [Eb, vab], [pO[g // 2]])
                gts = gTs[gi % 2]
                gi += 1
                trs = []
                for g in range(4):
                    b = fi % NB
                    fi += 1
                    hq = n * 4 + g
                    O = pO[g // 2]
                    c0 = (g % 2) * 256
                    S.add("dve", lambda e, b=b, O=O, c0=c0, hq=hq: e.tensor_scalar(out=zr[b][:, :], in0=O[:, c0 + 128:c0 + 129], scalar1=sinkb[:, hq:hq + 1], scalar2=None, op0=ALU.add),
                          [O, sinkb], [zr[b]])
                    S.add("dve", lambda e, b=b: e.reciprocal(out=zr[b][:, :], in_=zr[b][:, :]), [zr[b]], [zr[b]])
                    S.add("dve", lambda e, b=b, O=O, c0=c0: e.tensor_scalar(out=o_t[b][:, :], in0=O[:, c0:c0 + 128], scalar1=zr[b][:, 0:1], scalar2=None, op0=ALU.mult),
                          [O, zr[b]], [o_t[b]])
                    dma("sp", szt[b][:, :], sz_d[i * 128:(i + 1) * 128, hq * 128:(hq + 1) * 128], [("sz", hq // 4, i)], [szt[b]])
                    S.add("pool", lambda e, b=b: e.tensor_tensor(out=gtm[b][:, :], in0=o_t[b][:, :], in1=szt[b][:, :], op=ALU.mult), [o_t[b], szt[b]], [gtm[b]])
                    trs.append((b, g))

                def fin(trs=trs, gts=gts, n=n, i=i):
                    for (b, g) in trs:
                        S.add("pe", lambda e, b=b, g=g: e.transpose(out=pTg[:, g, :], in_=gtm[b][:, :], identity=ident[:, :]), [gtm[b], ident], [pTg])
                    S.add("act", lambda e, gts=gts: e.activation(out=gts[:, :, :], in_=pTg[:, :, :], func=AF.Copy), [pTg], [gts])
                    dma("pool", gT_d[n * 4:(n + 1) * 4, :, i * 128:(i + 1) * 128].rearrange("h p t -> p h t"), gts[:, :, :], [gts],
                        [("gT", n * 4 + g, i) for g in range(4)])
                pend.append(fin)
                while len(pend) > 1:
                    pend.pop(0)()
        while pend:
            pend.pop(0)()

    for l in range(nlayers):
        kind = kinds[l]
        need_ctx_out = any(kinds[j] != 0 for j in range(l + 1, 4))
        if l + 1 < nlayers:
            cast_weights(l + 1)
        with ExitStack() as st:
            modulation(l, st)
            S.barrier()
        ctx_t = list(range(NT_LAT, NT))
        ALLR = ("q", "k", "v", "z")
        if l == 0:
            with ExitStack() as st:
                phase_norm(l, st, list(range(NT)))
                S.barrier()
            with ExitStack() as st:
                conv_layer(l, st, True, list(range(NT_LAT)), list(range(NT_LAT)))
                S.barrier()
        elif l == 1:
            H1 = list(range(OWN + 2))
            with ExitStack() as st:
                phase_norm(l, st, list(range(NT)))
                S.barrier()
            roles = ([(s_, "q", s_) for s_ in range(8)] + [(8 + s_, "k", s_) for s_ in range(8)]
                     + [(16 + s_, "v", s_) for s_ in range(8)] + [(24 + s_, "zT", s_) for s_ in range(8)])
            ALLT = ("q", "k", "v", "zT")
            with ExitStack() as st:
                proj_phase(l, st, 64, roles, 0.125, rope64, [(H1, ALLT), (list(range(OWN + 2, NT_LAT)), ("k", "v")), (ctx_t, ALLT)])
                S.barrier()
            with ExitStack() as st:
                attn_diff(l, st, True, H1)
                S.barrier()
            if debug_out != "attn":
                with ExitStack() as st:
                    outproj_phase(l, st, True, H1)
                    S.barrier()
        elif l == 2:
            H1 = list(range(OWN + 2))
            H2 = list(range(OWN + 1))
            with ExitStack() as st:
                phase_norm(l, st, H1 + ctx_t)
                S.barrier()
            roles = ([(s_, "q", s_) for s_ in range(8)] + [(8 + s_, "k", s_) for s_ in range(2)]
                     + [(10 + s_, "v", s_) for s_ in range(2)] + [(12 + s_, "z", s_) for s_ in range(8)])
            with ExitStack() as st:
                proj_phase(l, st, 128, roles, 128 ** -0.5, rope128, [(H1, ALLR), (ctx_t, ("k", "v"))])
                S.barrier()
            with ExitStack() as st:
                attn_win(l, st, H2, H1)
                S.barrier()
            with ExitStack() as st:
                outproj_phase(l, st, False, H2)
                S.barrier()
        else:
            H2 = list(range(OWN + 1))
            with ExitStack() as st:
                phase_norm(l, st, H2)
                S.barrier()
            with ExitStack() as st:
                conv_layer(l, st, False, H2, list(range(OWN)))
                S.barrier()

    if debug_out is not None:
        dbg = dram("dbg", [NT * 128, D], F32, "ExternalOutput")
        with ExitStack() as st:
            t = sb(st, "dbgt", [128, D], F32)
            for tile in range(NT):
                src = xbuf[nlayers % 2][tile * 128:(tile + 1) * 128, :]
                dma("sp", t[:, :], src, [], [t])
                dma("sp", dbg[tile * 128:(tile + 1) * 128, :], t[:, :], [t], [("dbg", tile)])
            S.barrier()

    S.emit(nc, top)
    top.close()
    return nc, S


def make_in_maps(inputs, nlayers=4, cores=range(NCORES)):
    f = lambda a: np.ascontiguousarray(np.asarray(a, dtype=np.float32))
    kinds = [0, 1, 2, 0]
    maps = []
    for c in cores:
        b, mir = c // 2, (c % 2 == 1)
        flip = (lambda a: a[::-1]) if mir else (lambda a: a)
        m = {"x": f(flip(np.asarray(inputs["x"][b]))), "ctx": f(flip(np.asarray(inputs["ctx"][b])))}
        m["ident"] = np.eye(128, dtype=np.float32).astype(ml_dtypes.bfloat16)
        cc = np.stack([np.asarray(inputs["c"][b]), np.asarray(inputs["c_ctx"])], 0)
        m["cT"] = f(cc.reshape(2, KC, 128).transpose(2, 0, 1))
        for l in range(nlayers):
            p = f"l{l}_"
            m[p + "norm"] = f(inputs[p + "norm"]).reshape(1, D)
            m[p + "w_mod"] = f(inputs[p + "w_mod"])
            m[p + "b_mod"] = f(inputs[p + "b_mod"]).reshape(1, 3 * D)
            m[p + "w_in"] = f(inputs[p + "w_in"])
            m[p + "w_out"] = f(inputs[p + "w_out"])
            if kinds[l] == 0:
                cw = np.asarray(inputs[p + "conv_w"])
                if mir:
                    cw = cw[::-1]
                cb = np.asarray(inputs[p + "conv_b"])
                cp = np.concatenate([cw, cb[None, :]], 0)
                m[p + "convp"] = f(cp.reshape(4, 32, 128).transpose(2, 1, 0))
        if nlayers > 1:
            m["rope64"] = rope_table(64, mir)
            m["l1_lamv"] = f(np.concatenate([np.asarray(inputs["l1_lam_" + k_]) for k_ in ("q1", "q2", "k1", "k2")])).reshape(1, 256)
            for k_ in ("q_norm", "k_norm", "sub_norm"):
                m["l1_" + k_] = f(inputs["l1_" + k_]).reshape(1, -1)
        if nlayers > 2:
            m["rope128"] = rope_table(128, mir)
            qq = np.arange(128)[None, :]
            kk = np.arange(128)[:, None]
            mk = np.stack([(qq <= kk), (qq >= kk)], 1).astype(np.float32)
            m["masks"] = mk.astype(ml_dtypes.bfloat16)
            for k_ in ("q_norm", "k_norm", "sink"):
                m["l2_" + k_] = f(inputs["l2_" + k_]).reshape(1, -1)
        maps.append(m)
    return maps


_ROPE_CACHE = {}


def rope_table(head_dim, mirrored=False):
    if (head_dim, mirrored) in _ROPE_CACHE:
        return _ROPE_CACHE[(head_dim, mirrored)]
    rows = T_LAT // 64
    row = np.repeat(np.arange(rows), 64).astype(np.float32)
    col = np.tile(np.arange(64), rows).astype(np.float32)
    n_freq = head_dim // 4
    inv_freq = (np.float32(10000.0) ** (-(np.arange(n_freq, dtype=np.float32) / np.float32(n_freq)))).astype(np.float32)
    ang = np.concatenate([row[:, None] * inv_freq, col[:, None] * inv_freq], axis=-1).astype(np.float32)
    tab = np.zeros((T_LAT + T_CTX, 2, 2 * n_freq), np.float32)
    if mirrored:
        ang = ang[::-1]
    tab[:T_LAT, 0] = np.cos(ang)
    tab[:T_LAT, 1] = np.sin(ang)
    tab[T_LAT:, 0] = 1.0
    _ROPE_CACHE[(head_dim, mirrored)] = tab
    return tab


def kernel(**inputs):
    nc, S = build_program(4)
    maps = make_in_maps(inputs, 4)
    res = run_bass_kernel_spmd(nc, maps, core_ids=list(range(NCORES)))
    out = np.empty((4, T_LAT, D), np.float32)
    half = OWN * 128
    for c in range(NCORES):
        b, mir = c // 2, (c % 2 == 1)
        r = np.asarray(res.results[c]["out"], dtype=np.float32)
        if mir:
            out[b, half:] = r[::-1]
        else:
            out[b, :half] = r
    return out
```

```python
import numpy as np
import ml_dtypes
import concourse.bass as bass
import concourse.mybir as mybir
from concourse.bass_utils import run_bass_kernel_spmd

F32 = mybir.dt.float32
BF16 = mybir.dt.bfloat16
AF = mybir.ActivationFunctionType
ALU = mybir.AluOpType
AX = mybir.AxisListType

D = 2048
DI = 4096
T_LAT = 4096
T_CTX = 256
NT_LAT = T_LAT // 128
NT = (T_LAT + T_CTX) // 128
KC = D // 128
EPS = 1e-6
NCORES = 8
OWN = 16


class _Op:
    __slots__ = ("eng", "fn", "deps", "dma", "sig", "sem", "val", "lane")


class Sched:
    ENGS = ("pe", "act", "dve", "pool", "sp")
    NL = 8

    def __init__(self):
        self.ops = {e: [] for e in self.ENGS}
        self.last_w = {}
        self.rd_eng = {}
        self.rd_dma = {}
        self.lane_rr = {"sp": 0, "pool": 0}
        self.lane_last = {}
        self.last_on = {}

    def add(self, eng, fn, reads=(), writes=(), dma=False):
        op = _Op()
        op.eng, op.fn, op.dma, op.sig, op.sem, op.val, op.lane = eng, fn, dma, False, None, 0, None
        deps = set()
        for r in reads:
            w = self.last_w.get(r)
            if w is not None:
                deps.add(w)
        for r in writes:
            w = self.last_w.get(r)
            if w is not None:
                deps.add(w)
            for o in self.rd_eng.get(r, {}).values():
                deps.add(o)
            for o in self.rd_dma.get(r, ()):
                deps.add(o)
        for r in reads:
            if dma:
                self.rd_dma.setdefault(r, []).append(op)
            else:
                self.rd_eng.setdefault(r, {})[eng] = op
        for r in writes:
            self.last_w[r] = op
            self.rd_eng[r] = {}
            self.rd_dma[r] = []
        if dma:
            lane = (eng, self.lane_rr[eng])
            self.lane_rr[eng] = (self.lane_rr[eng] + 1) % self.NL
            op.lane = lane
            prev = self.lane_last.get(lane)
            if prev is not None:
                deps.add(prev)
            self.lane_last[lane] = op
        if eng == "pe":
            deps = {d for d in deps if d.dma or d.eng != "pe"}
        deps.discard(op)
        op.deps = deps
        self.ops[eng].append(op)
        if not dma:
            self.last_on[eng] = op
        return op

    def barrier(self):
        tails = [o for o in self.last_on.values()]
        for lane, o in self.lane_last.items():
            tails.append(o)
        for e in self.ENGS:
            op = self.add(e, None)
            op.deps = set(t for t in tails)
        self.last_w.clear(); self.rd_eng.clear(); self.rd_dma.clear()

    def emit(self, nc, stack):
        sems = {}
        for e in ("pe", "act", "dve", "pool"):
            sems[e] = stack.enter_context(nc.semaphore("sem_" + e))
        for q in ("sp", "pool"):
            for l in range(self.NL):
                sems[(q, l)] = stack.enter_context(nc.semaphore("lane_%s%d" % (q, l)))
        for e in self.ENGS:
            for op in self.ops[e]:
                for d in op.deps:
                    d.sig = True
        cnt = {k: 0 for k in sems}
        for e in self.ENGS:
            for op in self.ops[e]:
                if op.dma:
                    cnt[op.lane] += 16
                    op.sem, op.val = op.lane, cnt[op.lane]
                elif op.sig and op.fn is not None:
                    cnt[e] += 1
                    op.sem, op.val = e, cnt[e]
        final = dict(cnt)
        block = stack.enter_context(nc.Block())
        engmap = {"pe": block.tensor, "act": block.scalar, "dve": block.vector,
                  "pool": block.gpsimd, "sp": block.sync}
        nwaits = [0]

        def run(ename, e):
            waited = {}
            for op in self.ops[ename]:
                need = {}
                for d in op.deps:
                    if d.sem is None:
                        continue
                    if d.val > need.get(d.sem, 0):
                        need[d.sem] = d.val
                for s, v in need.items():
                    if v > waited.get(s, 0):
                        e.wait_ge(sems[s], v)
                        waited[s] = v
                        nwaits[0] += 1
                if op.fn is None:
                    continue
                ins = op.fn(e)
                if op.dma:
                    ins.then_inc(sems[op.sem], 16)
                elif op.sig:
                    ins.then_inc(sems[op.sem], 1)
            if ename in ("sp", "pool"):
                for l in range(self.NL):
                    if final[(ename, l)] > 0:
                        e.wait_ge(sems[(ename, l)], final[(ename, l)])

        for ename in self.ENGS:
            engmap[ename](lambda e, ename=ename: run(ename, e))
        self.nwaits = nwaits[0]


class Buf:
    def __init__(self, t, name):
        self.t = t
        self.name = name

    def __getitem__(self, idx):
        return self.t[idx]


def build_program(nlayers=4, debug_out=None):
    nc = bass.Bass("TRN2", target_bir_lowering=False)
    S = Sched()
    from contextlib import ExitStack
    top = ExitStack()

    def dram(name, shape, dt, kind="Internal"):
        return nc.dram_tensor(name, list(shape), dt, kind=kind).ap()

    x_in = dram("x", [T_LAT, D], F32, "ExternalInput")
    ctx_in = dram("ctx", [T_CTX, D], F32, "ExternalInput")
    cT_in = dram("cT", [128, 2, KC], F32, "ExternalInput")
    out_d = dram("out", [OWN * 128, D], F32, "ExternalOutput")
    ident_in = dram("ident", [128, 128], BF16, "ExternalInput")
    kinds = [0, 1, 2, 0]
    W = []
    for l in range(nlayers):
        kind = kinds[l]
        n_in = 16384 if kind != 2 else 10240
        w = dict(kind=kind, n_in=n_in)
        w["norm"] = dram(f"l{l}_norm", [1, D], F32, "ExternalInput")
        w["w_mod"] = dram(f"l{l}_w_mod", [D, 3 * D], F32, "ExternalInput")
        w["b_mod"] = dram(f"l{l}_b_mod", [1, 3 * D], F32, "ExternalInput")
        w["w_in"] = dram(f"l{l}_w_in", [D, n_in], F32, "ExternalInput")
        w["w_out"] = dram(f"l{l}_w_out", [DI, D], F32, "ExternalInput")
        if kind == 0:
            w["convp"] = dram(f"l{l}_convp", [128, 32, 4], F32, "ExternalInput")
        w["w_mod_b"] = dram(f"l{l}_w_mod_b", [12, 128, KC, 512], BF16)
        w["w_in_b"] = dram(f"l{l}_w_in_b", [n_in // 512, 128, KC, 512], BF16)
        w["w_out_b"] = dram(f"l{l}_w_out_b", [8, 128, KC, 512], BF16)
        w["mod_d"] = dram(f"l{l}_mod_d", [2, 3 * D], F32)
        W.append(w)

    xbuf = [dram("xbuf0", [NT * 128, D], F32), dram("xbuf1", [NT * 128, D], F32)]
    hT_d = dram("hT_d", [128, KC, NT * 128], BF16)
    UW = 1 + T_LAT + 2 + T_CTX + 1
    u_d = dram("u_d", [32, 128, UW], F32)

    NTOK = NT * 128
    dk = "ExternalOutput" if debug_out in ("full", "attn") else "Internal"
    if debug_out == "attn":
        dbg2 = dram("dbg2", [4, 128, 512], F32, "ExternalOutput")
    qT_d = dram("qT_d", [32, 128, NTOK], BF16, dk)
    kT_d = dram("kT_d", [32, 128, NTOK], BF16, dk)
    v_d = dram("v_d", [NTOK, DI], BF16, dk)
    sz_d = dram("sz_d", [NTOK, DI], F32, dk)
    gT_d = dram("gT_d", [32, 128, NTOK], BF16, dk)
    szT_d = dram("szT_d", [32, 128, NTOK], F32)
    if nlayers > 1:
        rope64 = dram("rope64", [NTOK, 2, 32], F32, "ExternalInput")
        lamv = dram("l1_lamv", [1, 4 * 64], F32, "ExternalInput")
        W[1]["q_norm"] = dram("l1_q_norm", [1, 64], F32, "ExternalInput")
        W[1]["k_norm"] = dram("l1_k_norm", [1, 64], F32, "ExternalInput")
        W[1]["sub_norm"] = dram("l1_sub_norm", [1, 128], F32, "ExternalInput")
    if nlayers > 2:
        rope128 = dram("rope128", [NTOK, 2, 64], F32, "ExternalInput")
        masks_in = dram("masks", [128, 2, 128], BF16, "ExternalInput")
        W[2]["q_norm"] = dram("l2_q_norm", [1, 128], F32, "ExternalInput")
        W[2]["k_norm"] = dram("l2_k_norm", [1, 128], F32, "ExternalInput")
        W[2]["sink"] = dram("l2_sink", [1, 32], F32, "ExternalInput")

    def ucol(tok):
        return 1 + tok if tok < T_LAT else 1 + tok + 2

    uid = [0]

    def sb(stack, name, shape, dt):
        uid[0] += 1
        name = "s%d_%s" % (uid[0], name)
        return Buf(stack.enter_context(nc.sbuf_tensor(name, list(shape), dt)), name)

    def ps(stack, name, shape, dt):
        uid[0] += 1
        name = "p%d_%s" % (uid[0], name)
        return Buf(stack.enter_context(nc.psum_tensor(name, list(shape), dt)), name)

    def dma(q, out_ap, in_ap, reads, writes, **kw):
        return S.add(q, lambda e: e.dma_start(out=out_ap, in_=in_ap, **kw), reads, writes, dma=True)

    def x_src(l, tile):
        if l == 0:
            if tile < NT_LAT:
                return x_in[tile * 128:(tile + 1) * 128, :], None
            return ctx_in[(tile - NT_LAT) * 128:(tile - NT_LAT + 1) * 128, :], None
        return xbuf[l % 2][tile * 128:(tile + 1) * 128, :], ("x", l, tile)

    def x_dst(l, tile):
        if l == nlayers - 1 and tile < OWN and debug_out is None:
            return out_d[tile * 128:(tile + 1) * 128, :], ("out", tile)
        return xbuf[(l + 1) % 2][tile * 128:(tile + 1) * 128, :], ("x", l + 1, tile)

    def cast_weights(l):
        w = W[l]
        for s in range(12):
            src = w["w_mod"][:, s * 512:(s + 1) * 512].rearrange("(k p) n -> p k n", p=128)
            dma("pool", w["w_mod_b"][s], src, [], [("wmod", l, s)])
        for s in range(w["n_in"] // 512):
            src = w["w_in"][:, s * 512:(s + 1) * 512].rearrange("(k p) n -> p k n", p=128)
            dma("pool", w["w_in_b"][s], src, [], [("win", l, s)])
        for n in range(4):
            for kh in range(2):
                src = w["w_out"][kh * 2048:(kh + 1) * 2048, n * 512:(n + 1) * 512].rearrange("(k p) n -> p k n", p=128)
                dma("pool", w["w_out_b"][n * 2 + kh], src, [], [("wout", l, n * 2 + kh)])

    eps_t = sb(top, "eps_t", [128, 1], F32)
    S.add("dve", lambda e: e.memset(eps_t[:, :], EPS), [], [eps_t])
    zcol = sb(top, "zcol", [128, 2], F32)
    S.add("dve", lambda e: e.memset(zcol[:, :], 0.0), [], [zcol])
    for (c0, n) in ((0, 1), (T_LAT + 1, 2), (UW - 1, 1)):
        dma("sp", u_d[:, :, c0:c0 + n].rearrange("j p c -> p j c"),
            zcol[:, 0:n].unsqueeze(1).to_broadcast([128, 32, n]), [zcol], [("upad", c0)], allow_slow_non_contiguous=True)

    ident = sb(top, "ident", [128, 128], BF16)
    dma("sp", ident[:, :], ident_in[:, :], [], [ident])

    cast_weights(0)

    def modulation(l, stack):
        w = W[l]
        cT = sb(stack, "cT", [128, 2, KC], F32)
        sc = sb(stack, "sc", [128, KC, 2], BF16)
        bm = sb(stack, "bm", [2, 3 * D], F32)
        gg = sb(stack, "gg", [2, D], F32)
        msb = sb(stack, "msb", [2, 3 * D], F32)
        wm = [sb(stack, "wm%d" % i, [128, KC, 512], BF16) for i in range(2)]
        pm = [ps(stack, "pm%d" % i, [2, 512], F32) for i in range(2)]
        dma("sp", cT[:, :, :], cT_in[:, :, :], [], [cT])
        dma("sp", bm[:, :], w["b_mod"][0:1, :].partition_broadcast(2).rearrange("p o n -> p (o n)"), [], [bm])
        dma("sp", gg[:, :], w["norm"][0:1, :].partition_broadcast(2).rearrange("p o n -> p (o n)"), [], [gg])
        S.add("act", lambda e: e.activation(out=sc[:, :, :].rearrange("p k r -> p r k"), in_=cT[:, :, :], func=AF.Silu),
              [cT], [sc])
        for s in range(12):
            wb = wm[s % 2]
            pb = pm[s % 2]
            dma("sp", wb[:, :, :], w["w_mod_b"][s], [("wmod", l, s)], [wb])
            for k in range(KC):
                S.add("pe", lambda e, k=k, wb=wb, pb=pb: e.matmul(pb[:, :], sc[:, k, :], wb[:, k, :], start=(k == 0), stop=(k == KC - 1)),
                      [sc, wb], [pb])
            S.add("dve", lambda e, s=s, pb=pb: e.tensor_tensor(out=msb[:, s * 512:(s + 1) * 512], in0=pb[:, :], in1=bm[:, s * 512:(s + 1) * 512], op=ALU.add),
                  [pb, bm], [msb])
        S.add("dve", lambda e: e.scalar_tensor_tensor(out=msb[:, D:2 * D], in0=msb[:, D:2 * D], scalar=1.0, in1=gg[:, :], op0=ALU.add, op1=ALU.mult),
              [msb, gg], [msb])
        md = w["mod_d"]
        dma("sp", md[:, 0:D], msb[:, D:2 * D], [msb], [("mod", l, 0)])
        dma("sp", md[:, D:2 * D], msb[:, 0:D], [msb], [("mod", l, 1)])
        dma("sp", md[:, 2 * D:3 * D], msb[:, 2 * D:3 * D], [msb], [("mod", l, 2)])

    def load_mod(l, tile_buf, row, which):
        md = W[l]["mod_d"]
        src = md[row:row + 1, which * D:(which + 1) * D].partition_broadcast(128).rearrange("p o n -> p (o n)")
        dma("sp", tile_buf[:, :], src, [("mod", l, which)], [tile_buf])

    def phase_norm(l, stack, tiles):
        gs = sb(stack, "gs", [128, D], F32)
        sh = sb(stack, "sh", [128, D], F32)
        xt = [sb(stack, "xt%d" % i, [128, D], F32) for i in range(2)]
        hb = [sb(stack, "hb%d" % i, [128, D], BF16) for i in range(2)]
        junk = sb(stack, "junk", [128, D], BF16)
        ss = [sb(stack, "ss%d" % i, [128, 1], F32) for i in range(2)]
        rs = [sb(stack, "rs%d" % i, [128, 1], F32) for i in range(2)]
        hTs = [sb(stack, "hTs%d" % i, [128, KC, 128], BF16) for i in range(2)]
        pT = [ps(stack, "pT%d" % i, [128, KC, 128], BF16) for i in range(2)]
        cur_row = None
        for n, tile in enumerate(tiles):
            row = 0 if tile < NT_LAT else 1
            if row != cur_row:
                load_mod(l, gs, row, 0)
                load_mod(l, sh, row, 1)
                cur_row = row
            b = n % 2
            x_ap, x_res = x_src(l, tile)
            dma("sp", xt[b][:, :], x_ap, [x_res] if x_res else [], [xt[b]])
            S.add("act", lambda e, b=b: e.activation(out=junk[:, :], in_=xt[b][:, :], func=AF.Square, accum_out=ss[b][:, :]),
                  [xt[b]], [junk, ss[b]])
            S.add("act", lambda e, b=b: e.activation(out=rs[b][:, :], in_=ss[b][:, :], func=AF.Sqrt, bias=eps_t[:, :], scale=1.0 / D),
                  [ss[b], eps_t], [rs[b]])
            S.add("dve", lambda e, b=b: e.reciprocal(out=rs[b][:, :], in_=rs[b][:, :]), [rs[b]], [rs[b]])
            S.add("dve", lambda e, b=b: e.scalar_tensor_tensor(out=xt[b][:, :], in0=xt[b][:, :], scalar=rs[b][:, 0:1], in1=gs[:, :], op0=ALU.mult, op1=ALU.mult),
                  [xt[b], rs[b], gs], [xt[b]])
            S.add("pool", lambda e, b=b: e.tensor_tensor(out=hb[b][:, :], in0=xt[b][:, :], in1=sh[:, :], op=ALU.add),
                  [xt[b], sh], [hb[b]])
            for k in range(KC):
                S.add("pe", lambda e, b=b, k=k: e.transpose(out=pT[b][:, k, :], in_=hb[b][:, k * 128:(k + 1) * 128], identity=ident[:, :]),
                      [hb[b], ident], [pT[b]])
            S.add("act", lambda e, b=b: e.activation(out=hTs[b][:, :, :], in_=pT[b][:, :, :], func=AF.Copy),
                  [pT[b]], [hTs[b]])
            dma("pool", hT_d[:, :, tile * 128:(tile + 1) * 128], hTs[b][:, :, :], [hTs[b]], [("hT", tile)])

    def token_blocks(tiles_lat, tiles_ctx, tb_tiles):
        blocks = []
        for group in (tiles_lat, tiles_ctx):
            for i in range(0, len(group), tb_tiles):
                blocks.append(group[i:i + tb_tiles])
        return blocks

    def load_hT_block(hTb, blk):
        n = len(blk)
        src = hT_d[:, :, blk[0] * 128:(blk[0] + n) * 128]
        dst = hTb[:, :, 0:n * 128]
        dma("sp", dst, src, [("hT", t) for t in blk], [hTb])

    def outproj_block(l, blk, gated, gate, wslab, pY, xo, xn, eit):
        w = W[l]
        for n in range(4):
            for kh in range(2):
                wo = wslab(w["w_out_b"][n * 2 + kh], ("wout", l, n * 2 + kh))
                for ti, tile in enumerate(blk):
                    for k in range(KC):
                        S.add("pe", lambda e, ti=ti, wo=wo, k=k, kh=kh: e.matmul(
                            pY[ti][:, :], gated[:, kh * 16 + k, ti * 128:(ti + 1) * 128], wo[:, k, :],
                            start=(kh == 0 and k == 0), stop=(kh == 1 and k == KC - 1)),
                            [wo, gated], [pY[ti]])
            for ti, tile in enumerate(blk):
                xob, xnb = xo[eit % 2], xn[eit % 2]
                eit += 1
                x_ap, x_res = x_src(l, tile)
                dma("sp", xob[:, :], x_ap[:, n * 512:(n + 1) * 512], [x_res] if x_res else [], [xob])
                S.add("dve", lambda e, xnb=xnb, ti=ti, n=n: e.tensor_tensor(out=xnb[:, :], in0=pY[ti][:, :], in1=gate[:, n * 512:(n + 1) * 512], op=ALU.mult),
                      [pY[ti], gate], [xnb])
                S.add("pool", lambda e, xnb=xnb, xob=xob: e.tensor_tensor(out=xnb[:, :], in0=xnb[:, :], in1=xob[:, :], op=ALU.add),
                      [xnb, xob], [xnb])
                d_ap, d_res = x_dst(l, tile)
                dma("pool", d_ap[:, n * 512:(n + 1) * 512], xnb[:, :], [xnb], [(d_res, n)])
        return eit

    def make_wslab(stack, nbuf=4):
        wsl = [sb(stack, "wsl%d" % i, [128, KC, 512], BF16) for i in range(nbuf)]
        wrr = [0]

        def wslab(src_ap, res):
            b = wsl[wrr[0] % nbuf]
            wrr[0] += 1
            dma("sp", b[:, :, :], src_ap, [res], [b])
            return b
        return wslab

    def outproj_phase(l, stack, need_ctx, lat):
        ctxt = list(range(NT_LAT, NT)) if need_ctx else []
        blocks = token_blocks(lat, ctxt, 4)
        wslab = make_wslab(stack)
        gated = sb(stack, "gated", [128, 32, 512], BF16)
        gate = sb(stack, "gate", [128, D], F32)
        xo = [sb(stack, "xo%d" % i, [128, 512], F32) for i in range(2)]
        xn = [sb(stack, "xn%d" % i, [128, 512], F32) for i in range(2)]
        pY = [ps(stack, "pY%d" % i, [128, 512], F32) for i in range(4)]
        cur_row = None
        eit = 0
        for blk in blocks:
            ntok = len(blk) * 128
            row = 0 if blk[0] < NT_LAT else 1
            if row != cur_row:
                load_mod(l, gate, row, 2)
                cur_row = row
            t0 = blk[0] * 128
            dma("sp", gated[:, :, 0:ntok], gT_d[:, :, t0:t0 + ntok].rearrange("j p t -> p j t"),
                [("gT", j, t) for j in range(32) for t in blk], [gated])
            eit = outproj_block(l, blk, gated, gate, wslab, pY, xo, xn, eit)

    def conv_layer(l, stack, need_ctx, latA, latB):
        w = W[l]
        ctxt = list(range(NT_LAT, NT)) if need_ctx else []
        blocksA = token_blocks(latA, ctxt, 4)
        blocksB = token_blocks(latB, ctxt, 4)
        blocks = blocksA
        wsl = [sb(stack, "wsl%d" % i, [128, KC, 512], BF16) for i in range(4)]
        wrr = [0]

        def wslab(src_ap, res):
            b = wsl[wrr[0] % 4]
            wrr[0] += 1
            dma("sp", b[:, :, :], src_ap, [res], [b])
            return b

        hTb = sb(stack, "hTb", [128, KC, 512], BF16)
        pA = [ps(stack, "pA%d" % i, [128, 512], F32) for i in range(4)]
        cg_sb = [sb(stack, "cg_sb%d" % i, [128, 512], F32) for i in range(2)]
        u_sb = [sb(stack, "u_sb%d" % i, [128, 512], F32) for i in range(2)]
        it = 0
        for blk in blocks:
            ntok = len(blk) * 128
            load_hT_block(hTb, blk)
            for g in range(8):
                wcg = wslab(w["w_in_b"][8 + g], ("win", l, 8 + g))
                wxt = wslab(w["w_in_b"][16 + g], ("win", l, 16 + g))
                for jj in range(4):
                    j = g * 4 + jj
                    p0 = pA[(it * 2) % 4]
                    p1 = pA[(it * 2 + 1) % 4]
                    cs = cg_sb[it % 2]
                    us = u_sb[it % 2]
                    it += 1
                    for (pp, ww) in ((p0, wcg), (p1, wxt)):
                        for k in range(KC):
                            S.add("pe", lambda e, pp=pp, ww=ww, k=k, jj=jj, ntok=ntok: e.matmul(
                                pp[:, 0:ntok], ww[:, k, jj * 128:(jj + 1) * 128], hTb[:, k, 0:ntok], start=(k == 0), stop=(k == KC - 1)),
                                [ww, hTb], [pp])
                    S.add("act", lambda e, cs=cs, p0=p0, ntok=ntok: e.activation(out=cs[:, 0:ntok], in_=p0[:, 0:ntok], func=AF.Copy),
                          [p0], [cs])
                    S.add("dve", lambda e, us=us, cs=cs, p1=p1, ntok=ntok: e.tensor_tensor(out=us[:, 0:ntok], in0=cs[:, 0:ntok], in1=p1[:, 0:ntok], op=ALU.mult),
                          [cs, p1], [us])
                    c0 = ucol(blk[0] * 128)
                    dma("pool", u_d[j, :, c0:c0 + ntok], us[:, 0:ntok], [us], [("u", j, blk[0])])
        gated = sb(stack, "gated", [128, 32, 512], BF16)
        uw = [sb(stack, "uw%d" % i, [128, 514], F32) for i in range(2)]
        yv = [sb(stack, "yv%d" % i, [128, 512], F32) for i in range(2)]
        sz = [sb(stack, "sz%d" % i, [128, 512], F32) for i in range(2)]
        cvp = sb(stack, "cvp", [128, 32, 4], F32)
        gate = sb(stack, "gate", [128, D], F32)
        xo = [sb(stack, "xo%d" % i, [128, 512], F32) for i in range(2)]
        xn = [sb(stack, "xn%d" % i, [128, 512], F32) for i in range(2)]
        pY = [ps(stack, "pY%d" % i, [128, 512], F32) for i in range(4)]
        dma("sp", cvp[:, :, :], w["convp"][:, :, :], [], [cvp])
        cur_row = None
        it = 0
        eit = 0
        for blk in blocksB:
            ntok = len(blk) * 128
            row = 0 if blk[0] < NT_LAT else 1
            if row != cur_row:
                load_mod(l, gate, row, 2)
                cur_row = row
            load_hT_block(hTb, blk)
            c0 = ucol(blk[0] * 128)
            for g in range(8):
                wbg = wslab(w["w_in_b"][g], ("win", l, g))
                wz = wslab(w["w_in_b"][24 + g], ("win", l, 24 + g))
                for jj in range(4):
                    j = g * 4 + jj
                    p0 = pA[(it * 2) % 4]
                    p1 = pA[(it * 2 + 1) % 4]
                    uwb, yb, szb = uw[it % 2], yv[it % 2], sz[it % 2]
                    it += 1
                    ures = [("u", j, bb[0]) for bb in blocksA]
                    dma("sp", uwb[:, 0:ntok + 2], u_d[j, :, c0 - 1:c0 + ntok + 1], ures + [("upad", 0), ("upad", T_LAT + 1), ("upad", UW - 1)], [uwb])
                    for (pp, ww) in ((p0, wbg), (p1, wz)):
                        for k in range(KC):
                            S.add("pe", lambda e, pp=pp, ww=ww, k=k, jj=jj, ntok=ntok: e.matmul(
                                pp[:, 0:ntok], ww[:, k, jj * 128:(jj + 1) * 128], hTb[:, k, 0:ntok], start=(k == 0), stop=(k == KC - 1)),
                                [ww, hTb], [pp])
                    S.add("pool", lambda e, yb=yb, uwb=uwb, j=j, ntok=ntok: e.tensor_scalar(
                        out=yb[:, 0:ntok], in0=uwb[:, 0:ntok], scalar1=cvp[:, j, 0:1], scalar2=cvp[:, j, 3:4], op0=ALU.mult, op1=ALU.add),
                        [uwb, cvp], [yb])
                    S.add("dve", lambda e, yb=yb, uwb=uwb, j=j, ntok=ntok: e.scalar_tensor_tensor(
                        out=yb[:, 0:ntok], in0=uwb[:, 1:ntok + 1], scalar=cvp[:, j, 1:2], in1=yb[:, 0:ntok], op0=ALU.mult, op1=ALU.add),
                        [uwb, cvp, yb], [yb])
                    S.add("dve", lambda e, yb=yb, uwb=uwb, j=j, ntok=ntok: e.scalar_tensor_tensor(
                        out=yb[:, 0:ntok], in0=uwb[:, 2:ntok + 2], scalar=cvp[:, j, 2:3], in1=yb[:, 0:ntok], op0=ALU.mult, op1=ALU.add),
                        [uwb, cvp, yb], [yb])
                    S.add("act", lambda e, szb=szb, p1=p1, ntok=ntok: e.activation(out=szb[:, 0:ntok], in_=p1[:, 0:ntok], func=AF.Silu),
                          [p1], [szb])
                    S.add("dve", lambda e, yb=yb, p0=p0, ntok=ntok: e.tensor_tensor(out=yb[:, 0:ntok], in0=yb[:, 0:ntok], in1=p0[:, 0:ntok], op=ALU.mult),
                          [yb, p0], [yb])
                    S.add("dve", lambda e, yb=yb, szb=szb, j=j, ntok=ntok: e.tensor_tensor(out=gated[:, j, 0:ntok], in0=yb[:, 0:ntok], in1=szb[:, 0:ntok], op=ALU.mult),
                          [yb, szb], [gated])
            eit = outproj_block(l, blk, gated, gate, wslab, pY, xo, xn, eit)

    def bload(tile_buf, src_row_ap, reads=()):
        dma("sp", tile_buf[:, :], src_row_ap.partition_broadcast(128).rearrange("p o n -> p (o n)"), list(reads), [tile_buf])

    def proj_phase(l, stack, hd, roles, qscale, rope_d, groups):
        w = W[l]
        ng = 512 // hd
        nf = hd // 4
        blocks = []
        for (gt, groles) in groups:
            for i in range(0, len(gt), 4):
                blocks.append((gt[i:i + 4], groles))
        wslab = make_wslab(stack)
        hTb = sb(stack, "hTb", [128, KC, 512], BF16)
        nw = {"q": sb(stack, "nwq", [128, hd], F32), "k": sb(stack, "nwk", [128, hd], F32)}
        bload(nw["q"], w["q_norm"][0:1, :])
        bload(nw["k"], w["k_norm"][0:1, :])
        S.add("dve", lambda e: e.tensor_scalar(out=nw["q"][:, :], in0=nw["q"][:, :], scalar1=float(qscale), scalar2=None, op0=ALU.mult),
              [nw["q"]], [nw["q"]])
        pp = [ps(stack, "pp%d" % i, [128, 512], F32) for i in range(4)]
        pT = [ps(stack, "pT%d" % i, [128, 4, 128], BF16) for i in range(2)]
        csx = [sb(stack, "csx%d" % i, [128, ng, 2 * nf], F32) for i in range(4)]
        snx = [sb(stack, "snx%d" % i, [128, ng, 2 * nf], F32) for i in range(4)]
        NB = 4
        qraw = [sb(stack, "qraw%d" % i, [128, 512], F32) for i in range(NB)]
        sq = [sb(stack, "sq%d" % i, [128, 512], F32) for i in range(NB)]
        ss8 = [sb(stack, "ss8%d" % i, [128, ng], F32) for i in range(NB)]
        rs8 = [sb(stack, "rs8%d" % i, [128, ng], F32) for i in range(NB)]
        qn = [sb(stack, "qn%d" % i, [128, 512], F32) for i in range(NB)]
        tA = [sb(stack, "tA%d" % i, [128, 256], F32) for i in range(NB)]
        tB = [sb(stack, "tB%d" % i, [128, 256], F32) for i in range(NB)]
        tC = [sb(stack, "tC%d" % i, [128, 256], F32) for i in range(NB)]
        tD = [sb(stack, "tD%d" % i, [128, 256], F32) for i in range(NB)]
        qr = [sb(stack, "qr%d" % i, [128, 512], BF16) for i in range(NB)]
        qTs = [sb(stack, "qTs%d" % i, [128, 4, 128], BF16) for i in range(NB)]
        vb = [sb(stack, "vb%d" % i, [128, 512], BF16) for i in range(NB)]
        szb = [sb(stack, "szb%d" % i, [128, 512], F32) for i in range(NB)]
        it = 0
        trc = [0]
        pending = []
        for (blk, broles) in blocks:
            ntok = len(blk) * 128
            load_hT_block(hTb, blk)
            for ti, tile in enumerate(blk):
                t0 = tile * 128
                dma("sp", csx[ti][:, :, :], rope_d[t0:t0 + 128, 0, :].unsqueeze(1).to_broadcast([128, ng, 2 * nf]), [], [csx[ti]])
                dma("sp", snx[ti][:, :, :], rope_d[t0:t0 + 128, 1, :].unsqueeze(1).to_broadcast([128, ng, 2 * nf]), [], [snx[ti]])
            for (s_idx, role, base) in roles:
                if role not in broles:
                    continue
                wsb = wslab(w["w_in_b"][s_idx], ("win", l, s_idx))
                if role == "zT":
                    tb0 = blk[0] * 128
                    for c in range(4):
                        p = pp[it % 4]
                        b = it % NB
                        it += 1
                        for k in range(KC):
                            S.add("pe", lambda e, p=p, k=k, c=c, wsb=wsb, ntok=ntok: e.matmul(
                                p[:, 0:ntok], wsb[:, k, c * 128:(c + 1) * 128], hTb[:, k, 0:ntok], start=(k == 0), stop=(k == KC - 1)),
                                [hTb, wsb], [p])
                        S.add("act", lambda e, p=p, b=b, ntok=ntok: e.activation(out=szb[b][:, 0:ntok], in_=p[:, 0:ntok], func=AF.Silu), [p], [szb[b]])
                        dma("pool", szT_d[base * 4 + c, :, tb0:tb0 + ntok], szb[b][:, 0:ntok], [szb[b]], [("szT", base * 4 + c, t) for t in blk])
                    continue
                for ti, tile in enumerate(blk):
                    t0 = tile * 128
                    p = pp[it % 4]
                    b = it % NB
                    it += 1
                    for k in range(KC):
                        S.add("pe", lambda e, p=p, k=k, ti=ti, wsb=wsb: e.matmul(
                            p[:, :], hTb[:, k, ti * 128:(ti + 1) * 128], wsb[:, k, :], start=(k == 0), stop=(k == KC - 1)),
                            [hTb, wsb], [p])
                    if role == "v":
                        S.add("act", lambda e, p=p, b=b: e.activation(out=vb[b][:, :], in_=p[:, :], func=AF.Copy), [p], [vb[b]])
                        dma("pool", v_d[t0:t0 + 128, base * 512:(base + 1) * 512], vb[b][:, :], [vb[b]], [("v", base, tile)])
                    elif role == "z":
                        S.add("act", lambda e, p=p, b=b: e.activation(out=szb[b][:, :], in_=p[:, :], func=AF.Silu), [p], [szb[b]])
                        dma("pool", sz_d[t0:t0 + 128, base * 512:(base + 1) * 512], szb[b][:, :], [szb[b]], [("sz", base, tile)])
                    else:
                        S.add("act", lambda e, p=p, b=b: e.activation(out=sq[b][:, :], in_=p[:, :], func=AF.Square), [p], [sq[b]])
                        S.add("act", lambda e, p=p, b=b: e.activation(out=qraw[b][:, :], in_=p[:, :], func=AF.Copy), [p], [qraw[b]])
                        p3 = qraw[b][:, :].rearrange("p (g d) -> p g d", d=hd)
                        S.add("dve", lambda e, b=b: e.tensor_reduce(out=ss8[b][:, :], in_=sq[b][:, :].rearrange("p (g d) -> p g d", d=hd), axis=AX.X, op=ALU.add),
                              [sq[b]], [ss8[b]])
                        S.add("act", lambda e, b=b: e.activation(out=rs8[b][:, :], in_=ss8[b][:, :], func=AF.Sqrt, bias=eps_t[:, :], scale=1.0 / hd),
                              [ss8[b], eps_t], [rs8[b]])
                        S.add("dve", lambda e, b=b: e.reciprocal(out=rs8[b][:, :], in_=rs8[b][:, :]), [rs8[b]], [rs8[b]])
                        S.add("dve", lambda e, b=b, p3=p3: e.tensor_tensor(out=qn[b][:, :].rearrange("p (g d) -> p g d", d=hd), in0=p3,
                                                                            in1=rs8[b][:, :].unsqueeze(2).to_broadcast([128, ng, hd]), op=ALU.mult),
                              [qraw[b], rs8[b]], [qn[b]])
                        nwt = nw[role]
                        S.add("pool", lambda e, b=b, nwt=nwt: e.tensor_tensor(out=qn[b][:, :].rearrange("p (g d) -> p g d", d=hd),
                                                                               in0=qn[b][:, :].rearrange("p (g d) -> p g d", d=hd),
                                                                               in1=nwt[:, :].unsqueeze(1).to_broadcast([128, ng, hd]), op=ALU.mult),
                              [qn[b], nwt], [qn[b]])
                        qv = qn[b][:, :].rearrange("p (ga h f) -> p ga h f", h=2, f=nf)
                        ov = qr[b][:, :].rearrange("p (ga h f) -> p ga h f", h=2, f=nf)
                        t1, t2 = qv[:, :, 0, :], qv[:, :, 1, :]
                        cs = csx[ti][:, :, :].rearrange("p g (a f) -> p (g a) f", f=nf)
                        sn = snx[ti][:, :, :].rearrange("p g (a f) -> p (g a) f", f=nf)
                        v3 = lambda tb: tb[:, :].rearrange("p (ga f) -> p ga f", f=nf)
                        S.add("pool", lambda e, b=b, t1=t1, cs=cs: e.tensor_tensor(out=v3(tA[b]), in0=t1, in1=cs, op=ALU.mult), [qn[b], csx[ti]], [tA[b]])
                        S.add("pool", lambda e, b=b, t2=t2, sn=sn: e.tensor_tensor(out=v3(tB[b]), in0=t2, in1=sn, op=ALU.mult), [qn[b], snx[ti]], [tB[b]])
                        S.add("dve", lambda e, b=b, ov=ov: e.tensor_tensor(out=ov[:, :, 0, :], in0=v3(tA[b]), in1=v3(tB[b]), op=ALU.subtract), [tA[b], tB[b]], [qr[b]])
                        S.add("dve", lambda e, b=b, t1=t1, sn=sn: e.tensor_tensor(out=v3(tC[b]), in0=t1, in1=sn, op=ALU.mult), [qn[b], snx[ti]], [tC[b]])
                        S.add("pool", lambda e, b=b, t2=t2, cs=cs: e.tensor_tensor(out=v3(tD[b]), in0=t2, in1=cs, op=ALU.mult), [qn[b], csx[ti]], [tD[b]])
                        S.add("dve", lambda e, b=b, ov=ov: e.tensor_tensor(out=ov[:, :, 1, :], in0=v3(tC[b]), in1=v3(tD[b]), op=ALU.add), [tC[b], tD[b]], [qr[b]])
                        def fin(b=b, role=role, base=base, tile=tile, t0=t0):
                            pTb = pT[trc[0] % 2]
                            trc[0] += 1
                            for c in range(4):
                                S.add("pe", lambda e, b=b, c=c, pTb=pTb: e.transpose(out=pTb[:, c, :], in_=qr[b][:, c * 128:(c + 1) * 128], identity=ident[:, :]),
                                      [qr[b], ident], [pTb])
                            S.add("act", lambda e, b=b, pTb=pTb: e.activation(out=qTs[b][:, :, :], in_=pTb[:, :, :], func=AF.Copy), [pTb], [qTs[b]])
                            dst_t = qT_d if role == "q" else kT_d
                            dma("pool", dst_t[base * 4:(base + 1) * 4, :, t0:t0 + 128].rearrange("h p t -> p h t"), qTs[b][:, :, :], [qTs[b]],
                                [(role + "T", base * 4 + c, tile) for c in range(4)])
                        pending.append(fin)
                        while len(pending) > 2:
                            pending.pop(0)()
        while pending:
            pending.pop(0)()

    def load_head_kv(kTb, vab, kidx, vcol, ranges):
        for tiles in ranges:
            load_head_kv1(kTb, vab, kidx, vcol, tiles)

    def load_head_kv1(kTb, vab, kidx, vcol, tiles):
        t0, t1 = tiles[0] * 128, (tiles[-1] + 1) * 128
        dma("sp", kTb[:, t0:t1], kT_d[kidx, :, t0:t1], [("kT", kidx, t) for t in tiles], [kTb])
        dma("sp", vab[:, tiles[0]:tiles[-1] + 1, 0:128], v_d[t0:t1, vcol * 128:(vcol + 1) * 128].rearrange("(t p) e -> p t e", p=128),
            [("v", vcol // 4, t) for t in tiles], [vab])

    def attn_diff(l, stack, need_ctx, qlat):
        import math
        w = W[l]
        lam_init = 0.8 - 0.6 * math.exp(-0.3 * l)
        lqk = sb(stack, "lqk", [128, 4 * 64], F32)
        bload(lqk, lamv[0:1, :])
        prod = sb(stack, "prod", [128, 2, 64], F32)
        s2 = sb(stack, "s2", [128, 2], F32)
        e2 = sb(stack, "e2", [128, 2], F32)
        nlam = sb(stack, "nlam", [128, 1], F32)
        S.add("dve", lambda e: e.tensor_tensor(out=prod[:, :, :], in0=lqk[:, 0:128].rearrange("p (a d) -> p a d", d=64),
                                                in1=lqk[:, 128:256].rearrange("p (a d) -> p a d", d=64), op=ALU.mult), [lqk], [prod])
        S.add("dve", lambda e: e.tensor_reduce(out=s2[:, :], in_=prod[:, :, :], axis=AX.X, op=ALU.add), [prod], [s2])
        S.add("act", lambda e: e.activation(out=e2[:, :], in_=s2[:, :], func=AF.Exp), [s2], [e2])
        S.add("dve", lambda e: e.tensor_tensor(out=nlam[:, :], in0=e2[:, 1:2], in1=e2[:, 0:1], op=ALU.subtract), [e2], [nlam])
        S.add("dve", lambda e: e.tensor_scalar(out=nlam[:, :], in0=nlam[:, :], scalar1=float(-lam_init), scalar2=None, op0=ALU.add), [nlam], [nlam])
        snwc = sb(stack, "snwc", [128, 1], F32)
        dma("sp", snwc[:, :], w["sub_norm"][0:1, :].rearrange("o n -> n o"), [], [snwc], allow_slow_non_contiguous=True)
        S.add("dve", lambda e: e.tensor_scalar(out=snwc[:, :], in0=snwc[:, :], scalar1=float(1.0 - lam_init), scalar2=None, op0=ALU.mult), [snwc], [snwc])
        ones_f = sb(stack, "ones_f", [128, 128], F32)
        S.add("dve", lambda e: e.memset(ones_f[:, :], 1.0), [], [ones_f])

        kTh = [sb(stack, "kTh%d" % i, [128, NTOK], BF16) for i in range(2)]
        vt = [sb(stack, "vt%d" % i, [128, NT, 128], BF16) for i in range(2)]
        qTb = [sb(stack, "qTb%d" % i, [128, 512], BF16) for i in range(2)]
        E2 = [sb(stack, "E2_%d" % i, [128, 2, 512], BF16) for i in range(6)]
        Eacc = [sb(stack, "Eacc%d" % i, [128, 512], F32) for i in range(2)]
        ones_b = sb(stack, "ones_b", [128, 128], BF16)
        S.add("dve", lambda e: e.memset(ones_b[:, :], 1.0), [], [ones_b])
        psZ1 = ps(stack, "psZ1", [128, 512], F32)
        psS2 = [ps(stack, "psS2_%d" % i, [128, 2, 512], F32) for i in range(2)]
        psOT = [ps(stack, "psOT%d" % m, [128, 512], F32) for m in range(2)]
        psB = ps(stack, "psB", [128, 512], F32)
        OTs = [sb(stack, "OTs%d" % m, [128, 512], F32) for m in range(2)]
        R = [sb(stack, "R%d" % m, [128, 512], F32) for m in range(2)]
        ta = sb(stack, "ta", [128, 512], F32)
        tb = sb(stack, "tb", [128, 512], F32)
        oT = sb(stack, "oT", [128, 512], F32)
        sq = sb(stack, "sq", [128, 512], F32)
        rstd = sb(stack, "rstd", [128, 512], F32)
        szT = [sb(stack, "szT%d" % i, [128, 512], F32) for i in range(2)]
        gT = [sb(stack, "gT%d" % i, [128, 512], BF16) for i in range(2)]
        qblocks = token_blocks(qlat, list(range(NT_LAT, NT)) if need_ctx else [], 4)
        rr = 0
        gi = 0
        for h in range(32):
            kTb, vab = kTh[h % 2], vt[h % 2]
            load_head_kv(kTb, vab, h, h, [list(range(NT))])
            for bi, blk in enumerate(qblocks):
                nq = len(blk) * 128
                q0 = blk[0] * 128
                is_ctx = blk[0] >= NT_LAT
                ktiles = list(range(NT_LAT, NT)) if is_ctx else list(range(NT))
                qb = qTb[gi % 2]
                EA = Eacc[gi % 2]
                szb, gtb = szT[gi % 2], gT[gi % 2]
                gi += 1
                dma("sp", qb[:, 0:nq], qT_d[h, :, q0:q0 + nq], [("qT", h, t) for t in blk], [qb])
                dma("sp", szb[:, 0:nq], szT_d[h, :, q0:q0 + nq], [("szT", h, t) for t in blk], [szb])
                npair = len(ktiles)
                bufs = [(psS2[(rr + p) % 2], E2[(rr + p) % 6]) for p in range(npair)]
                rr += npair

                def score(p):
                    kt = ktiles[p]
                    pS = bufs[p][0]
                    for m in range(2):
                        S.add("pe", lambda e, pS=pS, kt=kt, m=m, kTb=kTb, qb=qb, nq=nq: e.matmul(
                            pS[:, m, 0:nq], kTb[m * 64:(m + 1) * 64, kt * 128:(kt + 1) * 128], qb[m * 64:(m + 1) * 64, 0:nq], start=True, stop=True),
                            [kTb, qb], [pS])
                score(0)
                for p in range(npair):
                    kt = ktiles[p]
                    pS, Eb = bufs[p]
                    if p + 1 < npair:
                        score(p + 1)
                    S.add("act", lambda e, pS=pS, Eb=Eb, nq=nq: e.activation(out=Eb[:, :, 0:nq], in_=pS[:, :, 0:nq], func=AF.Exp), [pS], [Eb])
                    for m in range(2):
                        S.add("pe", lambda e, Eb=Eb, kt=kt, m=m, vab=vab, nq=nq, st=(p == 0), sp=(p == npair - 1): e.matmul(
                            psOT[m][:, 0:nq], vab[:, kt, :], Eb[:, m, 0:nq], start=st, stop=sp), [Eb, vab], [psOT[m]])
                    S.add("pe", lambda e, Eb=Eb, nq=nq, st=(p == 0), sp=(p == npair - 1): e.matmul(
                        psZ1[:, 0:nq], ones_b[:, :], Eb[:, 1, 0:nq], start=st, stop=sp), [Eb, ones_b], [psZ1])
                    if p == 0:
                        S.add("dve", lambda e, Eb=Eb, EA=EA, nq=nq: e.tensor_copy(out=EA[:, 0:nq], in_=Eb[:, 0, 0:nq]), [Eb], [EA])
                    else:
                        S.add("dve", lambda e, Eb=Eb, EA=EA, nq=nq: e.tensor_tensor(out=EA[:, 0:nq], in0=EA[:, 0:nq], in1=Eb[:, 0, 0:nq], op=ALU.add),
                              [Eb, EA], [EA])
                for m in range(2):
                    S.add("dve", lambda e, m=m, nq=nq: e.tensor_copy(out=OTs[m][:, 0:nq], in_=psOT[m][:, 0:nq]), [psOT[m]], [OTs[m]])
                S.add("dve", lambda e, nq=nq: e.tensor_copy(out=R[1][:, 0:nq], in_=psZ1[:, 0:nq]), [psZ1], [R[1]])
                S.add("dve", lambda e, nq=nq: e.reciprocal(out=R[1][:, 0:nq], in_=R[1][:, 0:nq]), [R[1]], [R[1]])
                S.add("pe", lambda e, EA=EA, nq=nq: e.matmul(psB[:, 0:nq], ones_f[:, :], EA[:, 0:nq], start=True, stop=True), [ones_f, EA], [psB])
                S.add("dve", lambda e, nq=nq: e.reciprocal(out=R[0][:, 0:nq], in_=psB[:, 0:nq]), [psB], [R[0]])
                S.add("dve", lambda e, nq=nq: e.tensor_tensor(out=ta[:, 0:nq], in0=OTs[0][:, 0:nq], in1=R[0][:, 0:nq], op=ALU.mult), [OTs[0], R[0]], [ta])
                S.add("dve", lambda e, nq=nq: e.tensor_tensor(out=tb[:, 0:nq], in0=OTs[1][:, 0:nq], in1=R[1][:, 0:nq], op=ALU.mult), [OTs[1], R[1]], [tb])
                S.add("dve", lambda e, nq=nq: e.scalar_tensor_tensor(out=oT[:, 0:nq], in0=tb[:, 0:nq], scalar=nlam[:, 0:1], in1=ta[:, 0:nq], op0=ALU.mult, op1=ALU.add),
                      [ta, tb, nlam], [oT])
                S.add("dve", lambda e, nq=nq: e.tensor_tensor(out=sq[:, 0:nq], in0=oT[:, 0:nq], in1=oT[:, 0:nq], op=ALU.mult), [oT], [sq])
                S.add("pe", lambda e, nq=nq: e.matmul(psB[:, 0:nq], ones_f[:, :], sq[:, 0:nq], start=True, stop=True), [ones_f, sq], [psB])
                S.add("act", lambda e, nq=nq: e.activation(out=rstd[:, 0:nq], in_=psB[:, 0:nq], func=AF.Ln, bias=eps_t[:, :], scale=1.0 / 128), [psB, eps_t], [rstd])
                S.add("act", lambda e, nq=nq: e.activation(out=rstd[:, 0:nq], in_=rstd[:, 0:nq], func=AF.Exp, scale=-0.5), [rstd], [rstd])
                S.add("dve", lambda e, nq=nq: e.scalar_tensor_tensor(out=oT[:, 0:nq], in0=oT[:, 0:nq], scalar=snwc[:, 0:1], in1=rstd[:, 0:nq], op0=ALU.mult, op1=ALU.mult),
                      [oT, snwc, rstd], [oT])
                S.add("dve", lambda e, nq=nq, szb=szb, gtb=gtb: e.tensor_tensor(out=gtb[:, 0:nq], in0=oT[:, 0:nq], in1=szb[:, 0:nq], op=ALU.mult), [oT, szb], [gtb])
                dma("sp", gT_d[h, :, q0:q0 + nq], gtb[:, 0:nq], [gtb], [("gT", h, t) for t in blk])

    def attn_win(l, stack, qtiles, klat):
        w = W[l]
        sinkb = sb(stack, "sinkb", [128, 32], F32)
        bload(sinkb, w["sink"][0:1, :])
        S.add("act", lambda e: e.activation(out=sinkb[:, :], in_=sinkb[:, :], func=AF.Exp), [sinkb], [sinkb])
        masks = sb(stack, "masks", [128, 2, 128], BF16)
        dma("sp", masks[:, :, :], masks_in[:, :, :], [], [masks])
        kTh = [sb(stack, "kTh%d" % i, [128, NTOK], BF16) for i in range(2)]
        vaug = [sb(stack, "vaug%d" % i, [128, NT, 129], BF16) for i in range(2)]
        for i in range(2):
            S.add("dve", lambda e, i=i: e.memset(vaug[i][:, :, 128:129], 1.0), [], [vaug[i]])
        qTb = [sb(stack, "qTb%d" % i, [128, 4, 128], BF16) for i in range(2)]
        E = [sb(stack, "E%d" % i, [128, 512], BF16) for i in range(4)]
        psS = [ps(stack, "psS%d" % i, [128, 512], F32) for i in range(3)]
        psO = [ps(stack, "psO%d" % i, [128, 512], F32) for i in range(4)]
        pTg = ps(stack, "pTg", [128, 4, 128], BF16)
        NB = 8
        pend = []
        zr = [sb(stack, "zr%d" % i, [128, 1], F32) for i in range(NB)]
        o_t = [sb(stack, "o_t%d" % i, [128, 128], F32) for i in range(NB)]
        szt = [sb(stack, "szt%d" % i, [128, 128], F32) for i in range(NB)]
        gtm = [sb(stack, "gtm%d" % i, [128, 128], BF16) for i in range(NB)]
        gTs = [sb(stack, "gTs%d" % i, [128, 4, 128], BF16) for i in range(2)]
        rr = 0
        fi = 0
        gi = 0
        for n in range(8):
            kTb, vab = kTh[n % 2], vaug[n % 2]
            load_head_kv(kTb, vab, n, n, [klat, list(range(NT_LAT, NT))])
            for i in qtiles:
                qb = qTb[gi % 2]
                dma("sp", qb[:, :, :], qT_d[n * 4:(n + 1) * 4, :, i * 128:(i + 1) * 128].rearrange("h p t -> p h t"),
                    [("qT", n * 4 + g, i) for g in range(4)], [qb])
                keys = []
                if i > 0:
                    keys.append((i - 1, 0))
                keys.append((i, None))
                if i + 1 <= klat[-1]:
                    keys.append((i + 1, 1))
                keys += [(t, None) for t in range(NT_LAT, NT)]
                pO = [psO[(gi % 2) * 2], psO[(gi % 2) * 2 + 1]]
                bufs = [(psS[(rr + si) % 3], E[(rr + si) % 4]) for si in range(len(keys))]
                rr += len(keys)

                def score(si):
                    kt = keys[si][0]
                    pS = bufs[si][0]
                    S.add("pe", lambda e, pS=pS, kt=kt, kTb=kTb, qb=qb: e.matmul(pS[:, :], kTb[:, kt * 128:(kt + 1) * 128], qb[:, :, :].rearrange("p g t -> p (g t)"),
                                                                  start=True, stop=True), [kTb, qb], [pS])
                score(0)
                for si, (kt, mk) in enumerate(keys):
                    pS, Eb = bufs[si]
                    if si + 1 < len(keys):
                        score(si + 1)
                    S.add("act", lambda e, pS=pS, Eb=Eb: e.activation(out=Eb[:, :], in_=pS[:, :], func=AF.Exp), [pS], [Eb])
                    if mk is not None:
                        S.add("pool", lambda e, Eb=Eb, mk=mk: e.tensor_tensor(out=Eb[:, :].rearrange("p (g t) -> p g t", t=128),
                                                                               in0=Eb[:, :].rearrange("p (g t) -> p g t", t=128),
                                                                               in1=masks[:, mk, :].unsqueeze(1).to_broadcast([128, 4, 128]), op=ALU.mult),
                              [Eb, masks], [Eb])
                    for g in range(4):
                        S.add("pe", lambda e, Eb=Eb, g=g, kt=kt, vab=vab, pOg=pO[g // 2], st=(si == 0 and g % 2 == 0), sp=(si == len(keys) - 1): e.matmul(
                            pOg[:, (g % 2) * 256:(g % 2) * 256 + 129], Eb[:, g * 128:(g + 1) * 128], vab[:, kt, :],
                            start=st, stop=sp, skip_group_check=True), [Eb, vab], [pO[g // 2]])
                gts = gTs[gi % 2]
                gi += 1
                trs = []
                for g in range(4):
                    b = fi % NB
                    fi += 1
                    hq = n * 4 + g
                    O = pO[g // 2]
                    c0 = (g % 2) * 256
                    S.add("dve", lambda e, b=b, O=O, c0=c0, hq=hq: e.tensor_scalar(out=zr[b][:, :], in0=O[:, c0 + 128:c0 + 129], scalar1=sinkb[:, hq:hq + 1], scalar2=None, op0=ALU.add),
                          [O, sinkb], [zr[b]])
                    S.add("dve", lambda e, b=b: e.reciprocal(out=zr[b][:, :], in_=zr[b][:, :]), [zr[b]], [zr[b]])
                    S.add("dve", lambda e, b=b, O=O, c0=c0: e.tensor_scalar(out=o_t[b][:, :], in0=O[:, c0:c0 + 128], scalar1=zr[b][:, 0:1], scalar2=None, op0=ALU.mult),
                          [O, zr[b]], [o_t[b]])
                    dma("sp", szt[b][:, :], sz_d[i * 128:(i + 1) * 128, hq * 128:(hq + 1) * 128], [("sz", hq // 4, i)], [szt[b]])
                    S.add("pool", lambda e, b=b: e.tensor_tensor(out=gtm[b][:, :], in0=o_t[b][:, :], in1=szt[b][:, :], op=ALU.mult), [o_t[b], szt[b]], [gtm[b]])
                    trs.append((b, g))

                def fin(trs=trs, gts=gts, n=n, i=i):
                    for (b, g) in trs:
                        S.add("pe", lambda e, b=b, g=g: e.transpose(out=pTg[:, g, :], in_=gtm[b][:, :], identity=ident[:, :]), [gtm[b], ident], [pTg])
                    S.add("act", lambda e, gts=gts: e.activation(out=gts[:, :, :], in_=pTg[:, :, :], func=AF.Copy), [pTg], [gts])
                    dma("pool", gT_d[n * 4:(n + 1) * 4, :, i * 128:(i + 1) * 128].rearrange("h p t -> p h t"), gts[:, :, :], [gts],
                        [("gT", n * 4 + g, i) for g in range(4)])
                pend.append(fin)
                while len(pend) > 1:
                    pend.pop(0)()
        while pend:
            pend.pop(0)()

    for l in range(nlayers):
        kind = kinds[l]
        need_ctx_out = any(kinds[j] != 0 for j in range(l + 1, 4))
        if l == 0 and nlayers > 1:
            cast_weights(1)
        with ExitStack() as st:
            modulation(l, st)
            S.barrier()
        ctx_t = list(range(NT_LAT, NT))
        ALLR = ("q", "k", "v", "z")
        if l == 0:
            with ExitStack() as st:
                phase_norm(l, st, list(range(NT)))
                S.barrier()
            with ExitStack() as st:
                conv_layer(l, st, True, list(range(NT_LAT)), list(range(NT_LAT)))
                S.barrier()
        elif l == 1:
            H1 = list(range(OWN + 2))
            with ExitStack() as st:
                phase_norm(l, st, list(range(NT)))
                S.barrier()
            roles = ([(s_, "q", s_) for s_ in range(8)] + [(8 + s_, "k", s_) for s_ in range(8)]
                     + [(16 + s_, "v", s_) for s_ in range(8)] + [(24 + s_, "zT", s_) for s_ in range(8)])
            ALLT = ("q", "k", "v", "zT")
            with ExitStack() as st:
                proj_phase(l, st, 64, roles, 0.125, rope64, [(H1, ALLT), (list(range(OWN + 2, NT_LAT)), ("k", "v")), (ctx_t, ALLT)])
                S.barrier()
            for l2 in range(2, nlayers):
                cast_weights(l2)
            with ExitStack() as st:
                attn_diff(l, st, True, H1)
                S.barrier()
            if debug_out != "attn":
                with ExitStack() as st:
                    outproj_phase(l, st, True, H1)
                    S.barrier()
        elif l == 2:
            H1 = list(range(OWN + 2))
            H2 = list(range(OWN + 1))
            with ExitStack() as st:
                phase_norm(l, st, H1 + ctx_t)
                S.barrier()
            roles = ([(s_, "q", s_) for s_ in range(8)] + [(8 + s_, "k", s_) for s_ in range(2)]
                     + [(10 + s_, "v", s_) for s_ in range(2)] + [(12 + s_, "z", s_) for s_ in range(8)])
            with ExitStack() as st:
                proj_phase(l, st, 128, roles, 128 ** -0.5, rope128, [(H1, ALLR), (ctx_t, ("k", "v"))])
                S.barrier()
            with ExitStack() as st:
                attn_win(l, st, H2, H1)
                S.barrier()
            with ExitStack() as st:
                outproj_phase(l, st, False, H2)
                S.barrier()
        else:
            H2 = list(range(OWN + 1))
            with ExitStack() as st:
                phase_norm(l, st, H2)
                S.barrier()
            with ExitStack() as st:
                conv_layer(l, st, False, H2, list(range(OWN)))
                S.barrier()

    if debug_out is not None:
        dbg = dram("dbg", [NT * 128, D], F32, "ExternalOutput")
        with ExitStack() as st:
            t = sb(st, "dbgt", [128, D], F32)
            for tile in range(NT):
                src = xbuf[nlayers % 2][tile * 128:(tile + 1) * 128, :]
                dma("sp", t[:, :], src, [], [t])
                dma("sp", dbg[tile * 128:(tile + 1) * 128, :], t[:, :], [t], [("dbg", tile)])
            S.barrier()

    S.emit(nc, top)
    top.close()
    return nc, S


def make_in_maps(inputs, nlayers=4, cores=range(NCORES)):
    f = lambda a: np.ascontiguousarray(np.asarray(a, dtype=np.float32))
    kinds = [0, 1, 2, 0]
    maps = []
    for c in cores:
        b, mir = c // 2, (c % 2 == 1)
        flip = (lambda a: a[::-1]) if mir else (lambda a: a)
        m = {"x": f(flip(np.asarray(inputs["x"][b]))), "ctx": f(flip(np.asarray(inputs["ctx"][b])))}
        m["ident"] = np.eye(128, dtype=np.float32).astype(ml_dtypes.bfloat16)
        cc = np.stack([np.asarray(inputs["c"][b]), np.asarray(inputs["c_ctx"])], 0)
        m["cT"] = f(cc.reshape(2, KC, 128).transpose(2, 0, 1))
        for l in range(nlayers):
            p = f"l{l}_"
            m[p + "norm"] = f(inputs[p + "norm"]).reshape(1, D)
            m[p + "w_mod"] = f(inputs[p + "w_mod"])
            m[p + "b_mod"] = f(inputs[p + "b_mod"]).reshape(1, 3 * D)
            m[p + "w_in"] = f(inputs[p + "w_in"])
            m[p + "w_out"] = f(inputs[p + "w_out"])
            if kinds[l] == 0:
                cw = np.asarray(inputs[p + "conv_w"])
                if mir:
                    cw = cw[::-1]
                cb = np.asarray(inputs[p + "conv_b"])
                cp = np.concatenate([cw, cb[None, :]], 0)
                m[p + "convp"] = f(cp.reshape(4, 32, 128).transpose(2, 1, 0))
        if nlayers > 1:
            m["rope64"] = rope_table(64, mir)
            m["l1_lamv"] = f(np.concatenate([np.asarray(inputs["l1_lam_" + k_]) for k_ in ("q1", "q2", "k1", "k2")])).reshape(1, 256)
            for k_ in ("q_norm", "k_norm", "sub_norm"):
                m["l1_" + k_] = f(inputs["l1_" + k_]).reshape(1, -1)
        if nlayers > 2:
            m["rope128"] = rope_table(128, mir)
            qq = np.arange(128)[None, :]
            kk = np.arange(128)[:, None]
            mk = np.stack([(qq <= kk), (qq >= kk)], 1).astype(np.float32)
            m["masks"] = mk.astype(ml_dtypes.bfloat16)
            for k_ in ("q_norm", "k_norm", "sink"):
                m["l2_" + k_] = f(inputs["l2_" + k_]).reshape(1, -1)
        maps.append(m)
    return maps


_ROPE_CACHE = {}


def rope_table(head_dim, mirrored=False):
    if (head_dim, mirrored) in _ROPE_CACHE:
        return _ROPE_CACHE[(head_dim, mirrored)]
    rows = T_LAT // 64
    row = np.repeat(np.arange(rows), 64).astype(np.float32)
    col = np.tile(np.arange(64), rows).astype(np.float32)
    n_freq = head_dim // 4
    inv_freq = (np.float32(10000.0) ** (-(np.arange(n_freq, dtype=np.float32) / np.float32(n_freq)))).astype(np.float32)
    ang = np.concatenate([row[:, None] * inv_freq, col[:, None] * inv_freq], axis=-1).astype(np.float32)
    tab = np.zeros((T_LAT + T_CTX, 2, 2 * n_freq), np.float32)
    if mirrored:
        ang = ang[::-1]
    tab[:T_LAT, 0] = np.cos(ang)
    tab[:T_LAT, 1] = np.sin(ang)
    tab[T_LAT:, 0] = 1.0
    _ROPE_CACHE[(head_dim, mirrored)] = tab
    return tab


def kernel(**inputs):
    nc, S = build_program(4)
    maps = make_in_maps(inputs, 4)
    res = run_bass_kernel_spmd(nc, maps, core_ids=list(range(NCORES)))
    out = np.empty((4, T_LAT, D), np.float32)
    half = OWN * 128
    for c in range(NCORES):
        b, mir = c // 2, (c % 2 == 1)
        r = np.asarray(res.results[c]["out"], dtype=np.float32)
        if mir:
            out[b, half:] = r[::-1]
        else:
            out[b, :half] = r
    return out
```

```python
import numpy as np
import ml_dtypes
import concourse.bass as bass
import concourse.mybir as mybir
from concourse.bass_utils import run_bass_kernel_spmd

F32 = mybir.dt.float32
BF16 = mybir.dt.bfloat16
AF = mybir.ActivationFunctionType
ALU = mybir.AluOpType
AX = mybir.AxisListType

D = 2048
DI = 4096
T_LAT = 4096
T_CTX = 256
NT_LAT = T_LAT // 128
NT = (T_LAT + T_CTX) // 128
KC = D // 128
EPS = 1e-6
NCORES = 8
OWN = 16


class _Op:
    __slots__ = ("eng", "fn", "deps", "dma", "sig", "sem", "val", "lane")


class Sched:
    ENGS = ("pe", "act", "dve", "pool", "sp")
    NL = 8

    def __init__(self):
        self.ops = {e: [] for e in self.ENGS}
        self.last_w = {}
        self.rd_eng = {}
        self.rd_dma = {}
        self.lane_rr = {"sp": 0, "pool": 0}
        self.lane_last = {}
        self.last_on = {}

    def add(self, eng, fn, reads=(), writes=(), dma=False):
        op = _Op()
        op.eng, op.fn, op.dma, op.sig, op.sem, op.val, op.lane = eng, fn, dma, False, None, 0, None
        deps = set()
        for r in reads:
            w = self.last_w.get(r)
            if w is not None:
                deps.add(w)
        for r in writes:
            w = self.last_w.get(r)
            if w is not None:
                deps.add(w)
            for o in self.rd_eng.get(r, {}).values():
                deps.add(o)
            for o in self.rd_dma.get(r, ()):
                deps.add(o)
        for r in reads:
            if dma:
                self.rd_dma.setdefault(r, []).append(op)
            else:
                self.rd_eng.setdefault(r, {})[eng] = op
        for r in writes:
            self.last_w[r] = op
            self.rd_eng[r] = {}
            self.rd_dma[r] = []
        if dma:
            lane = (eng, self.lane_rr[eng])
            self.lane_rr[eng] = (self.lane_rr[eng] + 1) % self.NL
            op.lane = lane
            prev = self.lane_last.get(lane)
            if prev is not None:
                deps.add(prev)
            self.lane_last[lane] = op
        if eng == "pe":
            deps = {d for d in deps if d.dma or d.eng != "pe"}
        deps.discard(op)
        op.deps = deps
        self.ops[eng].append(op)
        if not dma:
            self.last_on[eng] = op
        return op

    def barrier(self):
        tails = [o for o in self.last_on.values()]
        for lane, o in self.lane_last.items():
            tails.append(o)
        for e in self.ENGS:
            op = self.add(e, None)
            op.deps = set(t for t in tails)
        self.last_w.clear(); self.rd_eng.clear(); self.rd_dma.clear()

    def emit(self, nc, stack):
        sems = {}
        for e in ("pe", "act", "dve", "pool"):
            sems[e] = stack.enter_context(nc.semaphore("sem_" + e))
        for q in ("sp", "pool"):
            for l in range(self.NL):
                sems[(q, l)] = stack.enter_context(nc.semaphore("lane_%s%d" % (q, l)))
        for e in self.ENGS:
            for op in self.ops[e]:
                for d in op.deps:
                    d.sig = True
        cnt = {k: 0 for k in sems}
        for e in self.ENGS:
            for op in self.ops[e]:
                if op.dma:
                    cnt[op.lane] += 16
                    op.sem, op.val = op.lane, cnt[op.lane]
                elif op.sig and op.fn is not None:
                    cnt[e] += 1
                    op.sem, op.val = e, cnt[e]
        final = dict(cnt)
        block = stack.enter_context(nc.Block())
        engmap = {"pe": block.tensor, "act": block.scalar, "dve": block.vector,
                  "pool": block.gpsimd, "sp": block.sync}
        nwaits = [0]

        def run(ename, e):
            waited = {}
            for op in self.ops[ename]:
                need = {}
                for d in op.deps:
                    if d.sem is None:
                        continue
                    if d.val > need.get(d.sem, 0):
                        need[d.sem] = d.val
                for s, v in need.items():
                    if v > waited.get(s, 0):
                        e.wait_ge(sems[s], v)
                        waited[s] = v
                        nwaits[0] += 1
                if op.fn is None:
                    continue
                ins = op.fn(e)
                if op.dma:
                    ins.then_inc(sems[op.sem], 16)
                elif op.sig:
                    ins.then_inc(sems[op.sem], 1)
            if ename in ("sp", "pool"):
                for l in range(self.NL):
                    if final[(ename, l)] > 0:
                        e.wait_ge(sems[(ename, l)], final[(ename, l)])

        for ename in self.ENGS:
            engmap[ename](lambda e, ename=ename: run(ename, e))
        self.nwaits = nwaits[0]


class Buf:
    def __init__(self, t, name):
        self.t = t
        self.name = name

    def __getitem__(self, idx):
        return self.t[idx]


def build_program(nlayers=4, debug_out=None):
    nc = bass.Bass("TRN2", target_bir_lowering=False)
    S = Sched()
    from contextlib import ExitStack
    top = ExitStack()

    def dram(name, shape, dt, kind="Internal"):
        return nc.dram_tensor(name, list(shape), dt, kind=kind).ap()

    x_in = dram("x", [T_LAT, D], F32, "ExternalInput")
    ctx_in = dram("ctx", [T_CTX, D], F32, "ExternalInput")
    cT_in = dram("cT", [128, 2, KC], F32, "ExternalInput")
    out_d = dram("out", [OWN * 128, D], F32, "ExternalOutput")
    ident_in = dram("ident", [128, 128], BF16, "ExternalInput")
    kinds = [0, 1, 2, 0]
    W = []
    for l in range(nlayers):
        kind = kinds[l]
        n_in = 16384 if kind != 2 else 10240
        w = dict(kind=kind, n_in=n_in)
        w["norm"] = dram(f"l{l}_norm", [1, D], F32, "ExternalInput")
        w["w_mod"] = dram(f"l{l}_w_mod", [D, 3 * D], F32, "ExternalInput")
        w["b_mod"] = dram(f"l{l}_b_mod", [1, 3 * D], F32, "ExternalInput")
        w["w_in"] = dram(f"l{l}_w_in", [D, n_in], F32, "ExternalInput")
        w["w_out"] = dram(f"l{l}_w_out", [DI, D], F32, "ExternalInput")
        if kind == 0:
            w["convp"] = dram(f"l{l}_convp", [128, 32, 4], F32, "ExternalInput")
        w["w_mod_b"] = dram(f"l{l}_w_mod_b", [12, 128, KC, 512], BF16)
        w["w_in_b"] = dram(f"l{l}_w_in_b", [n_in // 512, 128, KC, 512], BF16)
        w["w_out_b"] = dram(f"l{l}_w_out_b", [8, 128, KC, 512], BF16)
        w["mod_d"] = dram(f"l{l}_mod_d", [2, 3 * D], F32)
        W.append(w)

    xbuf = [dram("xbuf0", [NT * 128, D], F32), dram("xbuf1", [NT * 128, D], F32)]
    hT_d = dram("hT_d", [128, KC, NT * 128], BF16)
    UW = 1 + T_LAT + 2 + T_CTX + 1
    u_d = dram("u_d", [32, 128, UW], F32)

    NTOK = NT * 128
    dk = "ExternalOutput" if debug_out in ("full", "attn") else "Internal"
    if debug_out == "attn":
        dbg2 = dram("dbg2", [4, 128, 512], F32, "ExternalOutput")
    qT_d = dram("qT_d", [32, 128, NTOK], BF16, dk)
    kT_d = dram("kT_d", [32, 128, NTOK], BF16, dk)
    v_d = dram("v_d", [NTOK, DI], BF16, dk)
    sz_d = dram("sz_d", [NTOK, DI], F32, dk)
    gT_d = dram("gT_d", [32, 128, NTOK], BF16, dk)
    szT_d = dram("szT_d", [32, 128, NTOK], F32)
    if nlayers > 1:
        rope64 = dram("rope64", [NTOK, 2, 32], F32, "ExternalInput")
        lamv = dram("l1_lamv", [1, 4 * 64], F32, "ExternalInput")
        W[1]["q_norm"] = dram("l1_q_norm", [1, 64], F32, "ExternalInput")
        W[1]["k_norm"] = dram("l1_k_norm", [1, 64], F32, "ExternalInput")
        W[1]["sub_norm"] = dram("l1_sub_norm", [1, 128], F32, "ExternalInput")
    if nlayers > 2:
        rope128 = dram("rope128", [NTOK, 2, 64], F32, "ExternalInput")
        masks_in = dram("masks", [128, 2, 128], BF16, "ExternalInput")
        W[2]["q_norm"] = dram("l2_q_norm", [1, 128], F32, "ExternalInput")
        W[2]["k_norm"] = dram("l2_k_norm", [1, 128], F32, "ExternalInput")
        W[2]["sink"] = dram("l2_sink", [1, 32], F32, "ExternalInput")

    def ucol(tok):
        return 1 + tok if tok < T_LAT else 1 + tok + 2

    uid = [0]

    def sb(stack, name, shape, dt):
        uid[0] += 1
        name = "s%d_%s" % (uid[0], name)
        return Buf(stack.enter_context(nc.sbuf_tensor(name, list(shape), dt)), name)

    def ps(stack, name, shape, dt):
        uid[0] += 1
        name = "p%d_%s" % (uid[0], name)
        return Buf(stack.enter_context(nc.psum_tensor(name, list(shape), dt)), name)

    def dma(q, out_ap, in_ap, reads, writes, **kw):
        return S.add(q, lambda e: e.dma_start(out=out_ap, in_=in_ap, **kw), reads, writes, dma=True)

    def x_src(l, tile):
        if l == 0:
            if tile < NT_LAT:
                return x_in[tile * 128:(tile + 1) * 128, :], None
            return ctx_in[(tile - NT_LAT) * 128:(tile - NT_LAT + 1) * 128, :], None
        return xbuf[l % 2][tile * 128:(tile + 1) * 128, :], ("x", l, tile)

    def x_dst(l, tile):
        if l == nlayers - 1 and tile < OWN and debug_out is None:
            return out_d[tile * 128:(tile + 1) * 128, :], ("out", tile)
        return xbuf[(l + 1) % 2][tile * 128:(tile + 1) * 128, :], ("x", l + 1, tile)

    def cast_weights(l, parts=("in", "out")):
        w = W[l]
        if "in" in parts:
            for s_ in range(w["n_in"] // 512):
                src = w["w_in"][:, s_ * 512:(s_ + 1) * 512].rearrange("(k p) n -> p k n", p=128)
                dma("pool", w["w_in_b"][s_], src, [], [("win", l, s_)])
        if "out" in parts:
            for n in range(4):
                for kh in range(2):
                    src = w["w_out"][kh * 2048:(kh + 1) * 2048, n * 512:(n + 1) * 512].rearrange("(k p) n -> p k n", p=128)
                    dma("pool", w["w_out_b"][n * 2 + kh], src, [], [("wout", l, n * 2 + kh)])

    eps_t = sb(top, "eps_t", [128, 1], F32)
    S.add("dve", lambda e: e.memset(eps_t[:, :], EPS), [], [eps_t])
    zcol = sb(top, "zcol", [128, 2], F32)
    S.add("dve", lambda e: e.memset(zcol[:, :], 0.0), [], [zcol])
    for (c0, n) in ((0, 1), (T_LAT + 1, 2), (UW - 1, 1)):
        dma("sp", u_d[:, :, c0:c0 + n].rearrange("j p c -> p j c"),
            zcol[:, 0:n].unsqueeze(1).to_broadcast([128, 32, n]), [zcol], [("upad", c0)], allow_slow_non_contiguous=True)

    ident = sb(top, "ident", [128, 128], BF16)
    dma("sp", ident[:, :], ident_in[:, :], [], [ident])

    cast_weights(0)

    def modulation(l, stack):
        w = W[l]
        cT = sb(stack, "cT", [128, 2, KC], F32)
        sc = sb(stack, "sc", [128, KC, 2], F32)
        bm = sb(stack, "bm", [2, 3 * D], F32)
        gg = sb(stack, "gg", [2, D], F32)
        msb = sb(stack, "msb", [2, 3 * D], F32)
        wm = [sb(stack, "wm%d" % i, [128, KC, 512], F32) for i in range(2)]
        pm = [ps(stack, "pm%d" % i, [2, 512], F32) for i in range(2)]
        dma("sp", cT[:, :, :], cT_in[:, :, :], [], [cT])
        dma("sp", bm[:, :], w["b_mod"][0:1, :].partition_broadcast(2).rearrange("p o n -> p (o n)"), [], [bm])
        dma("sp", gg[:, :], w["norm"][0:1, :].partition_broadcast(2).rearrange("p o n -> p (o n)"), [], [gg])
        S.add("act", lambda e: e.activation(out=sc[:, :, :].rearrange("p k r -> p r k"), in_=cT[:, :, :], func=AF.Silu),
              [cT], [sc])
        for s in range(12):
            wb = wm[s % 2]
            pb = pm[s % 2]
            dma("sp", wb[:, :, :], w["w_mod"][:, s * 512:(s + 1) * 512].rearrange("(k p) n -> p k n", p=128), [], [wb])
            for k in range(KC):
                S.add("pe", lambda e, k=k, wb=wb, pb=pb: e.matmul(pb[:, :], sc[:, k, :], wb[:, k, :], start=(k == 0), stop=(k == KC - 1)),
                      [sc, wb], [pb])
            S.add("dve", lambda e, s=s, pb=pb: e.tensor_tensor(out=msb[:, s * 512:(s + 1) * 512], in0=pb[:, :], in1=bm[:, s * 512:(s + 1) * 512], op=ALU.add),
                  [pb, bm], [msb])
        S.add("dve", lambda e: e.scalar_tensor_tensor(out=msb[:, D:2 * D], in0=msb[:, D:2 * D], scalar=1.0, in1=gg[:, :], op0=ALU.add, op1=ALU.mult),
              [msb, gg], [msb])
        md = w["mod_d"]
        dma("sp", md[:, 0:D], msb[:, D:2 * D], [msb], [("mod", l, 0)])
        dma("sp", md[:, D:2 * D], msb[:, 0:D], [msb], [("mod", l, 1)])
        dma("sp", md[:, 2 * D:3 * D], msb[:, 2 * D:3 * D], [msb], [("mod", l, 2)])

    def load_mod(l, tile_buf, row, which):
        md = W[l]["mod_d"]
        src = md[row:row + 1, which * D:(which + 1) * D].partition_broadcast(128).rearrange("p o n -> p (o n)")
        dma("sp", tile_buf[:, :], src, [("mod", l, which)], [tile_buf])

    def phase_norm(l, stack, tiles):
        gs = sb(stack, "gs", [128, D], F32)
        sh = sb(stack, "sh", [128, D], F32)
        xt = [sb(stack, "xt%d" % i, [128, D], F32) for i in range(2)]
        hb = [sb(stack, "hb%d" % i, [128, D], BF16) for i in range(2)]
        junk = sb(stack, "junk", [128, D], BF16)
        ss = [sb(stack, "ss%d" % i, [128, 1], F32) for i in range(2)]
        rs = [sb(stack, "rs%d" % i, [128, 1], F32) for i in range(2)]
        hTs = [sb(stack, "hTs%d" % i, [128, KC, 128], BF16) for i in range(2)]
        pT = [ps(stack, "pT%d" % i, [128, KC, 128], BF16) for i in range(2)]
        cur_row = None
        for n, tile in enumerate(tiles):
            row = 0 if tile < NT_LAT else 1
            if row != cur_row:
                load_mod(l, gs, row, 0)
                load_mod(l, sh, row, 1)
                cur_row = row
            b = n % 2
            x_ap, x_res = x_src(l, tile)
            dma("sp", xt[b][:, :], x_ap, [x_res] if x_res else [], [xt[b]])
            S.add("act", lambda e, b=b: e.activation(out=junk[:, :], in_=xt[b][:, :], func=AF.Square, accum_out=ss[b][:, :]),
                  [xt[b]], [junk, ss[b]])
            S.add("act", lambda e, b=b: e.activation(out=rs[b][:, :], in_=ss[b][:, :], func=AF.Sqrt, bias=eps_t[:, :], scale=1.0 / D),
                  [ss[b], eps_t], [rs[b]])
            S.add("dve", lambda e, b=b: e.reciprocal(out=rs[b][:, :], in_=rs[b][:, :]), [rs[b]], [rs[b]])
            S.add("dve", lambda e, b=b: e.scalar_tensor_tensor(out=xt[b][:, :], in0=xt[b][:, :], scalar=rs[b][:, 0:1], in1=gs[:, :], op0=ALU.mult, op1=ALU.mult),
                  [xt[b], rs[b], gs], [xt[b]])
            S.add("pool", lambda e, b=b: e.tensor_tensor(out=hb[b][:, :], in0=xt[b][:, :], in1=sh[:, :], op=ALU.add),
                  [xt[b], sh], [hb[b]])
            for k in range(KC):
                S.add("pe", lambda e, b=b, k=k: e.transpose(out=pT[b][:, k, :], in_=hb[b][:, k * 128:(k + 1) * 128], identity=ident[:, :]),
                      [hb[b], ident], [pT[b]])
            S.add("act", lambda e, b=b: e.activation(out=hTs[b][:, :, :], in_=pT[b][:, :, :], func=AF.Copy),
                  [pT[b]], [hTs[b]])
            dma("pool", hT_d[:, :, tile * 128:(tile + 1) * 128], hTs[b][:, :, :], [hTs[b]], [("hT", tile)])

    def token_blocks(tiles_lat, tiles_ctx, tb_tiles):
        blocks = []
        for group in (tiles_lat, tiles_ctx):
            for i in range(0, len(group), tb_tiles):
                blocks.append(group[i:i + tb_tiles])
        return blocks

    def load_hT_block(hTb, blk):
        n = len(blk)
        src = hT_d[:, :, blk[0] * 128:(blk[0] + n) * 128]
        dst = hTb[:, :, 0:n * 128]
        dma("sp", dst, src, [("hT", t) for t in blk], [hTb])

    def outproj_block(l, blk, gated, gate, wslab, pY, xo, xn, eit):
        w = W[l]
        for n in range(4):
            for kh in range(2):
                wo = wslab(w["w_out_b"][n * 2 + kh], ("wout", l, n * 2 + kh))
                for ti, tile in enumerate(blk):
                    for k in range(KC):
                        S.add("pe", lambda e, ti=ti, wo=wo, k=k, kh=kh: e.matmul(
                            pY[ti][:, :], gated[:, kh * 16 + k, ti * 128:(ti + 1) * 128], wo[:, k, :],
                            start=(kh == 0 and k == 0), stop=(kh == 1 and k == KC - 1)),
                            [wo, gated], [pY[ti]])
            for ti, tile in enumerate(blk):
                xob, xnb = xo[eit % 2], xn[eit % 2]
                eit += 1
                x_ap, x_res = x_src(l, tile)
                dma("sp", xob[:, :], x_ap[:, n * 512:(n + 1) * 512], [x_res] if x_res else [], [xob])
                S.add("dve", lambda e, xnb=xnb, ti=ti, n=n: e.tensor_tensor(out=xnb[:, :], in0=pY[ti][:, :], in1=gate[:, n * 512:(n + 1) * 512], op=ALU.mult),
                      [pY[ti], gate], [xnb])
                S.add("pool", lambda e, xnb=xnb, xob=xob: e.tensor_tensor(out=xnb[:, :], in0=xnb[:, :], in1=xob[:, :], op=ALU.add),
                      [xnb, xob], [xnb])
                d_ap, d_res = x_dst(l, tile)
                dma("pool", d_ap[:, n * 512:(n + 1) * 512], xnb[:, :], [xnb], [(d_res, n)])
        return eit

    def make_wslab(stack, nbuf=4):
        wsl = [sb(stack, "wsl%d" % i, [128, KC, 512], BF16) for i in range(nbuf)]
        wrr = [0]

        def wslab(src_ap, res):
            b = wsl[wrr[0] % nbuf]
            wrr[0] += 1
            dma("sp", b[:, :, :], src_ap, [res], [b])
            return b
        return wslab

    def outproj_phase(l, stack, need_ctx, lat):
        ctxt = list(range(NT_LAT, NT)) if need_ctx else []
        blocks = token_blocks(lat, ctxt, 4)
        wslab = make_wslab(stack)
        gated = sb(stack, "gated", [128, 32, 512], BF16)
        gate = sb(stack, "gate", [128, D], F32)
        xo = [sb(stack, "xo%d" % i, [128, 512], F32) for i in range(2)]
        xn = [sb(stack, "xn%d" % i, [128, 512], F32) for i in range(2)]
        pY = [ps(stack, "pY%d" % i, [128, 512], F32) for i in range(4)]
        cur_row = None
        eit = 0
        for blk in blocks:
            ntok = len(blk) * 128
            row = 0 if blk[0] < NT_LAT else 1
            if row != cur_row:
                load_mod(l, gate, row, 2)
                cur_row = row
            t0 = blk[0] * 128
            dma("sp", gated[:, :, 0:ntok], gT_d[:, :, t0:t0 + ntok].rearrange("j p t -> p j t"),
                [("gT", j, t) for j in range(32) for t in blk], [gated])
            eit = outproj_block(l, blk, gated, gate, wslab, pY, xo, xn, eit)

    def conv_layer(l, stack, need_ctx, latA, latB):
        w = W[l]
        ctxt = list(range(NT_LAT, NT)) if need_ctx else []
        blocksA = token_blocks(latA, ctxt, 4)
        blocksB = token_blocks(latB, ctxt, 4)
        blocks = blocksA
        wsl = [sb(stack, "wsl%d" % i, [128, KC, 512], BF16) for i in range(4)]
        wrr = [0]

        def wslab(src_ap, res):
            b = wsl[wrr[0] % 4]
            wrr[0] += 1
            dma("sp", b[:, :, :], src_ap, [res], [b])
            return b

        hTb = sb(stack, "hTb", [128, KC, 512], BF16)
        pA = [ps(stack, "pA%d" % i, [128, 512], F32) for i in range(4)]
        cg_sb = [sb(stack, "cg_sb%d" % i, [128, 512], F32) for i in range(2)]
        u_sb = [sb(stack, "u_sb%d" % i, [128, 512], F32) for i in range(2)]
        it = 0
        for blk in blocks:
            ntok = len(blk) * 128
            load_hT_block(hTb, blk)
            for g in range(8):
                wcg = wslab(w["w_in_b"][8 + g], ("win", l, 8 + g))
                wxt = wslab(w["w_in_b"][16 + g], ("win", l, 16 + g))
                for jj in range(4):
                    j = g * 4 + jj
                    p0 = pA[(it * 2) % 4]
                    p1 = pA[(it * 2 + 1) % 4]
                    cs = cg_sb[it % 2]
                    us = u_sb[it % 2]
                    it += 1
                    for (pp, ww) in ((p0, wcg), (p1, wxt)):
                        for k in range(KC):
                            S.add("pe", lambda e, pp=pp, ww=ww, k=k, jj=jj, ntok=ntok: e.matmul(
                                pp[:, 0:ntok], ww[:, k, jj * 128:(jj + 1) * 128], hTb[:, k, 0:ntok], start=(k == 0), stop=(k == KC - 1)),
                                [ww, hTb], [pp])
                    S.add("act", lambda e, cs=cs, p0=p0, ntok=ntok: e.activation(out=cs[:, 0:ntok], in_=p0[:, 0:ntok], func=AF.Copy),
                          [p0], [cs])
                    S.add("dve", lambda e, us=us, cs=cs, p1=p1, ntok=ntok: e.tensor_tensor(out=us[:, 0:ntok], in0=cs[:, 0:ntok], in1=p1[:, 0:ntok], op=ALU.mult),
                          [cs, p1], [us])
                    c0 = ucol(blk[0] * 128)
                    dma("pool", u_d[j, :, c0:c0 + ntok], us[:, 0:ntok], [us], [("u", j, blk[0])])
        gated = sb(stack, "gated", [128, 32, 512], BF16)
        uw = [sb(stack, "uw%d" % i, [128, 514], F32) for i in range(2)]
        yv = [sb(stack, "yv%d" % i, [128, 512], F32) for i in range(2)]
        sz = [sb(stack, "sz%d" % i, [128, 512], F32) for i in range(2)]
        cvp = sb(stack, "cvp", [128, 32, 4], F32)
        gate = sb(stack, "gate", [128, D], F32)
        xo = [sb(stack, "xo%d" % i, [128, 512], F32) for i in range(2)]
        xn = [sb(stack, "xn%d" % i, [128, 512], F32) for i in range(2)]
        pY = [ps(stack, "pY%d" % i, [128, 512], F32) for i in range(4)]
        dma("sp", cvp[:, :, :], w["convp"][:, :, :], [], [cvp])
        cur_row = None
        it = 0
        eit = 0
        for blk in blocksB:
            ntok = len(blk) * 128
            row = 0 if blk[0] < NT_LAT else 1
            if row != cur_row:
                load_mod(l, gate, row, 2)
                cur_row = row
            load_hT_block(hTb, blk)
            c0 = ucol(blk[0] * 128)
            for g in range(8):
                wbg = wslab(w["w_in_b"][g], ("win", l, g))
                wz = wslab(w["w_in_b"][24 + g], ("win", l, 24 + g))
                for jj in range(4):
                    j = g * 4 + jj
                    p0 = pA[(it * 2) % 4]
                    p1 = pA[(it * 2 + 1) % 4]
                    uwb, yb, szb = uw[it % 2], yv[it % 2], sz[it % 2]
                    it += 1
                    ures = [("u", j, bb[0]) for bb in blocksA]
                    dma("sp", uwb[:, 0:ntok + 2], u_d[j, :, c0 - 1:c0 + ntok + 1], ures + [("upad", 0), ("upad", T_LAT + 1), ("upad", UW - 1)], [uwb])
                    for (pp, ww) in ((p0, wbg), (p1, wz)):
                        for k in range(KC):
                            S.add("pe", lambda e, pp=pp, ww=ww, k=k, jj=jj, ntok=ntok: e.matmul(
                                pp[:, 0:ntok], ww[:, k, jj * 128:(jj + 1) * 128], hTb[:, k, 0:ntok], start=(k == 0), stop=(k == KC - 1)),
                                [ww, hTb], [pp])
                    S.add("pool", lambda e, yb=yb, uwb=uwb, j=j, ntok=ntok: e.tensor_scalar(
                        out=yb[:, 0:ntok], in0=uwb[:, 0:ntok], scalar1=cvp[:, j, 0:1], scalar2=cvp[:, j, 3:4], op0=ALU.mult, op1=ALU.add),
                        [uwb, cvp], [yb])
                    S.add("dve", lambda e, yb=yb, uwb=uwb, j=j, ntok=ntok: e.scalar_tensor_tensor(
                        out=yb[:, 0:ntok], in0=uwb[:, 1:ntok + 1], scalar=cvp[:, j, 1:2], in1=yb[:, 0:ntok], op0=ALU.mult, op1=ALU.add),
                        [uwb, cvp, yb], [yb])
                    S.add("dve", lambda e, yb=yb, uwb=uwb, j=j, ntok=ntok: e.scalar_tensor_tensor(
                        out=yb[:, 0:ntok], in0=uwb[:, 2:ntok + 2], scalar=cvp[:, j, 2:3], in1=yb[:, 0:ntok], op0=ALU.mult, op1=ALU.add),
                        [uwb, cvp, yb], [yb])
                    S.add("act", lambda e, szb=szb, p1=p1, ntok=ntok: e.activation(out=szb[:, 0:ntok], in_=p1[:, 0:ntok], func=AF.Silu),
                          [p1], [szb])
                    S.add("dve", lambda e, yb=yb, p0=p0, ntok=ntok: e.tensor_tensor(out=yb[:, 0:ntok], in0=yb[:, 0:ntok], in1=p0[:, 0:ntok], op=ALU.mult),
                          [yb, p0], [yb])
                    S.add("dve", lambda e, yb=yb, szb=szb, j=j, ntok=ntok: e.tensor_tensor(out=gated[:, j, 0:ntok], in0=yb[:, 0:ntok], in1=szb[:, 0:ntok], op=ALU.mult),
                          [yb, szb], [gated])
            eit = outproj_block(l, blk, gated, gate, wslab, pY, xo, xn, eit)

    def bload(tile_buf, src_row_ap, reads=()):
        dma("sp", tile_buf[:, :], src_row_ap.partition_broadcast(128).rearrange("p o n -> p (o n)"), list(reads), [tile_buf])

    def proj_phase(l, stack, hd, roles, qscale, rope_d, groups):
        w = W[l]
        ng = 512 // hd
        nf = hd // 4
        blocks = []
        for (gt, groles) in groups:
            for i in range(0, len(gt), 4):
                blocks.append((gt[i:i + 4], groles))
        wslab = make_wslab(stack)
        hTb = sb(stack, "hTb", [128, KC, 512], BF16)
        nw = {"q": sb(stack, "nwq", [128, hd], F32), "k": sb(stack, "nwk", [128, hd], F32)}
        bload(nw["q"], w["q_norm"][0:1, :])
        bload(nw["k"], w["k_norm"][0:1, :])
        S.add("dve", lambda e: e.tensor_scalar(out=nw["q"][:, :], in0=nw["q"][:, :], scalar1=float(qscale), scalar2=None, op0=ALU.mult),
              [nw["q"]], [nw["q"]])
        pp = [ps(stack, "pp%d" % i, [128, 512], F32) for i in range(4)]
        pT = [ps(stack, "pT%d" % i, [128, 4, 128], BF16) for i in range(2)]
        csx = [sb(stack, "csx%d" % i, [128, ng, 2 * nf], F32) for i in range(4)]
        snx = [sb(stack, "snx%d" % i, [128, ng, 2 * nf], F32) for i in range(4)]
        NB = 4
        qraw = [sb(stack, "qraw%d" % i, [128, 512], F32) for i in range(NB)]
        sq = [sb(stack, "sq%d" % i, [128, 512], F32) for i in range(NB)]
        ss8 = [sb(stack, "ss8%d" % i, [128, ng], F32) for i in range(NB)]
        rs8 = [sb(stack, "rs8%d" % i, [128, ng], F32) for i in range(NB)]
        qn = [sb(stack, "qn%d" % i, [128, 512], F32) for i in range(NB)]
        tA = [sb(stack, "tA%d" % i, [128, 256], F32) for i in range(NB)]
        tB = [sb(stack, "tB%d" % i, [128, 256], F32) for i in range(NB)]
        tC = [sb(stack, "tC%d" % i, [128, 256], F32) for i in range(NB)]
        tD = [sb(stack, "tD%d" % i, [128, 256], F32) for i in range(NB)]
        qr = [sb(stack, "qr%d" % i, [128, 512], BF16) for i in range(NB)]
        qTs = [sb(stack, "qTs%d" % i, [128, 4, 128], BF16) for i in range(NB)]
        vb = [sb(stack, "vb%d" % i, [128, 512], BF16) for i in range(NB)]
        szb = [sb(stack, "szb%d" % i, [128, 512], F32) for i in range(NB)]
        it = 0
        trc = [0]
        pending = []
        for (blk, broles) in blocks:
            ntok = len(blk) * 128
            load_hT_block(hTb, blk)
            for ti, tile in enumerate(blk):
                t0 = tile * 128
                dma("sp", csx[ti][:, :, :], rope_d[t0:t0 + 128, 0, :].unsqueeze(1).to_broadcast([128, ng, 2 * nf]), [], [csx[ti]])
                dma("sp", snx[ti][:, :, :], rope_d[t0:t0 + 128, 1, :].unsqueeze(1).to_broadcast([128, ng, 2 * nf]), [], [snx[ti]])
            for (s_idx, role, base) in roles:
                if role not in broles:
                    continue
                wsb = wslab(w["w_in_b"][s_idx], ("win", l, s_idx))
                if role == "zT":
                    tb0 = blk[0] * 128
                    for c in range(4):
                        p = pp[it % 4]
                        b = it % NB
                        it += 1
                        for k in range(KC):
                            S.add("pe", lambda e, p=p, k=k, c=c, wsb=wsb, ntok=ntok: e.matmul(
                                p[:, 0:ntok], wsb[:, k, c * 128:(c + 1) * 128], hTb[:, k, 0:ntok], start=(k == 0), stop=(k == KC - 1)),
                                [hTb, wsb], [p])
                        S.add("act", lambda e, p=p, b=b, ntok=ntok: e.activation(out=szb[b][:, 0:ntok], in_=p[:, 0:ntok], func=AF.Silu), [p], [szb[b]])
                        dma("pool", szT_d[base * 4 + c, :, tb0:tb0 + ntok], szb[b][:, 0:ntok], [szb[b]], [("szT", base * 4 + c, t) for t in blk])
                    continue
                for ti, tile in enumerate(blk):
                    t0 = tile * 128
                    p = pp[it % 4]
                    b = it % NB
                    it += 1
                    for k in range(KC):
                        S.add("pe", lambda e, p=p, k=k, ti=ti, wsb=wsb: e.matmul(
                            p[:, :], hTb[:, k, ti * 128:(ti + 1) * 128], wsb[:, k, :], start=(k == 0), stop=(k == KC - 1)),
                            [hTb, wsb], [p])
                    if role == "v":
                        S.add("act", lambda e, p=p, b=b: e.activation(out=vb[b][:, :], in_=p[:, :], func=AF.Copy), [p], [vb[b]])
                        dma("pool", v_d[t0:t0 + 128, base * 512:(base + 1) * 512], vb[b][:, :], [vb[b]], [("v", base, tile)])
                    elif role == "z":
                        S.add("act", lambda e, p=p, b=b: e.activation(out=szb[b][:, :], in_=p[:, :], func=AF.Silu), [p], [szb[b]])
                        dma("pool", sz_d[t0:t0 + 128, base * 512:(base + 1) * 512], szb[b][:, :], [szb[b]], [("sz", base, tile)])
                    else:
                        S.add("act", lambda e, p=p, b=b: e.activation(out=sq[b][:, :], in_=p[:, :], func=AF.Square), [p], [sq[b]])
                        S.add("act", lambda e, p=p, b=b: e.activation(out=qraw[b][:, :], in_=p[:, :], func=AF.Copy), [p], [qraw[b]])
                        p3 = qraw[b][:, :].rearrange("p (g d) -> p g d", d=hd)
                        S.add("dve", lambda e, b=b: e.tensor_reduce(out=ss8[b][:, :], in_=sq[b][:, :].rearrange("p (g d) -> p g d", d=hd), axis=AX.X, op=ALU.add),
                              [sq[b]], [ss8[b]])
                        S.add("act", lambda e, b=b: e.activation(out=rs8[b][:, :], in_=ss8[b][:, :], func=AF.Sqrt, bias=eps_t[:, :], scale=1.0 / hd),
                              [ss8[b], eps_t], [rs8[b]])
                        S.add("dve", lambda e, b=b: e.reciprocal(out=rs8[b][:, :], in_=rs8[b][:, :]), [rs8[b]], [rs8[b]])
                        S.add("dve", lambda e, b=b, p3=p3: e.tensor_tensor(out=qn[b][:, :].rearrange("p (g d) -> p g d", d=hd), in0=p3,
                                                                            in1=rs8[b][:, :].unsqueeze(2).to_broadcast([128, ng, hd]), op=ALU.mult),
                              [qraw[b], rs8[b]], [qn[b]])
                        nwt = nw[role]
                        S.add("pool", lambda e, b=b, nwt=nwt: e.tensor_tensor(out=qn[b][:, :].rearrange("p (g d) -> p g d", d=hd),
                                                                               in0=qn[b][:, :].rearrange("p (g d) -> p g d", d=hd),
                                                                               in1=nwt[:, :].unsqueeze(1).to_broadcast([128, ng, hd]), op=ALU.mult),
                              [qn[b], nwt], [qn[b]])
                        qv = qn[b][:, :].rearrange("p (ga h f) -> p ga h f", h=2, f=nf)
                        ov = qr[b][:, :].rearrange("p (ga h f) -> p ga h f", h=2, f=nf)
                        t1, t2 = qv[:, :, 0, :], qv[:, :, 1, :]
                        cs = csx[ti][:, :, :].rearrange("p g (a f) -> p (g a) f", f=nf)
                        sn = snx[ti][:, :, :].rearrange("p g (a f) -> p (g a) f", f=nf)
                        v3 = lambda tb: tb[:, :].rearrange("p (ga f) -> p ga f", f=nf)
                        S.add("pool", lambda e, b=b, t1=t1, cs=cs: e.tensor_tensor(out=v3(tA[b]), in0=t1, in1=cs, op=ALU.mult), [qn[b], csx[ti]], [tA[b]])
                        S.add("pool", lambda e, b=b, t2=t2, sn=sn: e.tensor_tensor(out=v3(tB[b]), in0=t2, in1=sn, op=ALU.mult), [qn[b], snx[ti]], [tB[b]])
                        S.add("dve", lambda e, b=b, ov=ov: e.tensor_tensor(out=ov[:, :, 0, :], in0=v3(tA[b]), in1=v3(tB[b]), op=ALU.subtract), [tA[b], tB[b]], [qr[b]])
                        S.add("dve", lambda e, b=b, t1=t1, sn=sn: e.tensor_tensor(out=v3(tC[b]), in0=t1, in1=sn, op=ALU.mult), [qn[b], snx[ti]], [tC[b]])
                        S.add("pool", lambda e, b=b, t2=t2, cs=cs: e.tensor_tensor(out=v3(tD[b]), in0=t2, in1=cs, op=ALU.mult), [qn[b], csx[ti]], [tD[b]])
                        S.add("dve", lambda e, b=b, ov=ov: e.tensor_tensor(out=ov[:, :, 1, :], in0=v3(tC[b]), in1=v3(tD[b]), op=ALU.add), [tC[b], tD[b]], [qr[b]])
                        def fin(b=b, role=role, base=base, tile=tile, t0=t0):
                            pTb = pT[trc[0] % 2]
                            trc[0] += 1
                            for c in range(4):
                                S.add("pe", lambda e, b=b, c=c, pTb=pTb: e.transpose(out=pTb[:, c, :], in_=qr[b][:, c * 128:(c + 1) * 128], identity=ident[:, :]),
                                      [qr[b], ident], [pTb])
                            S.add("act", lambda e, b=b, pTb=pTb: e.activation(out=qTs[b][:, :, :], in_=pTb[:, :, :], func=AF.Copy), [pTb], [qTs[b]])
                            dst_t = qT_d if role == "q" else kT_d
                            dma("pool", dst_t[base * 4:(base + 1) * 4, :, t0:t0 + 128].rearrange("h p t -> p h t"), qTs[b][:, :, :], [qTs[b]],
                                [(role + "T", base * 4 + c, tile) for c in range(4)])
                        pending.append(fin)
                        while len(pending) > 2:
                            pending.pop(0)()
        while pending:
            pending.pop(0)()

    def load_head_kv(kTb, vab, kidx, vcol, ranges):
        for tiles in ranges:
            load_head_kv1(kTb, vab, kidx, vcol, tiles)

    def load_head_kv1(kTb, vab, kidx, vcol, tiles):
        t0, t1 = tiles[0] * 128, (tiles[-1] + 1) * 128
        dma("sp", kTb[:, t0:t1], kT_d[kidx, :, t0:t1], [("kT", kidx, t) for t in tiles], [kTb])
        dma("sp", vab[:, tiles[0]:tiles[-1] + 1, 0:128], v_d[t0:t1, vcol * 128:(vcol + 1) * 128].rearrange("(t p) e -> p t e", p=128),
            [("v", vcol // 4, t) for t in tiles], [vab])

    def attn_diff(l, stack, need_ctx, qlat):
        import math
        w = W[l]
        lam_init = 0.8 - 0.6 * math.exp(-0.3 * l)
        lqk = sb(stack, "lqk", [128, 4 * 64], F32)
        bload(lqk, lamv[0:1, :])
        prod = sb(stack, "prod", [128, 2, 64], F32)
        s2 = sb(stack, "s2", [128, 2], F32)
        e2 = sb(stack, "e2", [128, 2], F32)
        nlam = sb(stack, "nlam", [128, 1], F32)
        S.add("dve", lambda e: e.tensor_tensor(out=prod[:, :, :], in0=lqk[:, 0:128].rearrange("p (a d) -> p a d", d=64),
                                                in1=lqk[:, 128:256].rearrange("p (a d) -> p a d", d=64), op=ALU.mult), [lqk], [prod])
        S.add("dve", lambda e: e.tensor_reduce(out=s2[:, :], in_=prod[:, :, :], axis=AX.X, op=ALU.add), [prod], [s2])
        S.add("act", lambda e: e.activation(out=e2[:, :], in_=s2[:, :], func=AF.Exp), [s2], [e2])
        S.add("dve", lambda e: e.tensor_tensor(out=nlam[:, :], in0=e2[:, 1:2], in1=e2[:, 0:1], op=ALU.subtract), [e2], [nlam])
        S.add("dve", lambda e: e.tensor_scalar(out=nlam[:, :], in0=nlam[:, :], scalar1=float(-lam_init), scalar2=None, op0=ALU.add), [nlam], [nlam])
        snwc = sb(stack, "snwc", [128, 1], F32)
        dma("sp", snwc[:, :], w["sub_norm"][0:1, :].rearrange("o n -> n o"), [], [snwc], allow_slow_non_contiguous=True)
        S.add("dve", lambda e: e.tensor_scalar(out=snwc[:, :], in0=snwc[:, :], scalar1=float(1.0 - lam_init), scalar2=None, op0=ALU.mult), [snwc], [snwc])
        ones_f = sb(stack, "ones_f", [128, 128], F32)
        S.add("dve", lambda e: e.memset(ones_f[:, :], 1.0), [], [ones_f])

        kTh = [sb(stack, "kTh%d" % i, [128, NTOK], BF16) for i in range(2)]
        vt = [sb(stack, "vt%d" % i, [128, NT, 128], BF16) for i in range(2)]
        qTb = [sb(stack, "qTb%d" % i, [128, 512], BF16) for i in range(2)]
        E2 = [sb(stack, "E2_%d" % i, [128, 2, 512], BF16) for i in range(6)]
        Eacc = [sb(stack, "Eacc%d" % i, [128, 512], F32) for i in range(2)]
        ones_b = sb(stack, "ones_b", [128, 128], BF16)
        S.add("dve", lambda e: e.memset(ones_b[:, :], 1.0), [], [ones_b])
        psZ1 = ps(stack, "psZ1", [128, 512], F32)
        psS2 = [ps(stack, "psS2_%d" % i, [128, 2, 512], F32) for i in range(2)]
        psOT = [ps(stack, "psOT%d" % m, [128, 512], F32) for m in range(2)]
        psB = ps(stack, "psB", [128, 512], F32)
        OTs = [sb(stack, "OTs%d" % m, [128, 512], F32) for m in range(2)]
        R = [sb(stack, "R%d" % m, [128, 512], F32) for m in range(2)]
        ta = sb(stack, "ta", [128, 512], F32)
        tb = sb(stack, "tb", [128, 512], F32)
        oT = sb(stack, "oT", [128, 512], F32)
        sq = sb(stack, "sq", [128, 512], F32)
        rstd = sb(stack, "rstd", [128, 512], F32)
        szT = [sb(stack, "szT%d" % i, [128, 512], F32) for i in range(2)]
        gT = [sb(stack, "gT%d" % i, [128, 512], BF16) for i in range(2)]
        qblocks = token_blocks(qlat, list(range(NT_LAT, NT)) if need_ctx else [], 4)
        rr = 0
        gi = 0
        for h in range(32):
            kTb, vab = kTh[h % 2], vt[h % 2]
            load_head_kv(kTb, vab, h, h, [list(range(NT))])
            for bi, blk in enumerate(qblocks):
                nq = len(blk) * 128
                q0 = blk[0] * 128
                is_ctx = blk[0] >= NT_LAT
                ktiles = list(range(NT_LAT, NT)) if is_ctx else list(range(NT))
                qb = qTb[gi % 2]
                EA = Eacc[gi % 2]
                szb, gtb = szT[gi % 2], gT[gi % 2]
                gi += 1
                dma("sp", qb[:, 0:nq], qT_d[h, :, q0:q0 + nq], [("qT", h, t) for t in blk], [qb])
                dma("sp", szb[:, 0:nq], szT_d[h, :, q0:q0 + nq], [("szT", h, t) for t in blk], [szb])
                npair = len(ktiles)
                bufs = [(psS2[(rr + p) % 2], E2[(rr + p) % 6]) for p in range(npair)]
                rr += npair

                def score(p):
                    kt = ktiles[p]
                    pS = bufs[p][0]
                    for m in range(2):
                        S.add("pe", lambda e, pS=pS, kt=kt, m=m, kTb=kTb, qb=qb, nq=nq: e.matmul(
                            pS[:, m, 0:nq], kTb[m * 64:(m + 1) * 64, kt * 128:(kt + 1) * 128], qb[m * 64:(m + 1) * 64, 0:nq], start=True, stop=True),
                            [kTb, qb], [pS])
                score(0)
                for p in range(npair):
                    kt = ktiles[p]
                    pS, Eb = bufs[p]
                    if p + 1 < npair:
                        score(p + 1)
                    S.add("act", lambda e, pS=pS, Eb=Eb, nq=nq: e.activation(out=Eb[:, :, 0:nq], in_=pS[:, :, 0:nq], func=AF.Exp), [pS], [Eb])
                    for m in range(2):
                        S.add("pe", lambda e, Eb=Eb, kt=kt, m=m, vab=vab, nq=nq, st=(p == 0), sp=(p == npair - 1): e.matmul(
                            psOT[m][:, 0:nq], vab[:, kt, :], Eb[:, m, 0:nq], start=st, stop=sp), [Eb, vab], [psOT[m]])
                    S.add("pe", lambda e, Eb=Eb, nq=nq, st=(p == 0), sp=(p == npair - 1): e.matmul(
                        psZ1[:, 0:nq], ones_b[:, :], Eb[:, 1, 0:nq], start=st, stop=sp), [Eb, ones_b], [psZ1])
                    if p == 0:
                        S.add("dve", lambda e, Eb=Eb, EA=EA, nq=nq: e.tensor_copy(out=EA[:, 0:nq], in_=Eb[:, 0, 0:nq]), [Eb], [EA])
                    else:
                        S.add("dve", lambda e, Eb=Eb, EA=EA, nq=nq: e.tensor_tensor(out=EA[:, 0:nq], in0=EA[:, 0:nq], in1=Eb[:, 0, 0:nq], op=ALU.add),
                              [Eb, EA], [EA])
                for m in range(2):
                    S.add("dve", lambda e, m=m, nq=nq: e.tensor_copy(out=OTs[m][:, 0:nq], in_=psOT[m][:, 0:nq]), [psOT[m]], [OTs[m]])
                S.add("dve", lambda e, nq=nq: e.tensor_copy(out=R[1][:, 0:nq], in_=psZ1[:, 0:nq]), [psZ1], [R[1]])
                S.add("dve", lambda e, nq=nq: e.reciprocal(out=R[1][:, 0:nq], in_=R[1][:, 0:nq]), [R[1]], [R[1]])
                S.add("pe", lambda e, EA=EA, nq=nq: e.matmul(psB[:, 0:nq], ones_f[:, :], EA[:, 0:nq], start=True, stop=True), [ones_f, EA], [psB])
                S.add("dve", lambda e, nq=nq: e.reciprocal(out=R[0][:, 0:nq], in_=psB[:, 0:nq]), [psB], [R[0]])
                S.add("dve", lambda e, nq=nq: e.tensor_tensor(out=ta[:, 0:nq], in0=OTs[0][:, 0:nq], in1=R[0][:, 0:nq], op=ALU.mult), [OTs[0], R[0]], [ta])
                S.add("dve", lambda e, nq=nq: e.tensor_tensor(out=tb[:, 0:nq], in0=OTs[1][:, 0:nq], in1=R[1][:, 0:nq], op=ALU.mult), [OTs[1], R[1]], [tb])
                S.add("dve", lambda e, nq=nq: e.scalar_tensor_tensor(out=oT[:, 0:nq], in0=tb[:, 0:nq], scalar=nlam[:, 0:1], in1=ta[:, 0:nq], op0=ALU.mult, op1=ALU.add),
                      [ta, tb, nlam], [oT])
                S.add("dve", lambda e, nq=nq: e.tensor_tensor(out=sq[:, 0:nq], in0=oT[:, 0:nq], in1=oT[:, 0:nq], op=ALU.mult), [oT], [sq])
                S.add("pe", lambda e, nq=nq: e.matmul(psB[:, 0:nq], ones_f[:, :], sq[:, 0:nq], start=True, stop=True), [ones_f, sq], [psB])
                S.add("act", lambda e, nq=nq: e.activation(out=rstd[:, 0:nq], in_=psB[:, 0:nq], func=AF.Ln, bias=eps_t[:, :], scale=1.0 / 128), [psB, eps_t], [rstd])
                S.add("act", lambda e, nq=nq: e.activation(out=rstd[:, 0:nq], in_=rstd[:, 0:nq], func=AF.Exp, scale=-0.5), [rstd], [rstd])
                S.add("dve", lambda e, nq=nq: e.scalar_tensor_tensor(out=oT[:, 0:nq], in0=oT[:, 0:nq], scalar=snwc[:, 0:1], in1=rstd[:, 0:nq], op0=ALU.mult, op1=ALU.mult),
                      [oT, snwc, rstd], [oT])
                S.add("dve", lambda e, nq=nq, szb=szb, gtb=gtb: e.tensor_tensor(out=gtb[:, 0:nq], in0=oT[:, 0:nq], in1=szb[:, 0:nq], op=ALU.mult), [oT, szb], [gtb])
                dma("sp", gT_d[h, :, q0:q0 + nq], gtb[:, 0:nq], [gtb], [("gT", h, t) for t in blk])

    def attn_win(l, stack, qtiles, klat):
        w = W[l]
        sinkb = sb(stack, "sinkb", [128, 32], F32)
        bload(sinkb, w["sink"][0:1, :])
        S.add("act", lambda e: e.activation(out=sinkb[:, :], in_=sinkb[:, :], func=AF.Exp), [sinkb], [sinkb])
        masks = sb(stack, "masks", [128, 2, 128], BF16)
        dma("sp", masks[:, :, :], masks_in[:, :, :], [], [masks])
        kTh = [sb(stack, "kTh%d" % i, [128, NTOK], BF16) for i in range(2)]
        vaug = [sb(stack, "vaug%d" % i, [128, NT, 129], BF16) for i in range(2)]
        for i in range(2):
            S.add("dve", lambda e, i=i: e.memset(vaug[i][:, :, 128:129], 1.0), [], [vaug[i]])
        qTb = [sb(stack, "qTb%d" % i, [128, 4, 128], BF16) for i in range(2)]
        E = [sb(stack, "E%d" % i, [128, 512], BF16) for i in range(4)]
        psS = [ps(stack, "psS%d" % i, [128, 512], F32) for i in range(3)]
        psO = [ps(stack, "psO%d" % i, [128, 512], F32) for i in range(4)]
        pTg = ps(stack, "pTg", [128, 4, 128], BF16)
        NB = 8
        pend = []
        zr = [sb(stack, "zr%d" % i, [128, 1], F32) for i in range(NB)]
        o_t = [sb(stack, "o_t%d" % i, [128, 128], F32) for i in range(NB)]
        szt = [sb(stack, "szt%d" % i, [128, 128], F32) for i in range(NB)]
        gtm = [sb(stack, "gtm%d" % i, [128, 128], BF16) for i in range(NB)]
        gTs = [sb(stack, "gTs%d" % i, [128, 4, 128], BF16) for i in range(2)]
        rr = 0
        fi = 0
        gi = 0
        for n in range(8):
            kTb, vab = kTh[n % 2], vaug[n % 2]
            load_head_kv(kTb, vab, n, n, [klat, list(range(NT_LAT, NT))])
            for i in qtiles:
                qb = qTb[gi % 2]
                dma("sp", qb[:, :, :], qT_d[n * 4:(n + 1) * 4, :, i * 128:(i + 1) * 128].rearrange("h p t -> p h t"),
                    [("qT", n * 4 + g, i) for g in range(4)], [qb])
                keys = []
                if i > 0:
                    keys.append((i - 1, 0))
                keys.append((i, None))
                if i + 1 <= klat[-1]:
                    keys.append((i + 1, 1))
                keys += [(t, None) for t in range(NT_LAT, NT)]
                pO = [psO[(gi % 2) * 2], psO[(gi % 2) * 2 + 1]]
                bufs = [(psS[(rr + si) % 3], E[(rr + si) % 4]) for si in range(len(keys))]
                rr += len(keys)

                def score(si):
                    kt = keys[si][0]
                    pS = bufs[si][0]
                    S.add("pe", lambda e, pS=pS, kt=kt, kTb=kTb, qb=qb: e.matmul(pS[:, :], kTb[:, kt * 128:(kt + 1) * 128], qb[:, :, :].rearrange("p g t -> p (g t)"),
                                                                  start=True, stop=True), [kTb, qb], [pS])
                score(0)
                for si, (kt, mk) in enumerate(keys):
                    pS, Eb = bufs[si]
                    if si + 1 < len(keys):
                        score(si + 1)
                    S.add("act", lambda e, pS=pS, Eb=Eb: e.activation(out=Eb[:, :], in_=pS[:, :], func=AF.Exp), [pS], [Eb])
                    if mk is not None:
                        S.add("pool", lambda e, Eb=Eb, mk=mk: e.tensor_tensor(out=Eb[:, :].rearrange("p (g t) -> p g t", t=128),
                                                                               in0=Eb[:, :].rearrange("p (g t) -> p g t", t=128),
                                                                               in1=masks[:, mk, :].unsqueeze(1).to_broadcast([128, 4, 128]), op=ALU.mult),
                              [Eb, masks], [Eb])
                    for g in range(4):
                        S.add("pe", lambda e, Eb=Eb, g=g, kt=kt, vab=vab, pOg=pO[g // 2], st=(si == 0 and g % 2 == 0), sp=(si == len(keys) - 1): e.matmul(
                            pOg[:, (g % 2) * 256:(g % 2) * 256 + 129], Eb[:, g * 128:(g + 1) * 128], vab[:, kt, :],
                            start=st, stop=sp, skip_group_check=True), [Eb, vab], [pO[g // 2]])
                gts = gTs[gi % 2]
                gi += 1
                trs = []
                for g in range(4):
                    b = fi % NB
                    fi += 1
                    hq = n * 4 + g
                    O = pO[g // 2]
                    c0 = (g % 2) * 256
                    S.add("dve", lambda e, b=b, O=O, c0=c0, hq=hq: e.tensor_scalar(out=zr[b][:, :], in0=O[:, c0 + 128:c0 + 129], scalar1=sinkb[:, hq:hq + 1], scalar2=None, op0=ALU.add),
                          [O, sinkb], [zr[b]])
                    S.add("dve", lambda e, b=b: e.reciprocal(out=zr[b][:, :], in_=zr[b][:, :]), [zr[b]], [zr[b]])
                    S.add("dve", lambda e, b=b, O=O, c0=c0: e.tensor_scalar(out=o_t[b][:, :], in0=O[:, c0:c0 + 128], scalar1=zr[b][:, 0:1], scalar2=None, op0=ALU.mult),
                          [O, zr[b]], [o_t[b]])
                    dma("sp", szt[b][:, :], sz_d[i * 128:(i + 1) * 128, hq * 128:(hq + 1) * 128], [("sz", hq // 4, i)], [szt[b]])
                    S.add("pool", lambda e, b=b: e.tensor_tensor(out=gtm[b][:, :], in0=o_t[b][:, :], in1=szt[b][:, :], op=ALU.mult), [o_t[b], szt[b]], [gtm[b]])
                    trs.append((b, g))

                def fin(trs=trs, gts=gts, n=n, i=i):
                    for (b, g) in trs:
                        S.add("pe", lambda e, b=b, g=g: e.transpose(out=pTg[:, g, :], in_=gtm[b][:, :], identity=ident[:, :]), [gtm[b], ident], [pTg])
                    S.add("act", lambda e, gts=gts: e.activation(out=gts[:, :, :], in_=pTg[:, :, :], func=AF.Copy), [pTg], [gts])
                    dma("pool", gT_d[n * 4:(n + 1) * 4, :, i * 128:(i + 1) * 128].rearrange("h p t -> p h t"), gts[:, :, :], [gts],
                        [("gT", n * 4 + g, i) for g in range(4)])
                pend.append(fin)
                while len(pend) > 1:
                    pend.pop(0)()
        while pend:
            pend.pop(0)()

    for l in range(nlayers):
        kind = kinds[l]
        need_ctx_out = any(kinds[j] != 0 for j in range(l + 1, 4))
        if l == 0 and nlayers > 1:
            cast_weights(1, ("in",))
        with ExitStack() as st:
            modulation(l, st)
            S.barrier()
        ctx_t = list(range(NT_LAT, NT))
        ALLR = ("q", "k", "v", "z")
        if l == 0:
            with ExitStack() as st:
                phase_norm(l, st, list(range(NT)))
                S.barrier()
            with ExitStack() as st:
                conv_layer(l, st, True, list(range(NT_LAT)), list(range(NT_LAT)))
                S.barrier()
        elif l == 1:
            H1 = list(range(OWN + 2))
            with ExitStack() as st:
                phase_norm(l, st, list(range(NT)))
                S.barrier()
            roles = ([(s_, "q", s_) for s_ in range(8)] + [(8 + s_, "k", s_) for s_ in range(8)]
                     + [(16 + s_, "v", s_) for s_ in range(8)] + [(24 + s_, "zT", s_) for s_ in range(8)])
            ALLT = ("q", "k", "v", "zT")
            with ExitStack() as st:
                proj_phase(l, st, 64, roles, 0.125, rope64, [(H1, ALLT), (list(range(OWN + 2, NT_LAT)), ("k", "v")), (ctx_t, ALLT)])
                S.barrier()
            cast_weights(1, ("out",))
            for l2 in range(2, nlayers):
                cast_weights(l2)
            with ExitStack() as st:
                attn_diff(l, st, True, H1)
                S.barrier()
            if debug_out != "attn":
                with ExitStack() as st:
                    outproj_phase(l, st, True, H1)
                    S.barrier()
        elif l == 2:
            H1 = list(range(OWN + 2))
            H2 = list(range(OWN + 1))
            with ExitStack() as st:
                phase_norm(l, st, H1 + ctx_t)
                S.barrier()
            roles = ([(s_, "q", s_) for s_ in range(8)] + [(8 + s_, "k", s_) for s_ in range(2)]
                     + [(10 + s_, "v", s_) for s_ in range(2)] + [(12 + s_, "z", s_) for s_ in range(8)])
            with ExitStack() as st:
                proj_phase(l, st, 128, roles, 128 ** -0.5, rope128, [(H1, ALLR), (ctx_t, ("k", "v"))])
                S.barrier()
            with ExitStack() as st:
                attn_win(l, st, H2, H1)
                S.barrier()
            with ExitStack() as st:
                outproj_phase(l, st, False, H2)
                S.barrier()
        else:
            H2 = list(range(OWN + 1))
            with ExitStack() as st:
                phase_norm(l, st, H2)
                S.barrier()
            with ExitStack() as st:
                conv_layer(l, st, False, H2, list(range(OWN)))
                S.barrier()

    if debug_out is not None:
        dbg = dram("dbg", [NT * 128, D], F32, "ExternalOutput")
        with ExitStack() as st:
            t = sb(st, "dbgt", [128, D], F32)
            for tile in range(NT):
                src = xbuf[nlayers % 2][tile * 128:(tile + 1) * 128, :]
                dma("sp", t[:, :], src, [], [t])
                dma("sp", dbg[tile * 128:(tile + 1) * 128, :], t[:, :], [t], [("dbg", tile)])
            S.barrier()

    S.emit(nc, top)
    top.close()
    return nc, S


def make_in_maps(inputs, nlayers=4, cores=range(NCORES)):
    f = lambda a: np.ascontiguousarray(np.asarray(a, dtype=np.float32))
    kinds = [0, 1, 2, 0]
    maps = []
    for c in cores:
        b, mir = c // 2, (c % 2 == 1)
        flip = (lambda a: a[::-1]) if mir else (lambda a: a)
        m = {"x": f(flip(np.asarray(inputs["x"][b]))), "ctx": f(flip(np.asarray(inputs["ctx"][b])))}
        m["ident"] = np.eye(128, dtype=np.float32).astype(ml_dtypes.bfloat16)
        cc = np.stack([np.asarray(inputs["c"][b]), np.asarray(inputs["c_ctx"])], 0)
        m["cT"] = f(cc.reshape(2, KC, 128).transpose(2, 0, 1))
        for l in range(nlayers):
            p = f"l{l}_"
            m[p + "norm"] = f(inputs[p + "norm"]).reshape(1, D)
            m[p + "w_mod"] = f(inputs[p + "w_mod"])
            m[p + "b_mod"] = f(inputs[p + "b_mod"]).reshape(1, 3 * D)
            m[p + "w_in"] = f(inputs[p + "w_in"])
            m[p + "w_out"] = f(inputs[p + "w_out"])
            if kinds[l] == 0:
                cw = np.asarray(inputs[p + "conv_w"])
                if mir:
                    cw = cw[::-1]
                cb = np.asarray(inputs[p + "conv_b"])
                cp = np.concatenate([cw, cb[None, :]], 0)
                m[p + "convp"] = f(cp.reshape(4, 32, 128).transpose(2, 1, 0))
        if nlayers > 1:
            m["rope64"] = rope_table(64, mir)
            m["l1_lamv"] = f(np.concatenate([np.asarray(inputs["l1_lam_" + k_]) for k_ in ("q1", "q2", "k1", "k2")])).reshape(1, 256)
            for k_ in ("q_norm", "k_norm", "sub_norm"):
                m["l1_" + k_] = f(inputs["l1_" + k_]).reshape(1, -1)
        if nlayers > 2:
            m["rope128"] = rope_table(128, mir)
            qq = np.arange(128)[None, :]
            kk = np.arange(128)[:, None]
            mk = np.stack([(qq <= kk), (qq >= kk)], 1).astype(np.float32)
            m["masks"] = mk.astype(ml_dtypes.bfloat16)
            for k_ in ("q_norm", "k_norm", "sink"):
                m["l2_" + k_] = f(inputs["l2_" + k_]).reshape(1, -1)
        maps.append(m)
    return maps


_ROPE_CACHE = {}


def rope_table(head_dim, mirrored=False):
    if (head_dim, mirrored) in _ROPE_CACHE:
        return _ROPE_CACHE[(head_dim, mirrored)]
    rows = T_LAT // 64
    row = np.repeat(np.arange(rows), 64).astype(np.float32)
    col = np.tile(np.arange(64), rows).astype(np.float32)
    n_freq = head_dim // 4
    inv_freq = (np.float32(10000.0) ** (-(np.arange(n_freq, dtype=np.float32) / np.float32(n_freq)))).astype(np.float32)
    ang = np.concatenate([row[:, None] * inv_freq, col[:, None] * inv_freq], axis=-1).astype(np.float32)
    tab = np.zeros((T_LAT + T_CTX, 2, 2 * n_freq), np.float32)
    if mirrored:
        ang = ang[::-1]
    tab[:T_LAT, 0] = np.cos(ang)
    tab[:T_LAT, 1] = np.sin(ang)
    tab[T_LAT:, 0] = 1.0
    _ROPE_CACHE[(head_dim, mirrored)] = tab
    return tab


def kernel(**inputs):
    nc, S = build_program(4)
    maps = make_in_maps(inputs, 4)
    res = run_bass_kernel_spmd(nc, maps, core_ids=list(range(NCORES)))
    out = np.empty((4, T_LAT, D), np.float32)
    half = OWN * 128
    for c in range(NCORES):
        b, mir = c // 2, (c % 2 == 1)
        r = np.asarray(res.results[c]["out"], dtype=np.float32)
        if mir:
            out[b, half:] = r[::-1]
        else:
            out[b, :half] = r
    return out
```

```python
import numpy as np
import ml_dtypes
import concourse.bass as bass
import concourse.mybir as mybir
from concourse.bass_utils import run_bass_kernel_spmd

F32 = mybir.dt.float32
BF16 = mybir.dt.bfloat16
AF = mybir.ActivationFunctionType
ALU = mybir.AluOpType
AX = mybir.AxisListType

D = 2048
DI = 4096
T_LAT = 4096
T_CTX = 256
NT_LAT = T_LAT // 128
NT = (T_LAT + T_CTX) // 128
KC = D // 128
EPS = 1e-6
NCORES = 8
OWN = 16


class _Op:
    __slots__ = ("eng", "fn", "deps", "dma", "sig", "sem", "val", "lane")


class Sched:
    ENGS = ("pe", "act", "dve", "pool", "sp")
    NL = 8

    def __init__(self):
        self.ops = {e: [] for e in self.ENGS}
        self.last_w = {}
        self.rd_eng = {}
        self.rd_dma = {}
        self.lane_rr = {"sp": 0, "pool": 0}
        self.lane_last = {}
        self.last_on = {}

    def add(self, eng, fn, reads=(), writes=(), dma=False):
        op = _Op()
        op.eng, op.fn, op.dma, op.sig, op.sem, op.val, op.lane = eng, fn, dma, False, None, 0, None
        deps = set()
        for r in reads:
            w = self.last_w.get(r)
            if w is not None:
                deps.add(w)
        for r in writes:
            w = self.last_w.get(r)
            if w is not None:
                deps.add(w)
            for o in self.rd_eng.get(r, {}).values():
                deps.add(o)
            for o in self.rd_dma.get(r, ()):
                deps.add(o)
        for r in reads:
            if dma:
                self.rd_dma.setdefault(r, []).append(op)
            else:
                self.rd_eng.setdefault(r, {})[eng] = op
        for r in writes:
            self.last_w[r] = op
            self.rd_eng[r] = {}
            self.rd_dma[r] = []
        if dma:
            lane = (eng, self.lane_rr[eng])
            self.lane_rr[eng] = (self.lane_rr[eng] + 1) % self.NL
            op.lane = lane
            prev = self.lane_last.get(lane)
            if prev is not None:
                deps.add(prev)
            self.lane_last[lane] = op
        if eng == "pe":
            deps = {d for d in deps if d.dma or d.eng != "pe"}
        deps.discard(op)
        op.deps = deps
        self.ops[eng].append(op)
        if not dma:
            self.last_on[eng] = op
        return op

    def barrier(self):
        tails = [o for o in self.last_on.values()]
        for lane, o in self.lane_last.items():
            tails.append(o)
        for e in self.ENGS:
            op = self.add(e, None)
            op.deps = set(t for t in tails)
        self.last_w.clear(); self.rd_eng.clear(); self.rd_dma.clear()

    def emit(self, nc, stack):
        sems = {}
        for e in ("pe", "act", "dve", "pool"):
            sems[e] = stack.enter_context(nc.semaphore("sem_" + e))
        for q in ("sp", "pool"):
            for l in range(self.NL):
                sems[(q, l)] = stack.enter_context(nc.semaphore("lane_%s%d" % (q, l)))
        for e in self.ENGS:
            for op in self.ops[e]:
                for d in op.deps:
                    d.sig = True
        cnt = {k: 0 for k in sems}
        for e in self.ENGS:
            for op in self.ops[e]:
                if op.dma:
                    cnt[op.lane] += 16
                    op.sem, op.val = op.lane, cnt[op.lane]
                elif op.sig and op.fn is not None:
                    cnt[e] += 1
                    op.sem, op.val = e, cnt[e]
        final = dict(cnt)
        block = stack.enter_context(nc.Block())
        engmap = {"pe": block.tensor, "act": block.scalar, "dve": block.vector,
                  "pool": block.gpsimd, "sp": block.sync}
        nwaits = [0]

        def run(ename, e):
            waited = {}
            for op in self.ops[ename]:
                need = {}
                for d in op.deps:
                    if d.sem is None:
                        continue
                    if d.val > need.get(d.sem, 0):
                        need[d.sem] = d.val
                for s, v in need.items():
                    if v > waited.get(s, 0):
                        e.wait_ge(sems[s], v)
                        waited[s] = v
                        nwaits[0] += 1
                if op.fn is None:
                    continue
                ins = op.fn(e)
                if op.dma:
                    ins.then_inc(sems[op.sem], 16)
                elif op.sig:
                    ins.then_inc(sems[op.sem], 1)
            if ename in ("sp", "pool"):
                for l in range(self.NL):
                    if final[(ename, l)] > 0:
                        e.wait_ge(sems[(ename, l)], final[(ename, l)])

        for ename in self.ENGS:
            engmap[ename](lambda e, ename=ename: run(ename, e))
        self.nwaits = nwaits[0]


class Buf:
    def __init__(self, t, name):
        self.t = t
        self.name = name

    def __getitem__(self, idx):
        return self.t[idx]


def build_program(nlayers=4, debug_out=None):
    nc = bass.Bass("TRN2", target_bir_lowering=False)
    S = Sched()
    from contextlib import ExitStack
    top = ExitStack()

    def dram(name, shape, dt, kind="Internal"):
        return nc.dram_tensor(name, list(shape), dt, kind=kind).ap()

    x_in = dram("x", [T_LAT, D], F32, "ExternalInput")
    ctx_in = dram("ctx", [T_CTX, D], F32, "ExternalInput")
    cT_in = dram("cT", [128, 2, KC], F32, "ExternalInput")
    out_d = dram("out", [OWN * 128, D], F32, "ExternalOutput")
    ident_in = dram("ident", [128, 128], BF16, "ExternalInput")
    kinds = [0, 1, 2, 0]
    W = []
    for l in range(nlayers):
        kind = kinds[l]
        n_in = 16384 if kind != 2 else 10240
        w = dict(kind=kind, n_in=n_in)
        w["norm"] = dram(f"l{l}_norm", [1, D], F32, "ExternalInput")
        w["w_mod"] = dram(f"l{l}_w_mod", [D, 3 * D], F32, "ExternalInput")
        w["b_mod"] = dram(f"l{l}_b_mod", [1, 3 * D], F32, "ExternalInput")
        w["w_in"] = dram(f"l{l}_w_in", [D, n_in], F32, "ExternalInput")
        w["w_out"] = dram(f"l{l}_w_out", [DI, D], F32, "ExternalInput")
        if kind == 0:
            w["convp"] = dram(f"l{l}_convp", [128, 32, 4], F32, "ExternalInput")
        w["w_mod_b"] = dram(f"l{l}_w_mod_b", [12, 128, KC, 512], BF16)
        w["w_in_b"] = dram(f"l{l}_w_in_b", [n_in // 512, 128, KC, 512], BF16)
        w["w_out_b"] = dram(f"l{l}_w_out_b", [8, 128, KC, 512], BF16)
        w["mod_d"] = dram(f"l{l}_mod_d", [2, 3 * D], F32)
        W.append(w)

    xbuf = [dram("xbuf0", [NT * 128, D], F32), dram("xbuf1", [NT * 128, D], F32)]
    hT_d = dram("hT_d", [128, KC, NT * 128], BF16)
    UW = 1 + T_LAT + 2 + T_CTX + 1
    u_d = dram("u_d", [32, 128, UW], F32)

    NTOK = NT * 128
    dk = "ExternalOutput" if debug_out in ("full", "attn") else "Internal"
    if debug_out == "attn":
        dbg2 = dram("dbg2", [4, 128, 512], F32, "ExternalOutput")
    qT_d = dram("qT_d", [32, 128, NTOK], BF16, dk)
    kT_d = dram("kT_d", [32, 128, NTOK], BF16, dk)
    v_d = dram("v_d", [NTOK, DI], BF16, dk)
    sz_d = dram("sz_d", [NTOK, DI], F32, dk)
    gT_d = dram("gT_d", [32, 128, NTOK], BF16, dk)
    szT_d = dram("szT_d", [32, 128, NTOK], F32)
    if nlayers > 1:
        rope64 = dram("rope64", [NTOK, 2, 32], F32, "ExternalInput")
        lamv = dram("l1_lamv", [1, 4 * 64], F32, "ExternalInput")
        W[1]["q_norm"] = dram("l1_q_norm", [1, 64], F32, "ExternalInput")
        W[1]["k_norm"] = dram("l1_k_norm", [1, 64], F32, "ExternalInput")
        W[1]["sub_norm"] = dram("l1_sub_norm", [1, 128], F32, "ExternalInput")
    if nlayers > 2:
        rope128 = dram("rope128", [NTOK, 2, 64], F32, "ExternalInput")
        masks_in = dram("masks", [128, 2, 128], BF16, "ExternalInput")
        W[2]["q_norm"] = dram("l2_q_norm", [1, 128], F32, "ExternalInput")
        W[2]["k_norm"] = dram("l2_k_norm", [1, 128], F32, "ExternalInput")
        W[2]["sink"] = dram("l2_sink", [1, 32], F32, "ExternalInput")

    def ucol(tok):
        return 1 + tok if tok < T_LAT else 1 + tok + 2

    uid = [0]

    def sb(stack, name, shape, dt):
        uid[0] += 1
        name = "s%d_%s" % (uid[0], name)
        return Buf(stack.enter_context(nc.sbuf_tensor(name, list(shape), dt)), name)

    def ps(stack, name, shape, dt):
        uid[0] += 1
        name = "p%d_%s" % (uid[0], name)
        return Buf(stack.enter_context(nc.psum_tensor(name, list(shape), dt)), name)

    def dma(q, out_ap, in_ap, reads, writes, **kw):
        return S.add(q, lambda e: e.dma_start(out=out_ap, in_=in_ap, **kw), reads, writes, dma=True)

    def x_src(l, tile):
        if l == 0:
            if tile < NT_LAT:
                return x_in[tile * 128:(tile + 1) * 128, :], None
            return ctx_in[(tile - NT_LAT) * 128:(tile - NT_LAT + 1) * 128, :], None
        return xbuf[l % 2][tile * 128:(tile + 1) * 128, :], ("x", l, tile)

    def x_dst(l, tile):
        if l == nlayers - 1 and tile < OWN and debug_out is None:
            return out_d[tile * 128:(tile + 1) * 128, :], ("out", tile)
        return xbuf[(l + 1) % 2][tile * 128:(tile + 1) * 128, :], ("x", l + 1, tile)

    def cast_weights(l, parts=("in", "out")):
        w = W[l]
        if "in" in parts:
            for s_ in range(w["n_in"] // 512):
                src = w["w_in"][:, s_ * 512:(s_ + 1) * 512].rearrange("(k p) n -> p k n", p=128)
                dma("pool", w["w_in_b"][s_], src, [], [("win", l, s_)])
        if "out" in parts:
            for n in range(4):
                for kh in range(2):
                    src = w["w_out"][kh * 2048:(kh + 1) * 2048, n * 512:(n + 1) * 512].rearrange("(k p) n -> p k n", p=128)
                    dma("pool", w["w_out_b"][n * 2 + kh], src, [], [("wout", l, n * 2 + kh)])

    eps_t = sb(top, "eps_t", [128, 1], F32)
    S.add("dve", lambda e: e.memset(eps_t[:, :], EPS), [], [eps_t])
    zcol = sb(top, "zcol", [128, 2], F32)
    S.add("dve", lambda e: e.memset(zcol[:, :], 0.0), [], [zcol])
    for (c0, n) in ((0, 1), (T_LAT + 1, 2), (UW - 1, 1)):
        dma("sp", u_d[:, :, c0:c0 + n].rearrange("j p c -> p j c"),
            zcol[:, 0:n].unsqueeze(1).to_broadcast([128, 32, n]), [zcol], [("upad", c0)], allow_slow_non_contiguous=True)

    ident = sb(top, "ident", [128, 128], BF16)
    dma("sp", ident[:, :], ident_in[:, :], [], [ident])

    cast_weights(0)

    def modulation(l, stack):
        w = W[l]
        cT = sb(stack, "cT", [128, 2, KC], F32)
        sc = sb(stack, "sc", [128, KC, 2], F32)
        bm = sb(stack, "bm", [2, 3 * D], F32)
        gg = sb(stack, "gg", [2, D], F32)
        msb = sb(stack, "msb", [2, 3 * D], F32)
        wm = [sb(stack, "wm%d" % i, [128, KC, 512], F32) for i in range(2)]
        pm = [ps(stack, "pm%d" % i, [2, 512], F32) for i in range(2)]
        dma("sp", cT[:, :, :], cT_in[:, :, :], [], [cT])
        dma("sp", bm[:, :], w["b_mod"][0:1, :].partition_broadcast(2).rearrange("p o n -> p (o n)"), [], [bm])
        dma("sp", gg[:, :], w["norm"][0:1, :].partition_broadcast(2).rearrange("p o n -> p (o n)"), [], [gg])
        S.add("act", lambda e: e.activation(out=sc[:, :, :].rearrange("p k r -> p r k"), in_=cT[:, :, :], func=AF.Silu),
              [cT], [sc])
        for s in range(12):
            wb = wm[s % 2]
            pb = pm[s % 2]
            dma("sp", wb[:, :, :], w["w_mod"][:, s * 512:(s + 1) * 512].rearrange("(k p) n -> p k n", p=128), [], [wb])
            for k in range(KC):
                S.add("pe", lambda e, k=k, wb=wb, pb=pb: e.matmul(pb[:, :], sc[:, k, :], wb[:, k, :], start=(k == 0), stop=(k == KC - 1)),
                      [sc, wb], [pb])
            S.add("dve", lambda e, s=s, pb=pb: e.tensor_tensor(out=msb[:, s * 512:(s + 1) * 512], in0=pb[:, :], in1=bm[:, s * 512:(s + 1) * 512], op=ALU.add),
                  [pb, bm], [msb])
        S.add("dve", lambda e: e.scalar_tensor_tensor(out=msb[:, D:2 * D], in0=msb[:, D:2 * D], scalar=1.0, in1=gg[:, :], op0=ALU.add, op1=ALU.mult),
              [msb, gg], [msb])
        md = w["mod_d"]
        dma("sp", md[:, 0:D], msb[:, D:2 * D], [msb], [("mod", l, 0)])
        dma("sp", md[:, D:2 * D], msb[:, 0:D], [msb], [("mod", l, 1)])
        dma("sp", md[:, 2 * D:3 * D], msb[:, 2 * D:3 * D], [msb], [("mod", l, 2)])

    def load_mod(l, tile_buf, row, which):
        md = W[l]["mod_d"]
        src = md[row:row + 1, which * D:(which + 1) * D].partition_broadcast(128).rearrange("p o n -> p (o n)")
        dma("sp", tile_buf[:, :], src, [("mod", l, which)], [tile_buf])

    def phase_norm(l, stack, tiles):
        gs = sb(stack, "gs", [128, D], F32)
        sh = sb(stack, "sh", [128, D], F32)
        xt = [sb(stack, "xt%d" % i, [128, D], F32) for i in range(2)]
        hb = [sb(stack, "hb%d" % i, [128, D], BF16) for i in range(2)]
        junk = sb(stack, "junk", [128, D], BF16)
        ss = [sb(stack, "ss%d" % i, [128, 1], F32) for i in range(2)]
        rs = [sb(stack, "rs%d" % i, [128, 1], F32) for i in range(2)]
        hTs = [sb(stack, "hTs%d" % i, [128, KC, 128], BF16) for i in range(2)]
        pT = [ps(stack, "pT%d" % i, [128, KC, 128], BF16) for i in range(2)]
        cur_row = None
        for n, tile in enumerate(tiles):
            row = 0 if tile < NT_LAT else 1
            if row != cur_row:
                load_mod(l, gs, row, 0)
                load_mod(l, sh, row, 1)
                cur_row = row
            b = n % 2
            x_ap, x_res = x_src(l, tile)
            dma("sp", xt[b][:, :], x_ap, [x_res] if x_res else [], [xt[b]])
            S.add("act", lambda e, b=b: e.activation(out=junk[:, :], in_=xt[b][:, :], func=AF.Square, accum_out=ss[b][:, :]),
                  [xt[b]], [junk, ss[b]])
            S.add("act", lambda e, b=b: e.activation(out=rs[b][:, :], in_=ss[b][:, :], func=AF.Sqrt, bias=eps_t[:, :], scale=1.0 / D),
                  [ss[b], eps_t], [rs[b]])
            S.add("dve", lambda e, b=b: e.reciprocal(out=rs[b][:, :], in_=rs[b][:, :]), [rs[b]], [rs[b]])
            S.add("dve", lambda e, b=b: e.scalar_tensor_tensor(out=xt[b][:, :], in0=xt[b][:, :], scalar=rs[b][:, 0:1], in1=gs[:, :], op0=ALU.mult, op1=ALU.mult),
                  [xt[b], rs[b], gs], [xt[b]])
            S.add("pool", lambda e, b=b: e.tensor_tensor(out=hb[b][:, :], in0=xt[b][:, :], in1=sh[:, :], op=ALU.add),
                  [xt[b], sh], [hb[b]])
            for k in range(KC):
                S.add("pe", lambda e, b=b, k=k: e.transpose(out=pT[b][:, k, :], in_=hb[b][:, k * 128:(k + 1) * 128], identity=ident[:, :]),
                      [hb[b], ident], [pT[b]])
            S.add("act", lambda e, b=b: e.activation(out=hTs[b][:, :, :], in_=pT[b][:, :, :], func=AF.Copy),
                  [pT[b]], [hTs[b]])
            dma("pool", hT_d[:, :, tile * 128:(tile + 1) * 128], hTs[b][:, :, :], [hTs[b]], [("hT", tile)])

    def token_blocks(tiles_lat, tiles_ctx, tb_tiles):
        blocks = []
        for group in (tiles_lat, tiles_ctx):
            for i in range(0, len(group), tb_tiles):
                blocks.append(group[i:i + tb_tiles])
        return blocks

    def load_hT_block(hTb, blk):
        n = len(blk)
        src = hT_d[:, :, blk[0] * 128:(blk[0] + n) * 128]
        dst = hTb[:, :, 0:n * 128]
        dma("sp", dst, src, [("hT", t) for t in blk], [hTb])

    def outproj_block(l, blk, gated, gate, wslab, pY, xo, xn, eit):
        w = W[l]
        for n in range(4):
            for kh in range(2):
                wo = wslab(w["w_out_b"][n * 2 + kh], ("wout", l, n * 2 + kh))
                for ti, tile in enumerate(blk):
                    for k in range(KC):
                        S.add("pe", lambda e, ti=ti, wo=wo, k=k, kh=kh: e.matmul(
                            pY[ti][:, :], gated[:, kh * 16 + k, ti * 128:(ti + 1) * 128], wo[:, k, :],
                            start=(kh == 0 and k == 0), stop=(kh == 1 and k == KC - 1)),
                            [wo, gated], [pY[ti]])
            for ti, tile in enumerate(blk):
                xob, xnb = xo[eit % 2], xn[eit % 2]
                eit += 1
                x_ap, x_res = x_src(l, tile)
                dma("sp", xob[:, :], x_ap[:, n * 512:(n + 1) * 512], [x_res] if x_res else [], [xob])
                S.add("dve", lambda e, xnb=xnb, ti=ti, n=n: e.tensor_tensor(out=xnb[:, :], in0=pY[ti][:, :], in1=gate[:, n * 512:(n + 1) * 512], op=ALU.mult),
                      [pY[ti], gate], [xnb])
                S.add("pool", lambda e, xnb=xnb, xob=xob: e.tensor_tensor(out=xnb[:, :], in0=xnb[:, :], in1=xob[:, :], op=ALU.add),
                      [xnb, xob], [xnb])
                d_ap, d_res = x_dst(l, tile)
                dma("pool", d_ap[:, n * 512:(n + 1) * 512], xnb[:, :], [xnb], [(d_res, n)])
        return eit

    def make_wslab(stack, nbuf=4):
        wsl = [sb(stack, "wsl%d" % i, [128, KC, 512], BF16) for i in range(nbuf)]
        wrr = [0]

        def wslab(src_ap, res):
            b = wsl[wrr[0] % nbuf]
            wrr[0] += 1
            dma("sp", b[:, :, :], src_ap, [res], [b])
            return b
        return wslab

    def outproj_phase(l, stack, need_ctx, lat):
        ctxt = list(range(NT_LAT, NT)) if need_ctx else []
        blocks = token_blocks(lat, ctxt, 4)
        wslab = make_wslab(stack)
        gated = sb(stack, "gated", [128, 32, 512], BF16)
        gate = sb(stack, "gate", [128, D], F32)
        xo = [sb(stack, "xo%d" % i, [128, 512], F32) for i in range(2)]
        xn = [sb(stack, "xn%d" % i, [128, 512], F32) for i in range(2)]
        pY = [ps(stack, "pY%d" % i, [128, 512], F32) for i in range(4)]
        cur_row = None
        eit = 0
        for blk in blocks:
            ntok = len(blk) * 128
            row = 0 if blk[0] < NT_LAT else 1
            if row != cur_row:
                load_mod(l, gate, row, 2)
                cur_row = row
            t0 = blk[0] * 128
            dma("sp", gated[:, :, 0:ntok], gT_d[:, :, t0:t0 + ntok].rearrange("j p t -> p j t"),
                [("gT", j, t) for j in range(32) for t in blk], [gated])
            eit = outproj_block(l, blk, gated, gate, wslab, pY, xo, xn, eit)

    def conv_layer(l, stack, need_ctx, latA, latB):
        w = W[l]
        ctxt = list(range(NT_LAT, NT)) if need_ctx else []
        blocksA = token_blocks(latA, ctxt, 4)
        blocksB = token_blocks(latB, ctxt, 4)
        blocks = blocksA
        wsl = [sb(stack, "wsl%d" % i, [128, KC, 512], BF16) for i in range(4)]
        wrr = [0]

        def wslab(src_ap, res):
            b = wsl[wrr[0] % 4]
            wrr[0] += 1
            dma("sp", b[:, :, :], src_ap, [res], [b])
            return b

        hTb = sb(stack, "hTb", [128, KC, 512], BF16)
        pA = [ps(stack, "pA%d" % i, [128, 512], F32) for i in range(4)]
        cg_sb = [sb(stack, "cg_sb%d" % i, [128, 512], F32) for i in range(2)]
        u_sb = [sb(stack, "u_sb%d" % i, [128, 512], F32) for i in range(2)]
        it = 0
        for blk in blocks:
            ntok = len(blk) * 128
            load_hT_block(hTb, blk)
            for g in range(8):
                wcg = wslab(w["w_in_b"][8 + g], ("win", l, 8 + g))
                wxt = wslab(w["w_in_b"][16 + g], ("win", l, 16 + g))
                for jj in range(4):
                    j = g * 4 + jj
                    p0 = pA[(it * 2) % 4]
                    p1 = pA[(it * 2 + 1) % 4]
                    cs = cg_sb[it % 2]
                    us = u_sb[it % 2]
                    it += 1
                    for (pp, ww) in ((p0, wcg), (p1, wxt)):
                        for k in range(KC):
                            S.add("pe", lambda e, pp=pp, ww=ww, k=k, jj=jj, ntok=ntok: e.matmul(
                                pp[:, 0:ntok], ww[:, k, jj * 128:(jj + 1) * 128], hTb[:, k, 0:ntok], start=(k == 0), stop=(k == KC - 1)),
                                [ww, hTb], [pp])
                    S.add("act", lambda e, cs=cs, p0=p0, ntok=ntok: e.activation(out=cs[:, 0:ntok], in_=p0[:, 0:ntok], func=AF.Copy),
                          [p0], [cs])
                    S.add("dve", lambda e, us=us, cs=cs, p1=p1, ntok=ntok: e.tensor_tensor(out=us[:, 0:ntok], in0=cs[:, 0:ntok], in1=p1[:, 0:ntok], op=ALU.mult),
                          [cs, p1], [us])
                    c0 = ucol(blk[0] * 128)
                    dma("pool", u_d[j, :, c0:c0 + ntok], us[:, 0:ntok], [us], [("u", j, blk[0])])
        gated = sb(stack, "gated", [128, 32, 512], BF16)
        uw = [sb(stack, "uw%d" % i, [128, 514], F32) for i in range(2)]
        yv = [sb(stack, "yv%d" % i, [128, 512], F32) for i in range(2)]
        sz = [sb(stack, "sz%d" % i, [128, 512], F32) for i in range(2)]
        cvp = sb(stack, "cvp", [128, 32, 4], F32)
        gate = sb(stack, "gate", [128, D], F32)
        xo = [sb(stack, "xo%d" % i, [128, 512], F32) for i in range(2)]
        xn = [sb(stack, "xn%d" % i, [128, 512], F32) for i in range(2)]
        pY = [ps(stack, "pY%d" % i, [128, 512], F32) for i in range(4)]
        dma("sp", cvp[:, :, :], w["convp"][:, :, :], [], [cvp])
        cur_row = None
        it = 0
        eit = 0
        for blk in blocksB:
            ntok = len(blk) * 128
            row = 0 if blk[0] < NT_LAT else 1
            if row != cur_row:
                load_mod(l, gate, row, 2)
                cur_row = row
            load_hT_block(hTb, blk)
            c0 = ucol(blk[0] * 128)
            for g in range(8):
                wbg = wslab(w["w_in_b"][g], ("win", l, g))
                wz = wslab(w["w_in_b"][24 + g], ("win", l, 24 + g))
                for jj in range(4):
                    j = g * 4 + jj
                    p0 = pA[(it * 2) % 4]
                    p1 = pA[(it * 2 + 1) % 4]
                    uwb, yb, szb = uw[it % 2], yv[it % 2], sz[it % 2]
                    it += 1
                    ures = [("u", j, bb[0]) for bb in blocksA]
                    dma("sp", uwb[:, 0:ntok + 2], u_d[j, :, c0 - 1:c0 + ntok + 1], ures + [("upad", 0), ("upad", T_LAT + 1), ("upad", UW - 1)], [uwb])
                    for (pp, ww) in ((p0, wbg), (p1, wz)):
                        for k in range(KC):
                            S.add("pe", lambda e, pp=pp, ww=ww, k=k, jj=jj, ntok=ntok: e.matmul(
                                pp[:, 0:ntok], ww[:, k, jj * 128:(jj + 1) * 128], hTb[:, k, 0:ntok], start=(k == 0), stop=(k == KC - 1)),
                                [ww, hTb], [pp])
                    S.add("pool", lambda e, yb=yb, uwb=uwb, j=j, ntok=ntok: e.tensor_scalar(
                        out=yb[:, 0:ntok], in0=uwb[:, 0:ntok], scalar1=cvp[:, j, 0:1], scalar2=cvp[:, j, 3:4], op0=ALU.mult, op1=ALU.add),
                        [uwb, cvp], [yb])
                    S.add("dve", lambda e, yb=yb, uwb=uwb, j=j, ntok=ntok: e.scalar_tensor_tensor(
                        out=yb[:, 0:ntok], in0=uwb[:, 1:ntok + 1], scalar=cvp[:, j, 1:2], in1=yb[:, 0:ntok], op0=ALU.mult, op1=ALU.add),
                        [uwb, cvp, yb], [yb])
                    S.add("dve", lambda e, yb=yb, uwb=uwb, j=j, ntok=ntok: e.scalar_tensor_tensor(
                        out=yb[:, 0:ntok], in0=uwb[:, 2:ntok + 2], scalar=cvp[:, j, 2:3], in1=yb[:, 0:ntok], op0=ALU.mult, op1=ALU.add),
                        [uwb, cvp, yb], [yb])
                    S.add("act", lambda e, szb=szb, p1=p1, ntok=ntok: e.activation(out=szb[:, 0:ntok], in_=p1[:, 0:ntok], func=AF.Silu),
                          [p1], [szb])
                    S.add("dve", lambda e, yb=yb, p0=p0, ntok=ntok: e.tensor_tensor(out=yb[:, 0:ntok], in0=yb[:, 0:ntok], in1=p0[:, 0:ntok], op=ALU.mult),
                          [yb, p0], [yb])
                    S.add("dve", lambda e, yb=yb, szb=szb, j=j, ntok=ntok: e.tensor_tensor(out=gated[:, j, 0:ntok], in0=yb[:, 0:ntok], in1=szb[:, 0:ntok], op=ALU.mult),
                          [yb, szb], [gated])
            eit = outproj_block(l, blk, gated, gate, wslab, pY, xo, xn, eit)

    def bload(tile_buf, src_row_ap, reads=()):
        dma("sp", tile_buf[:, :], src_row_ap.partition_broadcast(128).rearrange("p o n -> p (o n)"), list(reads), [tile_buf])

    def proj_phase(l, stack, hd, roles, qscale, rope_d, groups):
        w = W[l]
        ng = 512 // hd
        nf = hd // 4
        blocks = []
        for (gt, groles) in groups:
            for i in range(0, len(gt), 4):
                blocks.append((gt[i:i + 4], groles))
        wslab = make_wslab(stack)
        hTb = sb(stack, "hTb", [128, KC, 512], BF16)
        nw = {"q": sb(stack, "nwq", [128, hd], F32), "k": sb(stack, "nwk", [128, hd], F32)}
        bload(nw["q"], w["q_norm"][0:1, :])
        bload(nw["k"], w["k_norm"][0:1, :])
        S.add("dve", lambda e: e.tensor_scalar(out=nw["q"][:, :], in0=nw["q"][:, :], scalar1=float(qscale), scalar2=None, op0=ALU.mult),
              [nw["q"]], [nw["q"]])
        pp = [ps(stack, "pp%d" % i, [128, 512], F32) for i in range(4)]
        pT = [ps(stack, "pT%d" % i, [128, 4, 128], BF16) for i in range(2)]
        csx = [sb(stack, "csx%d" % i, [128, ng, 2 * nf], F32) for i in range(4)]
        snx = [sb(stack, "snx%d" % i, [128, ng, 2 * nf], F32) for i in range(4)]
        NB = 4
        qraw = [sb(stack, "qraw%d" % i, [128, 512], F32) for i in range(NB)]
        sq = [sb(stack, "sq%d" % i, [128, 512], F32) for i in range(NB)]
        ss8 = [sb(stack, "ss8%d" % i, [128, ng], F32) for i in range(NB)]
        rs8 = [sb(stack, "rs8%d" % i, [128, ng], F32) for i in range(NB)]
        qn = [sb(stack, "qn%d" % i, [128, 512], F32) for i in range(NB)]
        tA = [sb(stack, "tA%d" % i, [128, 256], F32) for i in range(NB)]
        tB = [sb(stack, "tB%d" % i, [128, 256], F32) for i in range(NB)]
        tC = [sb(stack, "tC%d" % i, [128, 256], F32) for i in range(NB)]
        tD = [sb(stack, "tD%d" % i, [128, 256], F32) for i in range(NB)]
        qr = [sb(stack, "qr%d" % i, [128, 512], BF16) for i in range(NB)]
        qTs = [sb(stack, "qTs%d" % i, [128, 4, 128], BF16) for i in range(NB)]
        vb = [sb(stack, "vb%d" % i, [128, 512], BF16) for i in range(NB)]
        szb = [sb(stack, "szb%d" % i, [128, 512], F32) for i in range(NB)]
        it = 0
        trc = [0]
        pending = []
        for (blk, broles) in blocks:
            ntok = len(blk) * 128
            load_hT_block(hTb, blk)
            for ti, tile in enumerate(blk):
                t0 = tile * 128
                dma("sp", csx[ti][:, :, :], rope_d[t0:t0 + 128, 0, :].unsqueeze(1).to_broadcast([128, ng, 2 * nf]), [], [csx[ti]])
                dma("sp", snx[ti][:, :, :], rope_d[t0:t0 + 128, 1, :].unsqueeze(1).to_broadcast([128, ng, 2 * nf]), [], [snx[ti]])
            for (s_idx, role, base) in roles:
                if role not in broles:
                    continue
                wsb = wslab(w["w_in_b"][s_idx], ("win", l, s_idx))
                if role == "zT":
                    tb0 = blk[0] * 128
                    for c in range(4):
                        p = pp[it % 4]
                        b = it % NB
                        it += 1
                        for k in range(KC):
                            S.add("pe", lambda e, p=p, k=k, c=c, wsb=wsb, ntok=ntok: e.matmul(
                                p[:, 0:ntok], wsb[:, k, c * 128:(c + 1) * 128], hTb[:, k, 0:ntok], start=(k == 0), stop=(k == KC - 1)),
                                [hTb, wsb], [p])
                        S.add("act", lambda e, p=p, b=b, ntok=ntok: e.activation(out=szb[b][:, 0:ntok], in_=p[:, 0:ntok], func=AF.Silu), [p], [szb[b]])
                        dma("pool", szT_d[base * 4 + c, :, tb0:tb0 + ntok], szb[b][:, 0:ntok], [szb[b]], [("szT", base * 4 + c, t) for t in blk])
                    continue
                for ti, tile in enumerate(blk):
                    t0 = tile * 128
                    p = pp[it % 4]
                    b = it % NB
                    it += 1
                    for k in range(KC):
                        S.add("pe", lambda e, p=p, k=k, ti=ti, wsb=wsb: e.matmul(
                            p[:, :], hTb[:, k, ti * 128:(ti + 1) * 128], wsb[:, k, :], start=(k == 0), stop=(k == KC - 1)),
                            [hTb, wsb], [p])
                    if role == "v":
                        S.add("act", lambda e, p=p, b=b: e.activation(out=vb[b][:, :], in_=p[:, :], func=AF.Copy), [p], [vb[b]])
                        dma("pool", v_d[t0:t0 + 128, base * 512:(base + 1) * 512], vb[b][:, :], [vb[b]], [("v", base, tile)])
                    elif role == "z":
                        S.add("act", lambda e, p=p, b=b: e.activation(out=szb[b][:, :], in_=p[:, :], func=AF.Silu), [p], [szb[b]])
                        dma("pool", sz_d[t0:t0 + 128, base * 512:(base + 1) * 512], szb[b][:, :], [szb[b]], [("sz", base, tile)])
                    else:
                        S.add("act", lambda e, p=p, b=b: e.activation(out=sq[b][:, :], in_=p[:, :], func=AF.Square), [p], [sq[b]])
                        S.add("act", lambda e, p=p, b=b: e.activation(out=qraw[b][:, :], in_=p[:, :], func=AF.Copy), [p], [qraw[b]])
                        p3 = qraw[b][:, :].rearrange("p (g d) -> p g d", d=hd)
                        S.add("dve", lambda e, b=b: e.tensor_reduce(out=ss8[b][:, :], in_=sq[b][:, :].rearrange("p (g d) -> p g d", d=hd), axis=AX.X, op=ALU.add),
                              [sq[b]], [ss8[b]])
                        S.add("act", lambda e, b=b: e.activation(out=rs8[b][:, :], in_=ss8[b][:, :], func=AF.Sqrt, bias=eps_t[:, :], scale=1.0 / hd),
                              [ss8[b], eps_t], [rs8[b]])
                        S.add("dve", lambda e, b=b: e.reciprocal(out=rs8[b][:, :], in_=rs8[b][:, :]), [rs8[b]], [rs8[b]])
                        S.add("dve", lambda e, b=b, p3=p3: e.tensor_tensor(out=qn[b][:, :].rearrange("p (g d) -> p g d", d=hd), in0=p3,
                                                                            in1=rs8[b][:, :].unsqueeze(2).to_broadcast([128, ng, hd]), op=ALU.mult),
                              [qraw[b], rs8[b]], [qn[b]])
                        nwt = nw[role]
                        S.add("pool", lambda e, b=b, nwt=nwt: e.tensor_tensor(out=qn[b][:, :].rearrange("p (g d) -> p g d", d=hd),
                                                                               in0=qn[b][:, :].rearrange("p (g d) -> p g d", d=hd),
                                                                               in1=nwt[:, :].unsqueeze(1).to_broadcast([128, ng, hd]), op=ALU.mult),
                              [qn[b], nwt], [qn[b]])
                        qv = qn[b][:, :].rearrange("p (ga h f) -> p ga h f", h=2, f=nf)
                        ov = qr[b][:, :].rearrange("p (ga h f) -> p ga h f", h=2, f=nf)
                        t1, t2 = qv[:, :, 0, :], qv[:, :, 1, :]
                        cs = csx[ti][:, :, :].rearrange("p g (a f) -> p (g a) f", f=nf)
                        sn = snx[ti][:, :, :].rearrange("p g (a f) -> p (g a) f", f=nf)
                        v3 = lambda tb: tb[:, :].rearrange("p (ga f) -> p ga f", f=nf)
                        S.add("pool", lambda e, b=b, t1=t1, cs=cs: e.tensor_tensor(out=v3(tA[b]), in0=t1, in1=cs, op=ALU.mult), [qn[b], csx[ti]], [tA[b]])
                        S.add("pool", lambda e, b=b, t2=t2, sn=sn: e.tensor_tensor(out=v3(tB[b]), in0=t2, in1=sn, op=ALU.mult), [qn[b], snx[ti]], [tB[b]])
                        S.add("dve", lambda e, b=b, ov=ov: e.tensor_tensor(out=ov[:, :, 0, :], in0=v3(tA[b]), in1=v3(tB[b]), op=ALU.subtract), [tA[b], tB[b]], [qr[b]])
                        S.add("dve", lambda e, b=b, t1=t1, sn=sn: e.tensor_tensor(out=v3(tC[b]), in0=t1, in1=sn, op=ALU.mult), [qn[b], snx[ti]], [tC[b]])
                        S.add("pool", lambda e, b=b, t2=t2, cs=cs: e.tensor_tensor(out=v3(tD[b]), in0=t2, in1=cs, op=ALU.mult), [qn[b], csx[ti]], [tD[b]])
                        S.add("dve", lambda e, b=b, ov=ov: e.tensor_tensor(out=ov[:, :, 1, :], in0=v3(tC[b]), in1=v3(tD[b]), op=ALU.add), [tC[b], tD[b]], [qr[b]])
                        def fin(b=b, role=role, base=base, tile=tile, t0=t0):
                            pTb = pT[trc[0] % 2]
                            trc[0] += 1
                            for c in range(4):
                                S.add("pe", lambda e, b=b, c=c, pTb=pTb: e.transpose(out=pTb[:, c, :], in_=qr[b][:, c * 128:(c + 1) * 128], identity=ident[:, :]),
                                      [qr[b], ident], [pTb])
                            S.add("act", lambda e, b=b, pTb=pTb: e.activation(out=qTs[b][:, :, :], in_=pTb[:, :, :], func=AF.Copy), [pTb], [qTs[b]])
                            dst_t = qT_d if role == "q" else kT_d
                            dma("pool", dst_t[base * 4:(base + 1) * 4, :, t0:t0 + 128].rearrange("h p t -> p h t"), qTs[b][:, :, :], [qTs[b]],
                                [(role + "T", base * 4 + c, tile) for c in range(4)])
                        pending.append(fin)
                        while len(pending) > 2:
                            pending.pop(0)()
        while pending:
            pending.pop(0)()

    def load_head_kv(kTb, vab, kidx, vcol, ranges):
        for tiles in ranges:
            load_head_kv1(kTb, vab, kidx, vcol, tiles)

    def load_head_kv1(kTb, vab, kidx, vcol, tiles):
        t0, t1 = tiles[0] * 128, (tiles[-1] + 1) * 128
        dma("sp", kTb[:, t0:t1], kT_d[kidx, :, t0:t1], [("kT", kidx, t) for t in tiles], [kTb])
        dma("sp", vab[:, tiles[0]:tiles[-1] + 1, 0:128], v_d[t0:t1, vcol * 128:(vcol + 1) * 128].rearrange("(t p) e -> p t e", p=128),
            [("v", vcol // 4, t) for t in tiles], [vab])

    def attn_diff(l, stack, need_ctx, qlat):
        import math
        w = W[l]
        lam_init = 0.8 - 0.6 * math.exp(-0.3 * l)
        lqk = sb(stack, "lqk", [128, 4 * 64], F32)
        bload(lqk, lamv[0:1, :])
        prod = sb(stack, "prod", [128, 2, 64], F32)
        s2 = sb(stack, "s2", [128, 2], F32)
        e2 = sb(stack, "e2", [128, 2], F32)
        nlam = sb(stack, "nlam", [128, 1], F32)
        S.add("dve", lambda e: e.tensor_tensor(out=prod[:, :, :], in0=lqk[:, 0:128].rearrange("p (a d) -> p a d", d=64),
                                                in1=lqk[:, 128:256].rearrange("p (a d) -> p a d", d=64), op=ALU.mult), [lqk], [prod])
        S.add("dve", lambda e: e.tensor_reduce(out=s2[:, :], in_=prod[:, :, :], axis=AX.X, op=ALU.add), [prod], [s2])
        S.add("act", lambda e: e.activation(out=e2[:, :], in_=s2[:, :], func=AF.Exp), [s2], [e2])
        S.add("dve", lambda e: e.tensor_tensor(out=nlam[:, :], in0=e2[:, 1:2], in1=e2[:, 0:1], op=ALU.subtract), [e2], [nlam])
        S.add("dve", lambda e: e.tensor_scalar(out=nlam[:, :], in0=nlam[:, :], scalar1=float(-lam_init), scalar2=None, op0=ALU.add), [nlam], [nlam])
        snwc = sb(stack, "snwc", [128, 1], F32)
        dma("sp", snwc[:, :], w["sub_norm"][0:1, :].rearrange("o n -> n o"), [], [snwc], allow_slow_non_contiguous=True)
        S.add("dve", lambda e: e.tensor_scalar(out=snwc[:, :], in0=snwc[:, :], scalar1=float(1.0 - lam_init), scalar2=None, op0=ALU.mult), [snwc], [snwc])
        ones_f = sb(stack, "ones_f", [128, 128], F32)
        S.add("dve", lambda e: e.memset(ones_f[:, :], 1.0), [], [ones_f])

        kTh = [sb(stack, "kTh%d" % i, [128, NTOK], BF16) for i in range(2)]
        vt = [sb(stack, "vt%d" % i, [128, NT, 128], BF16) for i in range(2)]
        qTb = [sb(stack, "qTb%d" % i, [128, 512], BF16) for i in range(2)]
        E2 = [sb(stack, "E2_%d" % i, [128, 2, 512], BF16) for i in range(6)]
        Eacc = [sb(stack, "Eacc%d" % i, [128, 512], F32) for i in range(2)]
        ones_b = sb(stack, "ones_b", [128, 128], BF16)
        S.add("dve", lambda e: e.memset(ones_b[:, :], 1.0), [], [ones_b])
        psZ1 = ps(stack, "psZ1", [128, 512], F32)
        psS2 = [ps(stack, "psS2_%d" % i, [128, 2, 512], F32) for i in range(2)]
        psOT = [ps(stack, "psOT%d" % m, [128, 512], F32) for m in range(2)]
        psB = ps(stack, "psB", [128, 512], F32)
        OTs = [sb(stack, "OTs%d" % m, [128, 512], F32) for m in range(2)]
        R = [sb(stack, "R%d" % m, [128, 512], F32) for m in range(2)]
        ta = sb(stack, "ta", [128, 512], F32)
        tb = sb(stack, "tb", [128, 512], F32)
        oT = sb(stack, "oT", [128, 512], F32)
        sq = sb(stack, "sq", [128, 512], F32)
        rstd = sb(stack, "rstd", [128, 512], F32)
        szT = [sb(stack, "szT%d" % i, [128, 512], F32) for i in range(2)]
        gT = [sb(stack, "gT%d" % i, [128, 512], BF16) for i in range(2)]
        qblocks = token_blocks(qlat, list(range(NT_LAT, NT)) if need_ctx else [], 4)
        rr = 0
        gi = 0
        for h in range(32):
            kTb, vab = kTh[h % 2], vt[h % 2]
            load_head_kv(kTb, vab, h, h, [list(range(NT))])
            for bi, blk in enumerate(qblocks):
                nq = len(blk) * 128
                q0 = blk[0] * 128
                is_ctx = blk[0] >= NT_LAT
                ktiles = list(range(NT_LAT, NT)) if is_ctx else list(range(NT))
                qb = qTb[gi % 2]
                EA = Eacc[gi % 2]
                szb, gtb = szT[gi % 2], gT[gi % 2]
                gi += 1
                dma("sp", qb[:, 0:nq], qT_d[h, :, q0:q0 + nq], [("qT", h, t) for t in blk], [qb])
                dma("sp", szb[:, 0:nq], szT_d[h, :, q0:q0 + nq], [("szT", h, t) for t in blk], [szb])
                npair = len(ktiles)
                bufs = [(psS2[(rr + p) % 2], E2[(rr + p) % 6]) for p in range(npair)]
                rr += npair

                def score(p):
                    kt = ktiles[p]
                    pS = bufs[p][0]
                    for m in range(2):
                        S.add("pe", lambda e, pS=pS, kt=kt, m=m, kTb=kTb, qb=qb, nq=nq: e.matmul(
                            pS[:, m, 0:nq], kTb[m * 64:(m + 1) * 64, kt * 128:(kt + 1) * 128], qb[m * 64:(m + 1) * 64, 0:nq], start=True, stop=True),
                            [kTb, qb], [pS])
                score(0)
                for p in range(npair):
                    kt = ktiles[p]
                    pS, Eb = bufs[p]
                    if p + 1 < npair:
                        score(p + 1)
                    S.add("act", lambda e, pS=pS, Eb=Eb, nq=nq: e.activation(out=Eb[:, :, 0:nq], in_=pS[:, :, 0:nq], func=AF.Exp), [pS], [Eb])
                    for m in range(2):
                        S.add("pe", lambda e, Eb=Eb, kt=kt, m=m, vab=vab, nq=nq, st=(p == 0), sp=(p == npair - 1): e.matmul(
                            psOT[m][:, 0:nq], vab[:, kt, :], Eb[:, m, 0:nq], start=st, stop=sp), [Eb, vab], [psOT[m]])
                    S.add("pe", lambda e, Eb=Eb, nq=nq, st=(p == 0), sp=(p == npair - 1): e.matmul(
                        psZ1[:, 0:nq], ones_b[:, :], Eb[:, 1, 0:nq], start=st, stop=sp), [Eb, ones_b], [psZ1])
                    if p == 0:
                        S.add("dve", lambda e, Eb=Eb, EA=EA, nq=nq: e.tensor_copy(out=EA[:, 0:nq], in_=Eb[:, 0, 0:nq]), [Eb], [EA])
                    else:
                        S.add("dve", lambda e, Eb=Eb, EA=EA, nq=nq: e.tensor_tensor(out=EA[:, 0:nq], in0=EA[:, 0:nq], in1=Eb[:, 0, 0:nq], op=ALU.add),
                              [Eb, EA], [EA])
                for m in range(2):
                    S.add("dve", lambda e, m=m, nq=nq: e.tensor_copy(out=OTs[m][:, 0:nq], in_=psOT[m][:, 0:nq]), [psOT[m]], [OTs[m]])
                S.add("dve", lambda e, nq=nq: e.tensor_copy(out=R[1][:, 0:nq], in_=psZ1[:, 0:nq]), [psZ1], [R[1]])
                S.add("dve", lambda e, nq=nq: e.reciprocal(out=R[1][:, 0:nq], in_=R[1][:, 0:nq]), [R[1]], [R[1]])
                S.add("pe", lambda e, EA=EA, nq=nq: e.matmul(psB[:, 0:nq], ones_f[:, :], EA[:, 0:nq], start=True, stop=True), [ones_f, EA], [psB])
                S.add("dve", lambda e, nq=nq: e.reciprocal(out=R[0][:, 0:nq], in_=psB[:, 0:nq]), [psB], [R[0]])
                S.add("dve", lambda e, nq=nq: e.tensor_tensor(out=ta[:, 0:nq], in0=OTs[0][:, 0:nq], in1=R[0][:, 0:nq], op=ALU.mult), [OTs[0], R[0]], [ta])
                S.add("dve", lambda e, nq=nq: e.tensor_tensor(out=tb[:, 0:nq], in0=OTs[1][:, 0:nq], in1=R[1][:, 0:nq], op=ALU.mult), [OTs[1], R[1]], [tb])
                S.add("dve", lambda e, nq=nq: e.scalar_tensor_tensor(out=oT[:, 0:nq], in0=tb[:, 0:nq], scalar=nlam[:, 0:1], in1=ta[:, 0:nq], op0=ALU.mult, op1=ALU.add),
                      [ta, tb, nlam], [oT])
                S.add("dve", lambda e, nq=nq: e.tensor_tensor(out=sq[:, 0:nq], in0=oT[:, 0:nq], in1=oT[:, 0:nq], op=ALU.mult), [oT], [sq])
                S.add("pe", lambda e, nq=nq: e.matmul(psB[:, 0:nq], ones_f[:, :], sq[:, 0:nq], start=True, stop=True), [ones_f, sq], [psB])
                S.add("act", lambda e, nq=nq: e.activation(out=rstd[:, 0:nq], in_=psB[:, 0:nq], func=AF.Ln, bias=eps_t[:, :], scale=1.0 / 128), [psB, eps_t], [rstd])
                S.add("act", lambda e, nq=nq: e.activation(out=rstd[:, 0:nq], in_=rstd[:, 0:nq], func=AF.Exp, scale=-0.5), [rstd], [rstd])
                S.add("dve", lambda e, nq=nq: e.scalar_tensor_tensor(out=oT[:, 0:nq], in0=oT[:, 0:nq], scalar=snwc[:, 0:1], in1=rstd[:, 0:nq], op0=ALU.mult, op1=ALU.mult),
                      [oT, snwc, rstd], [oT])
                S.add("dve", lambda e, nq=nq, szb=szb, gtb=gtb: e.tensor_tensor(out=gtb[:, 0:nq], in0=oT[:, 0:nq], in1=szb[:, 0:nq], op=ALU.mult), [oT, szb], [gtb])
                dma("sp", gT_d[h, :, q0:q0 + nq], gtb[:, 0:nq], [gtb], [("gT", h, t) for t in blk])

    def attn_win(l, stack, qtiles, klat):
        w = W[l]
        sinkb = sb(stack, "sinkb", [128, 32], F32)
        bload(sinkb, w["sink"][0:1, :])
        S.add("act", lambda e: e.activation(out=sinkb[:, :], in_=sinkb[:, :], func=AF.Exp), [sinkb], [sinkb])
        masks = sb(stack, "masks", [128, 2, 128], BF16)
        dma("sp", masks[:, :, :], masks_in[:, :, :], [], [masks])
        kTh = [sb(stack, "kTh%d" % i, [128, NTOK], BF16) for i in range(2)]
        vaug = [sb(stack, "vaug%d" % i, [128, NT, 129], BF16) for i in range(2)]
        for i in range(2):
            S.add("dve", lambda e, i=i: e.memset(vaug[i][:, :, 128:129], 1.0), [], [vaug[i]])
        qTb = [sb(stack, "qTb%d" % i, [128, 4, 128], BF16) for i in range(2)]
        E = [sb(stack, "E%d" % i, [128, 512], BF16) for i in range(4)]
        psS = [ps(stack, "psS%d" % i, [128, 512], F32) for i in range(3)]
        psO = [ps(stack, "psO%d" % i, [128, 512], F32) for i in range(4)]
        pTg = ps(stack, "pTg", [128, 4, 128], BF16)
        NB = 8
        pend = []
        zr = [sb(stack, "zr%d" % i, [128, 1], F32) for i in range(NB)]
        o_t = [sb(stack, "o_t%d" % i, [128, 128], F32) for i in range(NB)]
        szt = [sb(stack, "szt%d" % i, [128, 128], F32) for i in range(NB)]
        gtm = [sb(stack, "gtm%d" % i, [128, 128], BF16) for i in range(NB)]
        gTs = [sb(stack, "gTs%d" % i, [128, 4, 128], BF16) for i in range(2)]
        rr = 0
        fi = 0
        gi = 0
        for n in range(8):
            kTb, vab = kTh[n % 2], vaug[n % 2]
            load_head_kv(kTb, vab, n, n, [klat, list(range(NT_LAT, NT))])
            for i in qtiles:
                qb = qTb[gi % 2]
                dma("sp", qb[:, :, :], qT_d[n * 4:(n + 1) * 4, :, i * 128:(i + 1) * 128].rearrange("h p t -> p h t"),
                    [("qT", n * 4 + g, i) for g in range(4)], [qb])
                keys = []
                if i > 0:
                    keys.append((i - 1, 0))
                keys.append((i, None))
                if i + 1 <= klat[-1]:
                    keys.append((i + 1, 1))
                keys += [(t, None) for t in range(NT_LAT, NT)]
                pO = [psO[(gi % 2) * 2], psO[(gi % 2) * 2 + 1]]
                bufs = [(psS[(rr + si) % 3], E[(rr + si) % 4]) for si in range(len(keys))]
                rr += len(keys)

                def score(si):
                    kt = keys[si][0]
                    pS = bufs[si][0]
                    S.add("pe", lambda e, pS=pS, kt=kt, kTb=kTb, qb=qb: e.matmul(pS[:, :], kTb[:, kt * 128:(kt + 1) * 128], qb[:, :, :].rearrange("p g t -> p (g t)"),
                                                                  start=True, stop=True), [kTb, qb], [pS])
                score(0)
                for si, (kt, mk) in enumerate(keys):
                    pS, Eb = bufs[si]
                    if si + 1 < len(keys):
                        score(si + 1)
                    S.add("act", lambda e, pS=pS, Eb=Eb: e.activation(out=Eb[:, :], in_=pS[:, :], func=AF.Exp), [pS], [Eb])
                    if mk is not None:
                        S.add("pool", lambda e, Eb=Eb, mk=mk: e.tensor_tensor(out=Eb[:, :].rearrange("p (g t) -> p g t", t=128),
                                                                               in0=Eb[:, :].rearrange("p (g t) -> p g t", t=128),
                                                                               in1=masks[:, mk, :].unsqueeze(1).to_broadcast([128, 4, 128]), op=ALU.mult),
                              [Eb, masks], [Eb])
                    for g in range(4):
                        S.add("pe", lambda e, Eb=Eb, g=g, kt=kt, vab=vab, pOg=pO[g // 2], st=(si == 0 and g % 2 == 0), sp=(si == len(keys) - 1): e.matmul(
                            pOg[:, (g % 2) * 256:(g % 2) * 256 + 129], Eb[:, g * 128:(g + 1) * 128], vab[:, kt, :],
                            start=st, stop=sp, skip_group_check=True), [Eb, vab], [pO[g // 2]])
                gts = gTs[gi % 2]
                gi += 1
                trs = []
                for g in range(4):
                    b = fi % NB
                    fi += 1
                    hq = n * 4 + g
                    O = pO[g // 2]
                    c0 = (g % 2) * 256
                    S.add("dve", lambda e, b=b, O=O, c0=c0, hq=hq: e.tensor_scalar(out=zr[b][:, :], in0=O[:, c0 + 128:c0 + 129], scalar1=sinkb[:, hq:hq + 1], scalar2=None, op0=ALU.add),
                          [O, sinkb], [zr[b]])
                    S.add("dve", lambda e, b=b: e.reciprocal(out=zr[b][:, :], in_=zr[b][:, :]), [zr[b]], [zr[b]])
                    S.add("dve", lambda e, b=b, O=O, c0=c0: e.tensor_scalar(out=o_t[b][:, :], in0=O[:, c0:c0 + 128], scalar1=zr[b][:, 0:1], scalar2=None, op0=ALU.mult),
                          [O, zr[b]], [o_t[b]])
                    dma("sp", szt[b][:, :], sz_d[i * 128:(i + 1) * 128, hq * 128:(hq + 1) * 128], [("sz", hq // 4, i)], [szt[b]])
                    S.add("pool", lambda e, b=b: e.tensor_tensor(out=gtm[b][:, :], in0=o_t[b][:, :], in1=szt[b][:, :], op=ALU.mult), [o_t[b], szt[b]], [gtm[b]])
                    trs.append((b, g))

                def fin(trs=trs, gts=gts, n=n, i=i):
                    for (b, g) in trs:
                        S.add("pe", lambda e, b=b, g=g: e.transpose(out=pTg[:, g, :], in_=gtm[b][:, :], identity=ident[:, :]), [gtm[b], ident], [pTg])
                    S.add("act", lambda e, gts=gts: e.activation(out=gts[:, :, :], in_=pTg[:, :, :], func=AF.Copy), [pTg], [gts])
                    dma("pool", gT_d[n * 4:(n + 1) * 4, :, i * 128:(i + 1) * 128].rearrange("h p t -> p h t"), gts[:, :, :], [gts],
                        [("gT", n * 4 + g, i) for g in range(4)])
                pend.append(fin)
                while len(pend) > 1:
                    pend.pop(0)()
        while pend:
            pend.pop(0)()

    for l in range(nlayers):
        kind = kinds[l]
        need_ctx_out = any(kinds[j] != 0 for j in range(l + 1, 4))
        if l == 0 and nlayers > 1:
            cast_weights(1, ("in",))
        with ExitStack() as st:
            modulation(l, st)
            S.barrier()
        ctx_t = list(range(NT_LAT, NT))
        ALLR = ("q", "k", "v", "z")
        if l == 0:
            with ExitStack() as st:
                phase_norm(l, st, list(range(NT)))
                S.barrier()
            with ExitStack() as st:
                conv_layer(l, st, True, list(range(NT_LAT)), list(range(NT_LAT)))
                S.barrier()
        elif l == 1:
            H1 = list(range(OWN + 2))
            with ExitStack() as st:
                phase_norm(l, st, list(range(NT)))
                S.barrier()
            roles = ([(s_, "q", s_) for s_ in range(8)] + [(8 + s_, "k", s_) for s_ in range(8)]
                     + [(16 + s_, "v", s_) for s_ in range(8)] + [(24 + s_, "zT", s_) for s_ in range(8)])
            ALLT = ("q", "k", "v", "zT")
            with ExitStack() as st:
                proj_phase(l, st, 64, roles, 0.125, rope64, [(H1, ALLT), (list(range(OWN + 2, NT_LAT)), ("k", "v")), (ctx_t, ALLT)])
                S.barrier()
            cast_weights(1, ("out",))
            for l2 in range(2, nlayers):
                cast_weights(l2)
            with ExitStack() as st:
                attn_diff(l, st, True, H1)
                S.barrier()
            if debug_out != "attn":
                with ExitStack() as st:
                    outproj_phase(l, st, True, H1)
                    S.barrier()
        elif l == 2:
            H1 = list(range(OWN + 2))
            H2 = list(range(OWN + 1))
            with ExitStack() as st:
                phase_norm(l, st, H1 + ctx_t)
                S.barrier()
            roles = ([(s_, "q", s_) for s_ in range(8)] + [(8 + s_, "k", s_) for s_ in range(2)]
                     + [(10 + s_, "v", s_) for s_ in range(2)] + [(12 + s_, "z", s_) for s_ in range(8)])
            with ExitStack() as st:
                proj_phase(l, st, 128, roles, 128 ** -0.5, rope128, [(H2, ALLR), ([OWN + 1], ("k", "v")), (ctx_t, ("k", "v"))])
                S.barrier()
            with ExitStack() as st:
                attn_win(l, st, H2, H1)
                S.barrier()
            with ExitStack() as st:
                outproj_phase(l, st, False, H2)
                S.barrier()
        else:
            H2 = list(range(OWN + 1))
            with ExitStack() as st:
                phase_norm(l, st, H2)
                S.barrier()
            with ExitStack() as st:
                conv_layer(l, st, False, H2, list(range(OWN)))
                S.barrier()

    if debug_out is not None:
        dbg = dram("dbg", [NT * 128, D], F32, "ExternalOutput")
        with ExitStack() as st:
            t = sb(st, "dbgt", [128, D], F32)
            for tile in range(NT):
                src = xbuf[nlayers % 2][tile * 128:(tile + 1) * 128, :]
                dma("sp", t[:, :], src, [], [t])
                dma("sp", dbg[tile * 128:(tile + 1) * 128, :], t[:, :], [t], [("dbg", tile)])
            S.barrier()

    S.emit(nc, top)
    top.close()
    return nc, S


def make_in_maps(inputs, nlayers=4, cores=range(NCORES)):
    f = lambda a: np.ascontiguousarray(np.asarray(a, dtype=np.float32))
    kinds = [0, 1, 2, 0]
    maps = []
    for c in cores:
        b, mir = c // 2, (c % 2 == 1)
        flip = (lambda a: a[::-1]) if mir else (lambda a: a)
        m = {"x": f(flip(np.asarray(inputs["x"][b]))), "ctx": f(flip(np.asarray(inputs["ctx"][b])))}
        m["ident"] = np.eye(128, dtype=np.float32).astype(ml_dtypes.bfloat16)
        cc = np.stack([np.asarray(inputs["c"][b]), np.asarray(inputs["c_ctx"])], 0)
        m["cT"] = f(cc.reshape(2, KC, 128).transpose(2, 0, 1))
        for l in range(nlayers):
            p = f"l{l}_"
            m[p + "norm"] = f(inputs[p + "norm"]).reshape(1, D)
            m[p + "w_mod"] = f(inputs[p + "w_mod"])
            m[p + "b_mod"] = f(inputs[p + "b_mod"]).reshape(1, 3 * D)
            m[p + "w_in"] = f(inputs[p + "w_in"])
            m[p + "w_out"] = f(inputs[p + "w_out"])
            if kinds[l] == 0:
                cw = np.asarray(inputs[p + "conv_w"])
                if mir:
                    cw = cw[::-1]
                cb = np.asarray(inputs[p + "conv_b"])
                cp = np.concatenate([cw, cb[None, :]], 0)
                m[p + "convp"] = f(cp.reshape(4, 32, 128).transpose(2, 1, 0))
        if nlayers > 1:
            m["rope64"] = rope_table(64, mir)
            m["l1_lamv"] = f(np.concatenate([np.asarray(inputs["l1_lam_" + k_]) for k_ in ("q1", "q2", "k1", "k2")])).reshape(1, 256)
            for k_ in ("q_norm", "k_norm", "sub_norm"):
                m["l1_" + k_] = f(inputs["l1_" + k_]).reshape(1, -1)
        if nlayers > 2:
            m["rope128"] = rope_table(128, mir)
            qq = np.arange(128)[None, :]
            kk = np.arange(128)[:, None]
            mk = np.stack([(qq <= kk), (qq >= kk)], 1).astype(np.float32)
            m["masks"] = mk.astype(ml_dtypes.bfloat16)
            for k_ in ("q_norm", "k_norm", "sink"):
                m["l2_" + k_] = f(inputs["l2_" + k_]).reshape(1, -1)
        maps.append(m)
    return maps


_ROPE_CACHE = {}


def rope_table(head_dim, mirrored=False):
    if (head_dim, mirrored) in _ROPE_CACHE:
        return _ROPE_CACHE[(head_dim, mirrored)]
    rows = T_LAT // 64
    row = np.repeat(np.arange(rows), 64).astype(np.float32)
    col = np.tile(np.arange(64), rows).astype(np.float32)
    n_freq = head_dim // 4
    inv_freq = (np.float32(10000.0) ** (-(np.arange(n_freq, dtype=np.float32) / np.float32(n_freq)))).astype(np.float32)
    ang = np.concatenate([row[:, None] * inv_freq, col[:, None] * inv_freq], axis=-1).astype(np.float32)
    tab = np.zeros((T_LAT + T_CTX, 2, 2 * n_freq), np.float32)
    if mirrored:
        ang = ang[::-1]
    tab[:T_LAT, 0] = np.cos(ang)
    tab[:T_LAT, 1] = np.sin(ang)
    tab[T_LAT:, 0] = 1.0
    _ROPE_CACHE[(head_dim, mirrored)] = tab
    return tab


def kernel(**inputs):
    nc, S = build_program(4)
    maps = make_in_maps(inputs, 4)
    res = run_bass_kernel_spmd(nc, maps, core_ids=list(range(NCORES)))
    out = np.empty((4, T_LAT, D), np.float32)
    half = OWN * 128
    for c in range(NCORES):
        b, mir = c // 2, (c % 2 == 1)
        r = np.asarray(res.results[c]["out"], dtype=np.float32)
        if mir:
            out[b, half:] = r[::-1]
        else:
            out[b, :half] = r
    return out
```
